# Optimizing a Trainium2 kernel written in Bass

```python
import math
import jax, jax.numpy as jnp
from jax import lax
import numpy as np

D_MODEL = 1024
BATCH = 4
SEQ = 8192
DEPTH = 1

HEAD_DIM = 64
N_ATTN_HEADS = 8
N_KV_HEADS = 2
GQA_RATIO = N_ATTN_HEADS // N_KV_HEADS
ATTN_WIDTH = N_ATTN_HEADS * HEAD_DIM
KV_WIDTH = N_KV_HEADS * HEAD_DIM
CONV_GROUPS = 8
CONV_WIDTH = D_MODEL - ATTN_WIDTH
CONV_K = 3
CMP_BLOCK = 32
CMP_STRIDE = 16
CMP_HIDDEN = 256
SEL_BLOCK = 64
N_SELECT = 16
WINDOW = 512
Q_BLOCK = 64
SEL_FORCE = 1.0e6
REL_BUCKETS = 32
REL_MAX_DIST = 1024
D_FF = 4 * D_MODEL
N_MOD = 6
EPS = 1e-6
NEG = -1e30
IN_SIZES = [ATTN_WIDTH] + [KV_WIDTH] * 6 + [3 * N_ATTN_HEADS] + [CONV_WIDTH] * 3
IN_COLS = sum(IN_SIZES)
IN_OFFSETS = [int(v) for v in np.cumsum(IN_SIZES)[:-1]]

kernel_name = "hymba_nsa_shortconv_adaln_layer"


def rms_norm(x, g):
    x32 = x.astype(jnp.float32)
    y = x32 * lax.rsqrt(jnp.mean(x32 * x32, axis=-1, keepdims=True) + EPS)
    return (y * g.astype(jnp.float32)).astype(x.dtype)


def rel_bucket(dist):
    n = jnp.maximum(dist, 0)
    max_exact = REL_BUCKETS // 2
    nf = jnp.maximum(n, max_exact).astype(jnp.float32)
    large = max_exact + (jnp.log(nf / max_exact) / math.log(REL_MAX_DIST / max_exact)
                         * (REL_BUCKETS - max_exact)).astype(jnp.int32)
    large = jnp.minimum(large, REL_BUCKETS - 1)
    return jnp.where(n < max_exact, n, large)


def masked_softmax(s, mask):
    s = jnp.where(mask, s.astype(jnp.float32), NEG)
    s = s - jnp.max(s, axis=-1, keepdims=True)
    e = jnp.exp(s) * mask
    den = jnp.sum(e, axis=-1, keepdims=True)
    return e / jnp.where(den > 0, den, 1.0)


def compress(blocks, pe, w1, w2):
    b, n = blocks.shape[:2]
    h = blocks + pe[None, None, :, None, :]
    h = jnp.moveaxis(h, 3, 2).reshape(b, n, N_KV_HEADS, CMP_BLOCK * HEAD_DIM)
    return jax.nn.gelu(h @ w1) @ w2


def nsa_attention(q, kc, vc, ks, vs, kw, vw, gates, rel_bias):
    b, t = q.shape[:2]
    G, R = N_KV_HEADS, GQA_RATIO
    n_cmp = kc.shape[1]
    n_sb = t // SEL_BLOCK
    top_n = min(N_SELECT, n_sb)
    span_w = WINDOW + Q_BLOCK
    scale = HEAD_DIM ** -0.5
    cmp_end = jnp.arange(n_cmp) * CMP_STRIDE + CMP_BLOCK - 1
    c_start = np.arange(n_cmp)[:, None] * CMP_STRIDE
    s_start = np.arange(n_sb)[None, :] * SEL_BLOCK
    ov = np.clip(np.minimum(c_start + CMP_BLOCK, s_start + SEL_BLOCK) - np.maximum(c_start, s_start), 0, None)
    overlap = jnp.asarray(ov / CMP_BLOCK, dtype=jnp.float32)
    ks_blocks = ks.reshape(b, n_sb, SEL_BLOCK, G, HEAD_DIM).transpose(0, 3, 1, 2, 4)
    vs_blocks = vs.reshape(b, n_sb, SEL_BLOCK, G, HEAD_DIM).transpose(0, 3, 1, 2, 4)
    kw_pad = jnp.pad(kw, ((0, 0), (WINDOW, 0), (0, 0), (0, 0)))
    vw_pad = jnp.pad(vw, ((0, 0), (WINDOW, 0), (0, 0), (0, 0)))
    bias_gr = rel_bias.reshape(REL_BUCKETS, G, R)
    bi = jnp.arange(b)[:, None, None, None]
    gi = jnp.arange(G)[None, :, None, None]
    gi5 = jnp.arange(G)[None, :, None, None, None]
    jb = jnp.arange(n_sb)

    def block_fn(ci):
        t0 = ci * Q_BLOCK
        tpos = t0 + jnp.arange(Q_BLOCK)
        qc = lax.dynamic_slice_in_dim(q, t0, Q_BLOCK, 1).reshape(b, Q_BLOCK, G, R, HEAD_DIM)
        gc = lax.dynamic_slice_in_dim(gates, t0, Q_BLOCK, 1).reshape(b, Q_BLOCK, G, R, 3)
        dist_c = tpos[:, None] - cmp_end[None, :]
        bias_c = rel_bias[rel_bucket(dist_c)].reshape(Q_BLOCK, n_cmp, G, R).transpose(2, 3, 0, 1)
        s_c = jnp.einsum('bqgrd,bngd->bgrqn', qc, kc).astype(jnp.float32) * scale + bias_c
        p_c = masked_softmax(s_c, dist_c >= 0)
        o_c = jnp.einsum('bgrqn,bngd->bqgrd', p_c.astype(vc.dtype), vc)
        imp = jnp.einsum('bgrqn,nj->bgqj', p_c, overlap)
        cur = tpos // SEL_BLOCK
        valid = jb[None, :] <= cur[:, None]
        forced = (jb[None, :] == 0) | (jb[None, :] == cur[:, None]) | (jb[None, :] == cur[:, None] - 1)
        score = jnp.where(valid, jnp.where(forced, SEL_FORCE, imp), -1.0)
        _, idx = lax.top_k(score, top_n)
        ks_sel = ks_blocks[bi, gi, idx]
        vs_sel = vs_blocks[bi, gi, idx]
        kpos = idx[..., None] * SEL_BLOCK + jnp.arange(SEL_BLOCK)
        dist_s = tpos[:, None, None] - kpos
        bias_s = jnp.moveaxis(bias_gr[rel_bucket(dist_s), gi5], -1, 2)
        s_s = jnp.einsum('bqgrd,bgqnld->bgrqnl', qc, ks_sel).astype(jnp.float32) * scale + bias_s
        p_s = masked_softmax(s_s.reshape(b, G, R, Q_BLOCK, top_n * SEL_BLOCK),
                             (dist_s >= 0).reshape(b, G, 1, Q_BLOCK, top_n * SEL_BLOCK))
        o_s = jnp.einsum('bgrqnl,bgqnld->bqgrd', p_s.reshape(s_s.shape).astype(vs.dtype), vs_sel)
        kwc = lax.dynamic_slice_in_dim(kw_pad, t0, span_w, 1)
        vwc = lax.dynamic_slice_in_dim(vw_pad, t0, span_w, 1)
        kpos_w = t0 - WINDOW + jnp.arange(span_w)
        dist_w = tpos[:, None] - kpos_w[None, :]
        mask_w = (dist_w >= 0) & (dist_w < WINDOW) & (kpos_w >= 0)[None, :]
        bias_w = rel_bias[rel_bucket(dist_w)].reshape(Q_BLOCK, span_w, G, R).transpose(2, 3, 0, 1)
        s_w = jnp.einsum('bqgrd,bsgd->bgrqs', qc, kwc).astype(jnp.float32) * scale + bias_w
        p_w = masked_softmax(s_w, mask_w)
        o_w = jnp.einsum('bgrqs,bsgd->bqgrd', p_w.astype(vw.dtype), vwc)
        o = gc[..., 0:1] * o_c + gc[..., 1:2] * o_s + gc[..., 2:3] * o_w
        return o.reshape(b, Q_BLOCK, ATTN_WIDTH)

    out = lax.map(block_fn, jnp.arange(t // Q_BLOCK))
    return out.transpose(1, 0, 2, 3).reshape(b, t, ATTN_WIDTH)


def hybrid_layer(x, c, w_in, q_norm, k_norm, cmp_pe_k, cmp_w1_k, cmp_w2_k, cmp_pe_v, cmp_w1_v, cmp_w2_v,
                 rel_bias, conv_w, w_out, norm1, norm2, w_ada, b_ada, w_ff1, w_ff2):
    b, t, d = x.shape
    G = N_KV_HEADS
    mod = (jax.nn.silu(c) @ w_ada + b_ada).reshape(b, N_MOD, 1, d)
    shift1, scale1, gate1, shift2, scale2, gate2 = (mod[:, i] for i in range(N_MOD))
    h = rms_norm(x, norm1) * (1 + scale1) + shift1
    proj = h @ w_in
    q, kc_raw, vc_raw, ks, vs, kw, vw, g, cgate, bgate, u = jnp.split(proj, IN_OFFSETS, axis=-1)
    q = rms_norm(q.reshape(b, t, N_ATTN_HEADS, HEAD_DIM), q_norm)
    ks = rms_norm(ks.reshape(b, t, G, HEAD_DIM), k_norm)
    kw = rms_norm(kw.reshape(b, t, G, HEAD_DIM), k_norm)
    vs = vs.reshape(b, t, G, HEAD_DIM)
    vw = vw.reshape(b, t, G, HEAD_DIM)
    n_cmp = (t - CMP_BLOCK) // CMP_STRIDE + 1
    cidx = (jnp.arange(n_cmp) * CMP_STRIDE)[:, None] + jnp.arange(CMP_BLOCK)[None, :]
    kc_blocks = kc_raw.reshape(b, t, G, HEAD_DIM)[:, cidx]
    vc_blocks = vc_raw.reshape(b, t, G, HEAD_DIM)[:, cidx]
    kc = rms_norm(compress(kc_blocks, cmp_pe_k, cmp_w1_k, cmp_w2_k), k_norm)
    vc = compress(vc_blocks, cmp_pe_v, cmp_w1_v, cmp_w2_v)
    gates = jax.nn.sigmoid(g).reshape(b, t, N_ATTN_HEADS, 3)
    attn_out = nsa_attention(q, kc, vc, ks, vs, kw, vw, gates, rel_bias)
    z = cgate * u
    zc = lax.conv_general_dilated(z, conv_w.reshape(CONV_K, 1, CONV_WIDTH).astype(z.dtype),
                                  window_strides=(1,), padding=[(CONV_K - 1, 0)],
                                  dimension_numbers=('NWC', 'WIO', 'NWC'),
                                  feature_group_count=CONV_WIDTH)
    conv_out = bgate * zc
    mix = jnp.concatenate([attn_out, conv_out], axis=-1) @ w_out
    x = x + gate1 * mix
    h2 = rms_norm(x, norm2) * (1 + scale2) + shift2
    ff = jnp.square(jax.nn.relu(h2 @ w_ff1)) @ w_ff2
    return x + gate2 * ff


def setup_inputs(seed: int = 0) -> dict:
    key = jax.random.key(seed)
    ks = jax.random.split(key, 24)
    L = DEPTH

    def nrm(k, shape, s):
        return jax.random.normal(k, shape, jnp.float32) * s

    return {
        "x": nrm(ks[0], (BATCH, SEQ, D_MODEL), 1.0),
        "c": nrm(ks[1], (BATCH, D_MODEL), 1.0),
        "w_in": nrm(ks[2], (L, D_MODEL, IN_COLS), D_MODEL ** -0.5),
        "q_norm": 1.0 + nrm(ks[3], (L, HEAD_DIM), 0.02),
        "k_norm": 1.0 + nrm(ks[4], (L, HEAD_DIM), 0.02),
        "cmp_pe_k": nrm(ks[5], (L, CMP_BLOCK, HEAD_DIM), 0.1),
        "cmp_w1_k": nrm(ks[6], (L, CMP_BLOCK * HEAD_DIM, CMP_HIDDEN), (CMP_BLOCK * HEAD_DIM) ** -0.5),
        "cmp_w2_k": nrm(ks[7], (L, CMP_HIDDEN, HEAD_DIM), CMP_HIDDEN ** -0.5),
        "cmp_pe_v": nrm(ks[8], (L, CMP_BLOCK, HEAD_DIM), 0.1),
        "cmp_w1_v": nrm(ks[9], (L, CMP_BLOCK * HEAD_DIM, CMP_HIDDEN), (CMP_BLOCK * HEAD_DIM) ** -0.5),
        "cmp_w2_v": nrm(ks[10], (L, CMP_HIDDEN, HEAD_DIM), CMP_HIDDEN ** -0.5),
        "rel_bias": nrm(ks[11], (REL_BUCKETS, N_ATTN_HEADS), 0.2),
        "conv_w": nrm(ks[12], (L, CONV_K, CONV_WIDTH), CONV_K ** -0.5),
        "w_out": nrm(ks[13], (L, D_MODEL, D_MODEL), D_MODEL ** -0.5),
        "norm1": 1.0 + nrm(ks[14], (L, D_MODEL), 0.02),
        "norm2": 1.0 + nrm(ks[15], (L, D_MODEL), 0.02),
        "w_ada": nrm(ks[16], (L, D_MODEL, N_MOD * D_MODEL), 0.5 * D_MODEL ** -0.5),
        "b_ada": nrm(ks[17], (L, N_MOD * D_MODEL), 0.02),
        "w_ff1": nrm(ks[18], (L, D_MODEL, D_FF), D_MODEL ** -0.5),
        "w_ff2": nrm(ks[19], (L, D_FF, D_MODEL), D_FF ** -0.5),
    }


def reference(x, c, w_in, q_norm, k_norm, cmp_pe_k, cmp_w1_k, cmp_w2_k, cmp_pe_v, cmp_w1_v, cmp_w2_v,
              rel_bias, conv_w, w_out, norm1, norm2, w_ada, b_ada, w_ff1, w_ff2):
    for l in range(DEPTH):
        x = hybrid_layer(x, c, w_in[l], q_norm[l], k_norm[l], cmp_pe_k[l], cmp_w1_k[l], cmp_w2_k[l],
                         cmp_pe_v[l], cmp_w1_v[l], cmp_w2_v[l], rel_bias, conv_w[l], w_out[l],
                         norm1[l], norm2[l], w_ada[l], b_ada[l], w_ff1[l], w_ff2[l])
    return x
```

```python
import numpy as np
from contextlib import ExitStack
import concourse.bass as bass
import concourse.mybir as mybir
from concourse.bass_utils import run_bass_kernel_spmd

F32, BF16 = mybir.dt.float32, mybir.dt.bfloat16
AF = mybir.ActivationFunctionType
ALU = mybir.AluOpType
AX = mybir.AxisListType
NEGM = -30000.0
EPS = 1e-6
NT = 32
NSTEP1 = 16


class Sched:
    EPOCH = 1000
    KD = 12

    def __init__(self, nc):
        self.nc = nc
        self.eng = {'pe': nc.tensor, 'act': nc.scalar, 'dve': nc.vector, 'pool': nc.gpsimd, 'sp': nc.sync}
        self.cnt = {e: 0 for e in self.eng}
        self.sems = {e: [] for e in self.eng}
        self.seen = {e: {} for e in self.eng}
        self.dn = {e: 0 for e in self.eng}
        self.dsems = {e: [nc.alloc_semaphore(f"d_{e}_{i}") for i in range(self.KD)] for e in ('sp', 'pool')}
        self.lastw = {}
        self.readers = {}

    def _sem(self, e, idx):
        ep = idx // self.EPOCH
        while len(self.sems[e]) <= ep:
            self.sems[e].append(self.nc.alloc_semaphore(f"c_{e}_{len(self.sems[e])}"))
        return self.sems[e][ep], idx % self.EPOCH + 1

    def _wait(self, e, tok):
        kind, f, idx = tok
        if kind == 'c':
            if f == e and (e == 'pe' or self.cnt[e] - idx > 3):
                return
            key = ('c', f, idx // self.EPOCH)
            sem, val = self._sem(f, idx)
        else:
            key = ('d', f, idx % self.KD)
            sem, val = self.dsems[f][idx % self.KD], 16 * (idx // self.KD + 1)
        if self.seen[e].get(key, 0) >= val:
            return
        self.seen[e][key] = val
        self.eng[e].wait_ge(sem, val)

    def _deps(self, e, reads, writes):
        toks = []
        for b in reads:
            t = self.lastw.get(b)
            if t is not None:
                toks.append(t)
        for b in writes:
            t = self.lastw.get(b)
            if t is not None:
                toks.append(t)
            toks.extend(self.readers.get(b, ()))
        for t in dict.fromkeys(toks):
            self._wait(e, t)

    def _commit(self, tok, reads, writes):
        for b in reads:
            lst = self.readers.setdefault(b, [])
            lst.append(tok)
            if len(lst) > 24:
                del lst[0:len(lst) - 24]
        for b in writes:
            self.lastw[b] = tok
            self.readers[b] = []

    def op(self, e, fn, r=(), w=()):
        self._deps(e, r, w)
        ins = fn(self.eng[e])
        idx = self.cnt[e]
        sem, _ = self._sem(e, idx)
        ins.then_inc(sem, 1)
        self.cnt[e] = idx + 1
        self._commit(('c', e, idx), r, w)

    def dma(self, q, out, in_, r=(), w=()):
        n = self.dn[q]
        if n >= self.KD:
            self._wait(q, ('d', q, n - self.KD))
        self._deps(q, r, w)
        ins = self.eng[q].dma_start(out=out, in_=in_)
        ins.then_inc(self.dsems[q][n % self.KD], 16)
        self.dn[q] = n + 1
        self._commit(('d', q, n), r, w)

    def barrier(self):
        for e in self.eng:
            for f in self.eng:
                if f != e and self.cnt[f] > 0:
                    self._wait(e, ('c', f, self.cnt[f] - 1))
            for q in self.dsems:
                n = self.dn[q]
                for k in range(max(0, n - self.KD), n):
                    self._wait(e, ('d', q, k))

    def finish(self, e, bufs):
        for b in bufs:
            t = self.lastw.get(b)
            if t is not None:
                self._wait(e, t)


def bc_last(ap, shape):
    return ap.unsqueeze(len(shape) - 1).to_broadcast(list(shape))


def bc_mid(ap, shape):
    return ap.unsqueeze(1).to_broadcast(list(shape))


class _Stop(Exception):
    pass


def build_program(stop=None, dbg=False):
    nc = bass.Bass("TRN2", target_bir_lowering=False)

    def din(name, shape):
        return nc.dram_tensor(name, list(shape), F32, kind="ExternalInput").ap()

    xfull = din("xfull", [8192, 1024])
    xown = din("xown", [4096, 1024])
    xhalo = din("xhalo", [8, 8, 1024])
    hmask = din("hmask", [128, 64])
    cT_d = din("cT", [128, 8])
    n1c_d = din("n1c", [128, 8])
    n2c_d = din("n2c", [128, 8])
    badaC_d = din("badaC", [128, 48])
    badaR_d = din("badaR", [1, 6144])
    w_ada = din("w_ada", [1024, 6144])
    w_in = din("w_in", [1024, 2840])
    qg_d = din("qg", [128, 64])
    knc_d = din("knc", [128, 1])
    peTk_d = din("peTk", [128, 32])
    peTv_d = din("peTv", [128, 32])
    w1k_d = din("w1k", [2048, 256])
    w1v_d = din("w1v", [2048, 256])
    w2k_d = din("w2k", [256, 64])
    w2v_d = din("w2v", [256, 64])
    convc_d = din("convc", [128, 12])
    w_out = din("w_out", [1024, 1024])
    w_ff1 = din("w_ff1", [1024, 4096])
    w_ff2 = din("w_ff2", [4096, 1024])
    bias9_d = din("bias9", [9, 128, 1024])
    ms_d = din("ms", [9, 128, 128])
    mw_d = din("mw", [6, 128, 128])
    cb_d = din("cb", [12, 128, 1024])
    cm_d = din("cm", [2, 12, 128, 128])
    b31_d = din("b31", [128, 8])
    keep_d = din("keep", [128, 254])
    addc_d = din("addc", [128, 254])
    erows_d = din("erows", [64, 8192])
    ov_d = din("ov", [128, 512])
    ident_d = din("ident", [128, 128])
    bd_d = din("bd", [128, 128])
    out_own = nc.dram_tensor("out_own", [4096, 1024], F32, kind="ExternalOutput").ap()
    kw_s = nc.dram_tensor("kw_s", [128, 8192], BF16).ap()
    vw_s = nc.dram_tensor("vw_s", [64, 128, 130], BF16).ap()
    attn_s = nc.dram_tensor("attn_s", [4096, 512], BF16, **({'kind': 'ExternalOutput'} if dbg else {})).ap()
    if dbg:
        dbg_k0 = nc.dram_tensor("dbg_k0", [128, 1024], BF16, kind="ExternalOutput").ap()
        dbg_k1 = nc.dram_tensor("dbg_k1", [128, 1024], BF16, kind="ExternalOutput").ap()
        dbg_vs = nc.dram_tensor("dbg_vs", [128, 1040], BF16, kind="ExternalOutput").ap()
        dbg_kct = nc.dram_tensor("dbg_kct", [128, 512], BF16, kind="ExternalOutput").ap()
        dbg_vc = nc.dram_tensor("dbg_vc", [128, 520], BF16, kind="ExternalOutput").ap()
        dbg_g1 = nc.dram_tensor("dbg_g1", [128, 1024], F32, kind="ExternalOutput").ap()
        dbg_ab = nc.dram_tensor("dbg_ab", [128, 32], F32, kind="ExternalOutput").ap()
        dbg_cat = nc.dram_tensor("dbg_cat", [128, 4096], BF16, kind="ExternalOutput").ap()
        dbg_x1 = nc.dram_tensor("dbg_x1", [128, 4096], F32, kind="ExternalOutput").ap()
        dbg_h2 = nc.dram_tensor("dbg_h2", [128, 4096], BF16, kind="ExternalOutput").ap()
        dbg_hid = nc.dram_tensor("dbg_hid", [128, 2048], BF16, kind="ExternalOutput").ap()
    wff1_s = nc.dram_tensor("wff1_s", [128, 8, 4096], BF16).ap()
    wff2_s = nc.dram_tensor("wff2_s", [128, 32, 1024], BF16).ap()

    S = Sched(nc)
    op, dma = S.op, S.dma

    def chk(label):
        if stop == label:
            raise _Stop()

    def final():
        for q in ('sp', 'pool'):
            n = S.dn[q]
            for k in range(max(0, n - S.KD), n):
                S._wait('sp', ('d', q, k))

    def bk(*aps):
        ks = []
        for a in aps:
            t = getattr(a, 'tensor', a)
            nm = getattr(t, 'name', None)
            if isinstance(nm, str) and nm.startswith('p_'):
                ks.append('BANK_' + nm)
        return ks

    def mm(out, lhsT, rhs, start, stop, r, w):
        op('pe', lambda e: e.matmul(out, lhsT, rhs, start=start, stop=stop), r, list(w) + bk(out))

    def tp(out, in_, ident, r, w):
        op('pe', lambda e: e.transpose(out, in_, ident), r, list(w) + bk(out))

    def act(out, in_, func, r, w, scale=1.0, bias=None, accum=None):
        kw = {}
        if bias is not None:
            kw['bias'] = bias
        if accum is not None:
            kw['accum_out'] = accum
        op('act', lambda e: e.activation(out=out, in_=in_, func=func, scale=scale, **kw), r, list(w) + bk(out, in_))

    def tt(eng, out, in0, in1, o, r, w):
        op(eng, lambda e: e.tensor_tensor(out=out, in0=in0, in1=in1, op=o), r, list(w) + bk(out, in0, in1))

    def ts(eng, out, in0, s1, s2, o0, o1, r, w):
        if o1 is None:
            op(eng, lambda e: e.tensor_scalar(out, in0, s1, None, op0=o0), r, list(w) + bk(out, in0))
        else:
            op(eng, lambda e: e.tensor_scalar(out, in0, s1, s2, op0=o0, op1=o1), r, list(w) + bk(out, in0))

    def stt(eng, out, in0, sc, in1, o0, o1, r, w):
        op(eng, lambda e: e.scalar_tensor_tensor(out=out, in0=in0, scalar=sc, in1=in1, op0=o0, op1=o1), r, list(w) + bk(out, in0, in1))

    def cp(eng, out, in_, r, w):
        if eng == 'act':
            op('act', lambda e: e.copy(out, in_), r, list(w) + bk(out, in_))
        else:
            op(eng, lambda e: e.tensor_copy(out, in_), r, list(w) + bk(out, in_))

    def ms_(eng, ap, val, w):
        op(eng, lambda e: e.memset(ap, val), (), w)

    try:
        w_in_r = w_in.rearrange("(c p) n -> p c n", p=128)

        with ExitStack() as es1:
            ident_b = es1.enter_context(nc.sbuf_tensor("s_ident_b", [128, 128], BF16))
            bd_b = es1.enter_context(nc.sbuf_tensor("s_bd_b", [128, 128], BF16))
            A1 = es1.enter_context(nc.sbuf_tensor("s_A1", [128, 8], F32))
            B1 = es1.enter_context(nc.sbuf_tensor("s_B1", [128, 8], F32))
            A2 = es1.enter_context(nc.sbuf_tensor("s_A2", [128, 8], F32))
            B2 = es1.enter_context(nc.sbuf_tensor("s_B2", [128, 8], F32))
            g1bc = es1.enter_context(nc.sbuf_tensor("s_g1bc", [128, 1024], F32))
            g2bc = es1.enter_context(nc.sbuf_tensor("s_g2bc", [128, 1024], F32))
            knc = es1.enter_context(nc.sbuf_tensor("s_knc", [128, 1], F32))
            b31t = es1.enter_context(nc.sbuf_tensor("s_b31t", [128, 8], F32))
            ss = es1.enter_context(nc.sbuf_tensor("s_ss", [128, 8], F32))
            junk = es1.enter_context(nc.sbuf_tensor("s_junk", [128, 1024], BF16))
            ss_i = [0]

            def rms_rstd(xt, P, kx):
                k = ss_i[0] % 4
                ss_i[0] += 1
                a, b_ = ss[0:P, 2 * k:2 * k + 1], ss[0:P, 2 * k + 1:2 * k + 2]
                ka, kb = f"ss{2 * k}", f"ss{2 * k + 1}"
                act(junk[0:P, :], xt, AF.Square, [kx], ['junk', ka], accum=a)
                act(b_, a, AF.Ln, [ka], [kb], scale=1.0 / 1024, bias=EPS)
                act(a, b_, AF.Exp, [kb], [ka], scale=-0.5)
                return a, ka

            dma('pool', ident_b[:], ident_d, w=['ident'])
            dma('pool', bd_b[:], bd_d, w=['bd'])
            dma('sp', knc[:], knc_d, w=['knc'])
            dma('sp', b31t[:], b31_d, w=['b31t'])
            with ExitStack() as es2:
                cT = es2.enter_context(nc.sbuf_tensor("s_cT", [128, 8], F32))
                n1c = es2.enter_context(nc.sbuf_tensor("s_n1c", [128, 8], F32))
                n2c = es2.enter_context(nc.sbuf_tensor("s_n2c", [128, 8], F32))
                badaC = es2.enter_context(nc.sbuf_tensor("s_badaC", [128, 48], F32))
                badaR = es2.enter_context(nc.sbuf_tensor("s_badaR", [1, 6144], BF16))
                e1 = es2.enter_context(nc.sbuf_tensor("s_e1", [128, 8], F32))
                scT = es2.enter_context(nc.sbuf_tensor("s_scT", [128, 8], BF16))
                screp = es2.enter_context(nc.sbuf_tensor("s_screp", [128, 8, 128], BF16))
                ones1 = es2.enter_context(nc.sbuf_tensor("s_ones1", [1, 128], BF16))
                modc = es2.enter_context(nc.sbuf_tensor("s_modc", [128, 6, 8], F32))
                wst0 = es2.enter_context(nc.sbuf_tensor("s_wst0", [128, 8, 512], BF16))
                wst1 = es2.enter_context(nc.sbuf_tensor("s_wst1", [128, 8, 512], BF16))
                pm0 = es2.enter_context(nc.psum_tensor("p_pm0", [128, 512], F32))
                pm1 = es2.enter_context(nc.psum_tensor("p_pm1", [128, 512], F32))
                pcol = es2.enter_context(nc.psum_tensor("p_pcol", [128, 8], F32))
                wst = [wst0, wst1]
                pm = [pm0, pm1]
                dma('sp', cT[:], cT_d, w=['cT'])
                dma('sp', n1c[:], n1c_d, w=['n1c'])
                dma('sp', n2c[:], n2c_d, w=['n2c'])
                dma('sp', badaC[:], badaC_d, w=['badaC'])
                dma('pool', badaR[:], badaR_d, w=['badaR'])
                act(e1[:], cT[:], AF.Exp, ['cT'], ['e1'], scale=-1.0)
                ts('dve', e1[:], e1[:], 1.0, None, ALU.add, None, ['e1'], ['e1'])
                op('dve', lambda e: e.reciprocal(e1[:], e1[:]), ['e1'], ['e1'])
                tt('dve', scT[:], cT[:], e1[:], ALU.mult, ['cT', 'e1'], ['scT'])
                cp('dve', screp[:], bc_last(scT[:], [128, 8, 128]), ['scT'], ['screp'])
                ms_('dve', ones1[:], 1.0, ['ones1'])
                w_ada_r = w_ada.rearrange("(c p) n -> p c n", p=128)
                gbc = {2: g1bc, 5: g2bc}
                for n in range(12):
                    vec, half = n // 2, n % 2
                    wt = wst[n % 2]
                    kwt = f"wst{n % 2}"
                    dma('pool', wt[:], w_ada_r[:, :, n * 512:(n + 1) * 512], w=[kwt])
                    if vec in gbc:
                        p = pm[half]
                        kp = f"pm{half}"
                        for c in range(8):
                            mm(p[:], screp[:, c, :], wt[:, c, :], c == 0, False, [kwt, 'screp'], [kp])
                        mm(p[:], ones1[0:1, :], badaR[0:1, n * 512:(n + 1) * 512], False, True, ['ones1', 'badaR'], [kp])
                        cp('act', gbc[vec][:, half * 512:(half + 1) * 512], p[:], [kp], [f"gbc{vec}{half}"])
                    else:
                        for cc in range(4):
                            for c in range(8):
                                mm(pcol[:, cc:cc + 1], wt[:, c, cc * 128:(cc + 1) * 128], scT[:, c:c + 1], c == 0, c == 7, [kwt, 'scT'], ['pcol'])
                        tt('dve', modc[:, vec, half * 4:(half + 1) * 4], pcol[:, 0:4], badaC[:, n * 4:(n + 1) * 4], ALU.add, ['pcol', 'badaC'], ['modc'])
                ts('dve', e1[:], modc[:, 1, :], 1.0, None, ALU.add, None, ['modc', 'e1'], ['e1'])
                tt('dve', A1[:], e1[:], n1c[:], ALU.mult, ['e1', 'n1c'], ['A1'])
                cp('dve', B1[:], modc[:, 0, :], ['modc'], ['B1'])
                ts('dve', e1[:], modc[:, 4, :], 1.0, None, ALU.add, None, ['modc', 'e1', 'A1'], ['e1'])
                tt('dve', A2[:], e1[:], n2c[:], ALU.mult, ['e1', 'n2c'], ['A2'])
                cp('dve', B2[:], modc[:, 3, :], ['modc'], ['B2'])
                S.barrier()

            if dbg:
                dma('sp', dbg_g1, g1bc[:], r=['gbc20', 'gbc21'], w=['dbg_g1'])
                dma('sp', dbg_ab[:, 0:8], A1[:], r=['A1'], w=['dbg_ab0'])
                dma('sp', dbg_ab[:, 8:16], B1[:], r=['B1'], w=['dbg_ab1'])
                dma('sp', dbg_ab[:, 16:24], A2[:], r=['A2'], w=['dbg_ab2'])
                dma('sp', dbg_ab[:, 24:32], B2[:], r=['B2'], w=['dbg_ab3'])
            chk('p0')

            def norm_to_hT(xt, P, kx, xh, kxh, ptr, kptr, tmpf, ktmp, hT_dst, khT, A, B, kA, kB, ncols):
                rstd, kr = rms_rstd(xt, P, kx)
                ts('dve', xh[0:P, :], xt, rstd, None, ALU.mult, None, [kx, kr], [kxh])
                for c in range(8):
                    tp(ptr[:, c, 0:P], xh[0:P, c * 128:(c + 1) * 128], ident_b[0:P, 0:P], [kxh, 'ident'], [kptr])
                tt('dve', tmpf[:, :, 0:P], ptr[:, :, 0:P], bc_last(A[:], [128, 8, P]), ALU.mult, [kptr, kA], [ktmp])
                tt('pool', hT_dst, tmpf[:, :, 0:P], bc_last(B[:], [128, 8, P]), ALU.add, [ktmp, kB], [khT])

            with ExitStack() as es3:
                Kaug0 = es3.enter_context(nc.sbuf_tensor("s_Kaug0", [128, 8192], BF16))
                Kaug1 = es3.enter_context(nc.sbuf_tensor("s_Kaug1", [128, 8192], BF16))
                VS = es3.enter_context(nc.sbuf_tensor("s_VS", [128, 64, 2, 65], BF16))
                KCT = es3.enter_context(nc.sbuf_tensor("s_KCT", [128, 512], BF16))
                VC = es3.enter_context(nc.sbuf_tensor("s_VC", [128, 4, 2, 65], BF16))
                Kaug = [Kaug0, Kaug1]
                dma('pool', Kaug0[64:128, :], erows_d, w=['Kaug0e'])
                dma('pool', Kaug1[0:64, :], erows_d, w=['Kaug1e'])
                ms_('pool', VS[:, :, :, 64:65], 1.0, ['VSone'])
                ms_('pool', VC[:, :, :, 64:65], 1.0, ['VCone'])

                with ExitStack() as es4:
                    w1p = es4.enter_context(nc.sbuf_tensor("s_w1p", [128, 8, 768], BF16))
                    xr = es4.enter_context(nc.sbuf_tensor("s_xr", [128, 3, 1024], F32))
                    xh = es4.enter_context(nc.sbuf_tensor("s_xh", [128, 2, 1024], BF16))
                    tmpf = es4.enter_context(nc.sbuf_tensor("s_tmpf", [128, 2, 8, 128], F32))
                    hT = es4.enter_context(nc.sbuf_tensor("s_hT", [128, 2, 8, 512], BF16))
                    rawk = es4.enter_context(nc.sbuf_tensor("s_rawk", [128, 8192], BF16))
                    rawv = es4.enter_context(nc.sbuf_tensor("s_rawv", [128, 8192], BF16))
                    W1k = es4.enter_context(nc.sbuf_tensor("s_W1k", [128, 32, 256], BF16))
                    W1v = es4.enter_context(nc.sbuf_tensor("s_W1v", [128, 32, 256], BF16))
                    sq = es4.enter_context(nc.sbuf_tensor("s_sq", [128, 512], BF16))
                    rs = es4.enter_context(nc.sbuf_tensor("s_rs", [128, 512], F32))
                    kwst = es4.enter_context(nc.sbuf_tensor("s_kwst", [128, 2, 512], BF16))
                    vwst = es4.enter_context(nc.sbuf_tensor("s_vwst", [128, 2, 2, 65], BF16))
                    peTk = es4.enter_context(nc.sbuf_tensor("s_peTk", [128, 32], BF16))
                    peTv = es4.enter_context(nc.sbuf_tensor("s_peTv", [128, 32], BF16))
                    cbias = es4.enter_context(nc.sbuf_tensor("s_cbias", [128, 4], F32))
                    gx = es4.enter_context(nc.sbuf_tensor("s_gx", [128, 512], F32))
                    g2 = es4.enter_context(nc.sbuf_tensor("s_g2", [128, 512], F32))
                    gel = es4.enter_context(nc.sbuf_tensor("s_gel", [128, 4, 2, 512], BF16))
                    w2kp = es4.enter_context(nc.sbuf_tensor("s_w2kp", [128, 2, 2, 128], BF16))
                    w2vb = es4.enter_context(nc.sbuf_tensor("s_w2vb", [128, 2, 64], BF16))
                    ptr0 = es4.enter_context(nc.psum_tensor("p_ptr0", [128, 8, 128], BF16))
                    ptr1 = es4.enter_context(nc.psum_tensor("p_ptr1", [128, 8, 128], BF16))
                    pf0 = es4.enter_context(nc.psum_tensor("p_pf0", [128, 512], F32))
                    pf1 = es4.enter_context(nc.psum_tensor("p_pf1", [128, 512], F32))
                    pbd = es4.enter_context(nc.psum_tensor("p_pbd", [128, 512], F32))
                    ptok = es4.enter_context(nc.psum_tensor("p_ptok", [128, 256], F32))
                    pcb = es4.enter_context(nc.psum_tensor("p_pcb", [128, 8], F32))
                    ptr = [ptr0, ptr1]
                    pfb = [pf0, pf1]
                    pfi = [0]
                    dma('pool', w1p[:], w_in_r[:, :, 512:1280], w=['w1p'])
                    for kv, (W1, w1d, peT, ped) in enumerate(((W1k, w1k_d, peTk, peTk_d), (W1v, w1v_d, peTv, peTv_d))):
                        w1r = w1d.rearrange("(l d) c -> d l c", d=64)
                        dma('pool', W1[0:64, :, :], w1r, w=[f"W1{kv}a"])
                        dma('pool', W1[64:128, :, :], w1r, w=[f"W1{kv}b"])
                        dma('pool', peT[:], ped, w=[f"peT{kv}"])
                    ms_('dve', w2kp[:], 0.0, ['w2kp'])
                    for cc in range(2):
                        for g in range(2):
                            dma('pool', w2kp[:, cc, g, 64 * g:64 * g + 64], w2k_d[cc * 128:(cc + 1) * 128, :], r=['w2kp'], w=[f"w2kp{cc}{g}"])
                        dma('pool', w2vb[:, cc, :], w2v_d[cc * 128:(cc + 1) * 128, :], w=[f"w2vb{cc}"])
                    ms_('pool', vwst[:, :, :, 64:65], 1.0, ['vwst1'])
                    ms_('dve', gel[:, :, :, 511:512], 0.0, ['gelz'])

                    def stacked_norm(pf, kpf, N, outs):
                        act(sq[:, 0:N], pf[:, 0:N], AF.Square, [kpf], ['sq'])
                        mm(pbd[:, 0:N], bd_b[:], sq[:, 0:N], True, True, ['bd', 'sq'], ['pbd'])
                        act(rs[:, 0:N], pbd[:, 0:N], AF.Ln, ['pbd'], ['rs'], scale=1.0 / 64, bias=EPS)
                        act(rs[:, 0:N], rs[:, 0:N], AF.Exp, ['rs'], ['rs'], scale=-0.5)
                        for dst, p0, p1, kd in outs:
                            stt('dve', dst, pf[p0:p1, 0:N], knc[p0:p1, 0:1], rs[p0:p1, 0:N], ALU.mult, ALU.mult, [kpf, 'rs', 'knc'], [kd])

                    chk('p1a')
                    wff_jobs = [('f1', c) for c in range(8)] + [('f2', c) for c in range(32)]
                    for s in range(NSTEP1):
                        hTs = hT[:, s % 2]
                        khT = f"hT{s % 2}"
                        for j in range(4):
                            T = 4 * s + j
                            xt = xr[:, T % 3, :]
                            kx = f"xr{T % 3}"
                            dma('sp', xt, xfull[T * 128:(T + 1) * 128, :], w=[kx])
                            norm_to_hT(xt, 128, kx, xh[:, T % 2, :], f"xh{T % 2}", ptr[T % 2], f"ptr{T % 2}",
                                       tmpf[:, T % 2], f"tmpf{T % 2}", hTs[:, :, j * 128:(j + 1) * 128], khT, A1, B1, 'A1', 'B1', 128)
                        tok = slice(s * 512, (s + 1) * 512)
                        for name, col in (('ks', 256), ('kw', 512), ('kc', 0), ('vc', 128)):
                            pf = pfb[pfi[0] % 2]
                            kpf = f"pf{pfi[0] % 2}"
                            pfi[0] += 1
                            for c in range(8):
                                mm(pf[:], w1p[:, c, col:col + 128], hTs[:, c, :], c == 0, c == 7, ['w1p', khT], [kpf])
                            if name == 'kc':
                                cp('act', rawk[:, tok], pf[:], [kpf], ['rawk'])
                            elif name == 'vc':
                                cp('act', rawv[:, tok], pf[:], [kpf], ['rawv'])
                            elif name == 'ks':
                                stacked_norm(pf, kpf, 512, [(Kaug0[0:64, tok], 0, 64, 'Kaug0d'), (Kaug1[64:128, tok], 64, 128, 'Kaug1d')])
                            else:
                                stacked_norm(pf, kpf, 512, [(kwst[:, s % 2, :], 0, 128, f"kwst{s % 2}")])
                                dma('sp', kw_s[:, tok], kwst[:, s % 2, :], r=[f"kwst{s % 2}"], w=['kw_s'])
                        for j in range(4):
                            T = 4 * s + j
                            for c in range(8):
                                mm(ptok[:, 0:128], hTs[:, c, j * 128:(j + 1) * 128], w1p[:, c, 384:512], c == 0, c == 7, ['w1p', khT], ['ptok'])
                            for c in range(8):
                                mm(ptok[:, 128:256], hTs[:, c, j * 128:(j + 1) * 128], w1p[:, c, 640:768], c == 0, c == 7, ['w1p', khT], ['ptok'])
                            cp('act', VS[:, T, :, 0:64], ptok[:, 0:128].rearrange("p (g d) -> p g d", g=2), ['ptok'], ['VSd'])
                            cp('dve', vwst[:, T % 2, :, 0:64], ptok[:, 128:256].rearrange("p (g d) -> p g d", g=2), ['ptok'], [f"vwst{T % 2}"])
                            dma('sp', vw_s[T], vwst[:, T % 2].rearrange("p g d -> p (g d)"), r=[f"vwst{T % 2}", 'vwst1'], w=['vw_s'])
                        if s == 0:
                            chk('p1b')
                        for _ in range(3):
                            if wff_jobs:
                                kind, c = wff_jobs.pop(0)
                                if kind == 'f1':
                                    dma('pool', wff1_s[:, c, :], w_ff1[c * 128:(c + 1) * 128, :], w=['wff1_s'])
                                else:
                                    dma('pool', wff2_s[:, c, :], w_ff2[c * 128:(c + 1) * 128, :], w=['wff2_s'])
                    while wff_jobs:
                        kind, c = wff_jobs.pop(0)
                        if kind == 'f1':
                            dma('pool', wff1_s[:, c, :], w_ff1[c * 128:(c + 1) * 128, :], w=['wff1_s'])
                        else:
                            dma('pool', wff2_s[:, c, :], w_ff2[c * 128:(c + 1) * 128, :], w=['wff2_s'])

                    chk('p1c')
                    for kv, (W1, peT) in enumerate(((W1k, peTk), (W1v, peTv))):
                        for cc in range(2):
                            for l in range(32):
                                mm(pcb[:, kv * 2 + cc:kv * 2 + cc + 1], W1[0:64, l, cc * 128:(cc + 1) * 128], peT[0:64, l:l + 1],
                                   l == 0, l == 31, [f"W1{kv}a", f"peT{kv}"], ['pcb'])
                    cp('dve', cbias[:], pcb[:, 0:4], ['pcb'], ['cbias'])
                    for kv, (W1, raw, kraw) in enumerate(((W1k, rawk, 'rawk'), (W1v, rawv, 'rawv'))):
                        for g in range(2):
                            rows = slice(64 * g, 64 * g + 64)
                            for cc in range(2):
                                pf = pfb[pfi[0] % 2]
                                kpf = f"pf{pfi[0] % 2}"
                                pfi[0] += 1
                                for l in range(32):
                                    mm(pf[:, 0:511], W1[rows, l, cc * 128:(cc + 1) * 128], raw[rows, l:l + 16 * 510 + 1:16], l == 0, l == 31,
                                       [f"W1{kv}a", f"W1{kv}b", kraw], [kpf])
                                kg = f"gel{kv}{g}"
                                act(gx[:, 0:511], pf[:, 0:511], AF.Identity, [kpf, 'cbias'], ['gx'], bias=cbias[:, kv * 2 + cc:kv * 2 + cc + 1])
                                tt('dve', g2[:, 0:511], gx[:, 0:511], gx[:, 0:511], ALU.mult, ['gx'], ['g2'])
                                ts('dve', g2[:, 0:511], g2[:, 0:511], 0.044715, 1.0, ALU.mult, ALU.add, ['g2'], ['g2'])
                                tt('dve', g2[:, 0:511], g2[:, 0:511], gx[:, 0:511], ALU.mult, ['g2', 'gx'], ['g2'])
                                act(g2[:, 0:511], g2[:, 0:511], AF.Tanh, ['g2'], ['g2'], scale=0.7978845608028654)
                                ts('dve', g2[:, 0:511], g2[:, 0:511], 0.5, 0.5, ALU.mult, ALU.add, ['g2'], ['g2'])
                                tt('dve', gel[:, kv * 2 + g, cc, 0:511], g2[:, 0:511], gx[:, 0:511], ALU.mult, ['g2', 'gx', 'gelz'], [kg])
                    pf = pfb[pfi[0] % 2]
                    kpf = f"pf{pfi[0] % 2}"
                    pfi[0] += 1
                    n_ = 0
                    for g in range(2):
                        for cc in range(2):
                            mm(pf[:], w2kp[:, cc, g, :], gel[:, g, cc, :], n_ == 0, n_ == 3, [f"w2kp{cc}{g}", f"gel0{g}", 'gelz'], [kpf])
                            n_ += 1
                    stacked_norm(pf, kpf, 512, [(KCT[:], 0, 128, 'KCT')])
                    for ct in range(4):
                        for g in range(2):
                            for cc in range(2):
                                mm(ptok[:, 0:64], gel[:, 2 + g, cc, ct * 128:(ct + 1) * 128], w2vb[:, cc, :], cc == 0, cc == 1,
                                   [f"gel1{g}", 'gelz', f"w2vb{cc}"], ['ptok'])
                            cp('act', VC[:, ct, g, 0:64], ptok[:, 0:64], ['ptok'], ['VCd'])
                    S.barrier()

                if dbg:
                    dma('sp', dbg_k0, Kaug0[:, 0:1024], r=['Kaug0d', 'Kaug0e'], w=['dbg_k0'])
                    dma('sp', dbg_k1, Kaug1[:, 0:1024], r=['Kaug1d', 'Kaug1e'], w=['dbg_k1'])
                    dma('sp', dbg_vs, VS[:, 0:8].rearrange("p t g d -> p (t g d)"), r=['VSd', 'VSone'], w=['dbg_vs'])
                    dma('sp', dbg_kct, KCT[:], r=['KCT'], w=['dbg_kct'])
                    dma('sp', dbg_vc, VC[:].rearrange("p t g d -> p (t g d)"), r=['VCd', 'VCone'], w=['dbg_vc'])
                chk('p1')
                with ExitStack() as es5:
                    w2p = es5.enter_context(nc.sbuf_tensor("s_w2p", [128, 8, 536], BF16))
                    SELT = es5.enter_context(nc.sbuf_tensor("s_SELT", [128, 9, 1024], BF16))
                    WINT = es5.enter_context(nc.sbuf_tensor("s_WINT", [128, 6, 1024], BF16))
                    CFAR = es5.enter_context(nc.sbuf_tensor("s_CFAR", [128, 1024], BF16))
                    ctab = es5.enter_context(nc.sbuf_tensor("s_ctab", [128, 4, 1024], BF16))
                    stg = es5.enter_context(nc.sbuf_tensor("s_stg", [128, 2, 1024], F32))
                    stg2 = es5.enter_context(nc.sbuf_tensor("s_stg2", [128, 1024], F32))
                    mst = es5.enter_context(nc.sbuf_tensor("s_mst", [128, 4, 128], F32))
                    keep = es5.enter_context(nc.sbuf_tensor("s_keep", [128, 254], F32))
                    addc = es5.enter_context(nc.sbuf_tensor("s_addc", [128, 254], F32))
                    OVb = es5.enter_context(nc.sbuf_tensor("s_OVb", [128, 512], BF16))
                    qg = es5.enter_context(nc.sbuf_tensor("s_qg", [128, 64], F32))
                    KWr = es5.enter_context(nc.sbuf_tensor("s_KWr", [128, 8, 128], BF16))
                    VWr = es5.enter_context(nc.sbuf_tensor("s_VWr", [128, 8, 130], BF16))
                    xo = es5.enter_context(nc.sbuf_tensor("s_xo", [128, 2, 1024], F32))
                    xho = es5.enter_context(nc.sbuf_tensor("s_xho", [128, 1024], BF16))
                    tmpo = es5.enter_context(nc.sbuf_tensor("s_tmpo", [128, 8, 128], F32))
                    hTo = es5.enter_context(nc.sbuf_tensor("s_hTo", [128, 8, 128], BF16))
                    qsq = es5.enter_context(nc.sbuf_tensor("s_qsq", [128, 512], F32))
                    qss = es5.enter_context(nc.sbuf_tensor("s_qss", [128, 16], F32))
                    qtmp = es5.enter_context(nc.sbuf_tensor("s_qtmp", [128, 512], F32))
                    qnp = es5.enter_context(nc.sbuf_tensor("s_qnp", [128, 512], BF16))
                    QQ = es5.enter_context(nc.sbuf_tensor("s_QQ", [128, 2, 4, 512], BF16))
                    sig = es5.enter_context(nc.sbuf_tensor("s_sig", [128, 24], F32))
                    PT = es5.enter_context(nc.sbuf_tensor("s_PT", [128, 4, 512], BF16))
                    OCs = es5.enter_context(nc.sbuf_tensor("s_OCs", [128, 2, 260], F32))
                    dsm = es5.enter_context(nc.sbuf_tensor("s_dsm", [128, 32], F32))
                    imp = es5.enter_context(nc.sbuf_tensor("s_imp", [128, 128], F32))
                    scr = es5.enter_context(nc.sbuf_tensor("s_scr", [128, 128], F32))
                    wk = es5.enter_context(nc.sbuf_tensor("s_wk", [128, 128], F32))
                    m8 = es5.enter_context(nc.sbuf_tensor("s_m8", [128, 16], F32))
                    NS = es5.enter_context(nc.sbuf_tensor("s_NS", [128, 2, 128], BF16))
                    acc = es5.enter_context(nc.sbuf_tensor("s_acc", [128, 256], F32))
                    acc2 = es5.enter_context(nc.sbuf_tensor("s_acc2", [128, 256], F32))
                    attn_bf = es5.enter_context(nc.sbuf_tensor("s_attn_bf", [128, 2, 512], BF16))
                    ST0 = es5.enter_context(nc.psum_tensor("p_ST0", [128, 512], F32))
                    ST1 = es5.enter_context(nc.psum_tensor("p_ST1", [128, 512], F32))
                    OC = es5.enter_context(nc.psum_tensor("p_OC", [128, 260], F32))
                    IMP = es5.enter_context(nc.psum_tensor("p_IMP", [128, 512], F32))
                    OS = es5.enter_context(nc.psum_tensor("p_OS", [128, 260], F32))
                    OW = es5.enter_context(nc.psum_tensor("p_OW", [128, 260], F32))
                    M0 = es5.enter_context(nc.psum_tensor("p_M0", [128, 512], F32))
                    M1 = es5.enter_context(nc.psum_tensor("p_M1", [128, 1024], BF16))
                    ST = [ST0, ST1]
                    dma('pool', w2p[:, :, 0:512], w_in_r[:, :, 0:512], w=['w2pq'])
                    dma('pool', w2p[:, :, 512:536], w_in_r[:, :, 1280:1304], w=['w2pg'])
                    dma('sp', keep[:], keep_d, w=['keep'])
                    dma('sp', addc[:], addc_d, w=['addc'])
                    dma('pool', OVb[:], ov_d, w=['OVb'])
                    dma('sp', qg[:], qg_d, w=['qg'])
                    ts('dve', qg[:], qg[:], 0.125, None, ALU.mult, None, ['qg'], ['qg'])
                    cp('dve', CFAR[:].rearrange("p (h q) -> p h q", h=8), bc_last(b31t[:], [128, 8, 128]), ['b31t'], ['CFAR'])
                    b31b = bc_last(b31t[:], [128, 8, 128])
                    for d in range(9):
                        sg = stg[:, d % 2, :]
                        ksg = f"stg{d % 2}"
                        dma('sp', sg, bias9_d[d], w=[ksg])
                        dma('sp', mst[:, d % 2, :], ms_d[d], w=[f"mst{d % 2}"])
                        if d < 6:
                            dma('sp', mst[:, 2 + d % 2, :], mw_d[d], w=[f"mst{2 + d % 2}"])
                            tt('pool', WINT[:, d, :].rearrange("p (h q) -> p h q", h=8), sg.rearrange("p (h q) -> p h q", h=8),
                               bc_mid(mst[:, 2 + d % 2, :], [128, 8, 128]), ALU.add, [ksg, f"mst{2 + d % 2}"], ['WINT'])
                        tt('dve', stg2[:].rearrange("p (h q) -> p h q", h=8), sg.rearrange("p (h q) -> p h q", h=8), b31b, ALU.subtract, [ksg, 'b31t'], ['stg2'])
                        tt('dve', SELT[:, d, :].rearrange("p (h q) -> p h q", h=8), stg2[:].rearrange("p (h q) -> p h q", h=8),
                           bc_mid(mst[:, d % 2, :], [128, 8, 128]), ALU.add, ['stg2', f"mst{d % 2}"], ['SELT'])

                    cnt = [0]
                    tabn = [0]

                    def branch(O, kO, kts, kfn, vfn, qfn, tabfn):
                        nk = len(kts)
                        for n, kt in enumerate(kts):
                            st = ST[cnt[0] % 2]
                            kst = f"ST{cnt[0] % 2}"
                            pt = PT[:, cnt[0] % 4, :]
                            kpt = f"PT{cnt[0] % 4}"
                            cnt[0] += 1
                            lk, rk = kfn(kt)
                            qa, rq = qfn(kt)
                            tab = tabfn(kt)
                            mm(st[:], lk, qa, True, tab is None, rk + rq, [kst])
                            if tab is not None:
                                mm(st[:], ident_b[:], tab[0], False, True, ['ident'] + tab[1], [kst])
                            act(pt, st[:], AF.Exp, [kst], [kpt])
                            vv, rv = vfn(kt)
                            for r in range(4):
                                mm(O[:, r * 65:(r + 1) * 65], pt[:, r * 128:(r + 1) * 128], vv, n == 0 and r == 0, n == nk - 1, [kpt] + rv, [kO])
                            yield pt, kpt, n

                    for i in range(NT):
                        xt = xo[:, i % 2, :]
                        kx = f"xo{i % 2}"
                        dma('sp', xt, xown[i * 128:(i + 1) * 128, :], w=[kx])
                        for T in (2 * i, 2 * i + 1):
                            dma('sp', KWr[:, T % 8, :], kw_s[:, T * 128:(T + 1) * 128], r=['kw_s'], w=[f"KWr{T % 8}"])
                            dma('sp', VWr[:, T % 8, :], vw_s[T], r=['vw_s'], w=[f"VWr{T % 8}"])
                        nct = (2 * i + 1) // 16 + 1
                        ctabs = {}
                        for ct in range(nct):
                            dpp = 2 * i + 1 - 16 * ct
                            if dpp <= 23:
                                idx = (dpp - 1) // 2
                                sl = tabn[0] % 4
                                tabn[0] += 1
                                sg = stg[:, sl % 2, :]
                                ksg = f"stg{sl % 2}"
                                dma('sp', sg, cb_d[idx], w=[ksg])
                                dma('sp', mst[:, sl % 2, :], cm_d[1 if ct == 3 else 0, idx], w=[f"mst{sl % 2}"])
                                tt('pool', ctab[:, sl, :].rearrange("p (h q) -> p h q", h=8), sg.rearrange("p (h q) -> p h q", h=8),
                                   bc_mid(mst[:, sl % 2, :], [128, 8, 128]), ALU.add, [ksg, f"mst{sl % 2}"], [f"ctab{sl}"])
                                ctabs[ct] = (ctab[:, sl, :], [f"ctab{sl}"])
                            else:
                                ctabs[ct] = (CFAR[:], ['CFAR'])
                        norm_to_hT(xt, 128, kx, xho[:], 'xho', M1[:].rearrange("p (c t) -> p c t", c=8), 'M1', tmpo[:], 'tmpo', hTo[:], 'hTo', A1, B1, 'A1', 'B1', 128)
                        for c in range(8):
                            mm(M0[:], hTo[:, c, :], w2p[:, c, 0:512], c == 0, c == 7, ['hTo', 'w2pq'], ['M0'])
                        act(qsq[:], M0[:], AF.Square, ['M0'], ['qsq'])
                        op('dve', lambda e: e.tensor_reduce(out=qss[:, 0:8], in_=qsq[:].rearrange("p (h d) -> p h d", h=8), axis=AX.X, op=ALU.add), ['qsq'], ['qss0'])
                        act(qss[:, 8:16], qss[:, 0:8], AF.Ln, ['qss0'], ['qss1'], scale=1.0 / 64, bias=EPS)
                        act(qss[:, 0:8], qss[:, 8:16], AF.Exp, ['qss1'], ['qss0'], scale=-0.5)
                        tt('dve', qtmp[:].rearrange("p (h d) -> p h d", h=8), M0[:].rearrange("p (h d) -> p h d", h=8),
                           bc_last(qss[:, 0:8], [128, 8, 64]), ALU.mult, ['M0', 'qss0'], ['qtmp'])
                        tt('dve', qnp[:].rearrange("p (r g d) -> p g r d", r=4, g=2), qtmp[:].rearrange("p (g r d) -> p g r d", g=2, r=4),
                           qg[:].unsqueeze(1).unsqueeze(1).to_broadcast([128, 2, 4, 64]), ALU.mult, ['qtmp', 'qg'], ['qnp'])
                        for c in range(8):
                            mm(M0[:, 0:24], hTo[:, c, :], w2p[:, c, 512:536], c == 0, c == 7, ['hTo', 'w2pg', 'qtmp'], ['M0'])
                        act(sig[:], M0[:, 0:24], AF.Exp, ['M0'], ['sig'], scale=-1.0)
                        ts('dve', sig[:], sig[:], 1.0, None, ALU.add, None, ['sig'], ['sig'])
                        op('dve', lambda e: e.reciprocal(sig[:], sig[:]), ['sig'], ['sig'])
                        b = i % 2
                        useB = i >= 16
                        for r in range(4):
                            tp(M1[:, r * 128:(r + 1) * 128], qnp[:, r * 128:(r + 1) * 128], ident_b[:], ['qnp', 'ident', 'tmpo'], ['M1'])
                        QA = [QQ[:, b, 0, :], QQ[:, b, 1, :]]
                        QB = [QQ[:, b, 2, :], QQ[:, b, 3, :]]
                        kQ = [f"QQ{b}{n}" for n in range(4)]
                        cp('act', QA[0][0:64, :], M1[0:64, 0:512], ['M1'], [kQ[0] + 'd'])
                        cp('act', QA[1][64:128, :], M1[64:128, 0:512], ['M1'], [kQ[1] + 'd'])
                        if useB:
                            cp('act', QB[0][0:64, :], M1[0:64, 0:512], ['M1'], [kQ[2] + 'd'])
                            cp('act', QB[1][64:128, :], M1[64:128, 0:512], ['M1'], [kQ[3] + 'd'])
                        o_ = 126 - 4 * i
                        for g in range(2):
                            rows = slice(64 * g, 64 * g + 64)
                            gen = branch(OC, 'OC', list(range(nct)),
                                         lambda ct: (KCT[rows, ct * 128:(ct + 1) * 128], ['KCT']),
                                         lambda ct: (VC[:, ct, g, :], ['VCd', 'VCone']),
                                         lambda ct: (QA[g][rows, :], [kQ[g] + 'd']),
                                         lambda ct: (ctabs[ct][0][:, g * 512:(g + 1) * 512], ctabs[ct][1]))
                            for pt, kpt, n in gen:
                                for r in range(4):
                                    mm(IMP[:, r * 128:(r + 1) * 128], pt[:, r * 128:(r + 1) * 128], OVb[:, n * 128:(n + 1) * 128], n == 0 and r == 0, n == nct - 1, [kpt, 'OVb'], ['IMP'])
                            cp('dve', OCs[:, g, :], OC[:], ['OC'], [f"OCs{g}"])
                            dc = dsm[:, 0:4]
                            cp('dve', dc, OCs[:, g, 64:260:65], [f"OCs{g}"], ['dc'])
                            ts('dve', dc, dc, 1e-30, None, ALU.max, None, ['dc'], ['dc'])
                            op('dve', lambda e: e.reciprocal(dc, dc), ['dc'], ['dc'])
                            ts('dve', imp[:], IMP[:, 0:128], dsm[:, 0:1], None, ALU.mult, None, ['IMP', 'dc'], ['imp'])
                            for r in range(1, 4):
                                stt('dve', imp[:], IMP[:, r * 128:(r + 1) * 128], dsm[:, r:r + 1], imp[:], ALU.mult, ALU.add, ['IMP', 'dc', 'imp'], ['imp'])
                            tt('dve', scr[:], imp[:], keep[:, o_:o_ + 128], ALU.mult, ['imp', 'keep'], ['scr'])
                            tt('dve', scr[:], scr[:], addc[:, o_:o_ + 128], ALU.add, ['scr', 'addc'], ['scr'])
                            ms_('dve', scr[:, 0:1], 1.0e6, ['scr'])
                            op('dve', lambda e: e.max(out=m8[:, 0:8], in_=scr[:]), ['scr'], ['m8a'])
                            op('dve', lambda e: e.match_replace(out=wk[:], in_to_replace=m8[:, 0:8], in_values=scr[:], imm_value=-1.0e30), ['scr', 'm8a'], ['wk'])
                            op('dve', lambda e: e.max(out=m8[:, 8:16], in_=wk[:]), ['wk'], ['m8b'])
                            cA = slice(64, 128) if g == 0 else slice(0, 64)
                            ts('dve', NS[:, 0, cA], scr[:, 0:64], m8[:, 15:16], NEGM, ALU.is_lt, ALU.mult, ['scr', 'm8b'], [f"NS0{g}"])
                            ts('dve', NS[:, 1, cA], scr[:, 64:128], m8[:, 15:16], NEGM, ALU.is_lt, ALU.mult, ['scr', 'm8b'], [f"NS1{g}"])
                        for hb in range(2 if useB else 1):
                            tp(M1[:, 512 + hb * 128:512 + (hb + 1) * 128], NS[:, hb, :], ident_b[:], [f"NS{hb}0", f"NS{hb}1", 'ident'], ['M1'])
                            src = M1[:, 512 + hb * 128:512 + (hb + 1) * 128]
                            Qh = QA if hb == 0 else QB
                            tt('dve', Qh[0][64:128, :].rearrange("p (r q) -> p r q", r=4), bc_mid(src[64:128, :], [64, 4, 128]),
                               bc_last(b31t[64:128, 0:4], [64, 4, 128]), ALU.add, ['M1', 'b31t'], [kQ[2 * hb] + 'n'])
                            tt('dve', Qh[1][0:64, :].rearrange("p (r q) -> p r q", r=4), bc_mid(src[0:64, :], [64, 4, 128]),
                               bc_last(b31t[0:64, 4:8], [64, 4, 128]), ALU.add, ['M1', 'b31t'], [kQ[2 * hb + 1] + 'n'])
                        nk = 2 * i + 2
                        for g in range(2):
                            rows = slice(64 * g, 64 * g + 64)
                            for _ in branch(OS, 'OS', list(range(nk)),
                                            lambda kt: (Kaug[g][:, kt * 128:(kt + 1) * 128], [f"Kaug{g}d", f"Kaug{g}e"]),
                                            lambda kt: (VS[:, kt, g, :], ['VSd', 'VSone']),
                                            lambda kt: ((QA if kt < 32 else QB)[g][:, :], [kQ[(0 if kt < 32 else 2) + g] + 'd', kQ[(0 if kt < 32 else 2) + g] + 'n']),
                                            lambda kt: ((SELT[:, nk - 1 - kt, g * 512:(g + 1) * 512], ['SELT']) if nk - 1 - kt <= 8 else None)):
                                pass
                            wk_ts = list(range(max(0, 2 * i - 4), nk))
                            for _ in branch(OW, 'OW', wk_ts,
                                            lambda kt: (KWr[rows, kt % 8, :], [f"KWr{kt % 8}"]),
                                            lambda kt: (VWr[:, kt % 8, g * 65:(g + 1) * 65], [f"VWr{kt % 8}"]),
                                            lambda kt: (QA[g][rows, :], [kQ[g] + 'd']),
                                            lambda kt: (WINT[:, nk - 1 - kt, g * 512:(g + 1) * 512], ['WINT'])):
                                pass
                            den = dsm[:, 8:20].rearrange("p (b r) -> p b r", b=3)
                            cp('dve', den[:, 0, :], OCs[:, g, 64:260:65], [f"OCs{g}"], ['den'])
                            cp('dve', den[:, 1, :], OS[:, 64:260:65], ['OS'], ['den'])
                            cp('dve', den[:, 2, :], OW[:, 64:260:65], ['OW'], ['den'])
                            ts('dve', dsm[:, 8:20], dsm[:, 8:20], 1e-30, None, ALU.max, None, ['den'], ['den'])
                            op('dve', lambda e: e.reciprocal(dsm[:, 8:20], dsm[:, 8:20]), ['den'], ['den'])
                            tt('dve', den, den, sig[:, 12 * g:12 * g + 12].rearrange("p (r b) -> p b r", b=3), ALU.mult, ['den', 'sig'], ['den'])
                            a3 = acc[:].rearrange("p (r d) -> p r d", r=4)
                            a23 = acc2[:].rearrange("p (r d) -> p r d", r=4)
                            srcs = [(OCs[:, g, :], f"OCs{g}"), (OS[:], 'OS'), (OW[:], 'OW')]
                            for br, (Oap, kOb) in enumerate(srcs):
                                s3 = Oap.rearrange("p (r e) -> p r e", e=65)[:, :, 0:64]
                                cbr = bc_last(den[:, br, :], [128, 4, 64])
                                if br == 0:
                                    tt('dve', a3, s3, cbr, ALU.mult, [kOb, 'den'], ['acc'])
                                else:
                                    tt('dve', a23, s3, cbr, ALU.mult, [kOb, 'den'], ['acc2'])
                                    dst = acc[:] if br == 1 else attn_bf[:, b, g * 256:(g + 1) * 256]
                                    tt('pool', dst, acc[:], acc2[:], ALU.add, ['acc', 'acc2'], ['acc'] if br == 1 else [f"attn{b}"])
                        dma('sp', attn_s[i * 128:(i + 1) * 128, :], attn_bf[:, b, :], r=[f"attn{b}"], w=['attn_s'])
                    S.barrier()

            chk('p2')
            with ExitStack() as es6:
                w3c = es6.enter_context(nc.sbuf_tensor("s_w3c", [128, 8, 1536], BF16))
                wo = es6.enter_context(nc.sbuf_tensor("s_wo", [128, 8, 1024], BF16))
                convc = es6.enter_context(nc.sbuf_tensor("s_convc", [128, 12], F32))
                hm = es6.enter_context(nc.sbuf_tensor("s_hm", [128, 64], F32))
                xk = es6.enter_context(nc.sbuf_tensor("s_xk", [128, 4, 1024], F32))
                xhl = es6.enter_context(nc.sbuf_tensor("s_xhl", [8, 1024], F32))
                xh3 = es6.enter_context(nc.sbuf_tensor("s_xh3", [128, 1024], BF16))
                tmp3 = es6.enter_context(nc.sbuf_tensor("s_tmp3", [128, 8, 128], F32))
                hT3 = es6.enter_context(nc.sbuf_tensor("s_hT3", [128, 8, 512], BF16))
                hTh = es6.enter_context(nc.sbuf_tensor("s_hTh", [128, 8, 8], BF16))
                zt = es6.enter_context(nc.sbuf_tensor("s_zt", [128, 4, 130], F32))
                cgs = es6.enter_context(nc.sbuf_tensor("s_cgs", [128, 512], F32))
                cacc = es6.enter_context(nc.sbuf_tensor("s_cacc", [128, 512], F32))
                zh8 = es6.enter_context(nc.sbuf_tensor("s_zh8", [128, 16], F32))
                at4 = es6.enter_context(nc.sbuf_tensor("s_at4", [128, 4, 512], BF16))
                catT = es6.enter_context(nc.sbuf_tensor("s_catT", [128, 8, 512], BF16))
                mixt = es6.enter_context(nc.sbuf_tensor("s_mixt", [128, 512], F32))
                h2T = es6.enter_context(nc.sbuf_tensor("s_h2T", [128, 8, 512], BF16))
                hidT = es6.enter_context(nc.sbuf_tensor("s_hidT", [128, 32, 512], BF16))
                relu = es6.enter_context(nc.sbuf_tensor("s_relu", [128, 2, 512], F32))
                wr1 = es6.enter_context(nc.sbuf_tensor("s_wr1", [128, 3, 8, 512], BF16))
                wr2 = es6.enter_context(nc.sbuf_tensor("s_wr2", [128, 3, 4, 512], BF16))
                ACC0 = es6.enter_context(nc.psum_tensor("p_ACC0", [128, 512], F32))
                ACC1 = es6.enter_context(nc.psum_tensor("p_ACC1", [128, 512], F32))
                ACC2 = es6.enter_context(nc.psum_tensor("p_ACC2", [128, 512], F32))
                ACC3 = es6.enter_context(nc.psum_tensor("p_ACC3", [128, 512], F32))
                PA = es6.enter_context(nc.psum_tensor("p_PA", [128, 512], F32))
                PB = es6.enter_context(nc.psum_tensor("p_PB", [128, 512], F32))
                PC = es6.enter_context(nc.psum_tensor("p_PC", [128, 512], F32))
                PTR = es6.enter_context(nc.psum_tensor("p_PTR", [128, 8, 128], BF16))
                ACC = [ACC0, ACC1, ACC2, ACC3]
                P3 = [PA, PB, PC]
                p3i = [0]

                def nextp():
                    k = p3i[0] % 3
                    p3i[0] += 1
                    return P3[k], f"P3{k}"

                dma('pool', w3c[:], w_in_r[:, :, 1304:2840], w=['w3c'])
                dma('pool', wo[:], w_out.rearrange("(c p) n -> p c n", p=128), w=['wo'])
                dma('sp', convc[:], convc_d, w=['convc'])
                dma('sp', hm[:], hmask, w=['hm'])
                w1n = [0]
                w2n = [0]
                for s in range(8):
                    for j in range(4):
                        kx = f"xk{j}"
                        dma('sp', xk[:, j, :], xown[(4 * s + j) * 128:(4 * s + j + 1) * 128, :], w=[kx])
                    dma('sp', xhl[:], xhalo[s], w=['xhl'])
                    dma('sp', at4[:], attn_s[s * 512:(s + 1) * 512, :].rearrange("(j t) f -> t j f", t=128), r=['attn_s'], w=['at4'])
                    for j in range(4):
                        norm_to_hT(xk[:, j, :], 128, f"xk{j}", xh3[:], 'xh3', PTR, 'PTR', tmp3[:], 'tmp3', hT3[:, :, j * 128:(j + 1) * 128], 'hT3', A1, B1, 'A1', 'B1', 128)
                    norm_to_hT(xhl[:], 8, 'xhl', xh3[0:8, :], 'xh3', PTR, 'PTR', tmp3, 'tmp3', hTh[:], 'hTh', A1, B1, 'A1', 'B1', 8)
                    for j in range(4):
                        for fc in range(4):
                            tp(PTR[:, fc, :], at4[:, j, fc * 128:(fc + 1) * 128], ident_b[:], ['at4', 'ident', 'tmp3'], ['PTR'])
                        cp('act', catT[:, 0:4, j * 128:(j + 1) * 128], PTR[:, 0:4, :], ['PTR'], ['catTa'])
                    for m in range(4):
                        pcg, kcg = nextp()
                        pu, ku = nextp()
                        pbg, kbg = nextp()
                        for c in range(8):
                            mm(pcg[:], w3c[:, c, m * 128:(m + 1) * 128], hT3[:, c, :], c == 0, c == 7, ['w3c', 'hT3'], [kcg])
                        for c in range(8):
                            mm(pu[:], w3c[:, c, 1024 + m * 128:1024 + (m + 1) * 128], hT3[:, c, :], c == 0, c == 7, ['w3c', 'hT3'], [ku])
                        for c in range(8):
                            mm(pbg[:], w3c[:, c, 512 + m * 128:512 + (m + 1) * 128], hT3[:, c, :], c == 0, c == 7, ['w3c', 'hT3'], [kbg])
                        cp('act', cgs[:], pcg[:], [kcg], ['cgs'])
                        tt('dve', zt[:, :, 2:130], cgs[:].rearrange("p (j t) -> p j t", j=4), pu[:].rearrange("p (j t) -> p j t", j=4), ALU.mult, ['cgs', ku], ['ztm'])
                        ph, kph = nextp()
                        for c in range(8):
                            mm(ph[:, 0:8], w3c[:, c, m * 128:(m + 1) * 128], hTh[:, c, :], c == 0, c == 7, ['w3c', 'hTh'], [kph])
                        for c in range(8):
                            mm(ph[:, 8:16], w3c[:, c, 1024 + m * 128:1024 + (m + 1) * 128], hTh[:, c, :], c == 0, c == 7, ['w3c', 'hTh'], [kph])
                        cp('act', zh8[:], ph[:, 0:16], [kph], ['zh8'])
                        tt('dve', zh8[:, 0:8], zh8[:, 0:8], zh8[:, 8:16], ALU.mult, ['zh8'], ['zh8'])
                        tt('dve', zt[:, :, 0:2], zh8[:, 0:8].rearrange("p (j e) -> p j e", j=4), hm[:, s * 8:(s + 1) * 8].rearrange("p (j e) -> p j e", j=4),
                           ALU.mult, ['zh8', 'hm'], ['zth'])
                        c3 = cacc[:].rearrange("p (j t) -> p j t", j=4)
                        ts('dve', c3, zt[:, :, 0:128], convc[:, m * 3:m * 3 + 1], None, ALU.mult, None, ['ztm', 'zth', 'convc'], ['cacc'])
                        stt('dve', c3, zt[:, :, 1:129], convc[:, m * 3 + 1:m * 3 + 2], c3, ALU.mult, ALU.add, ['ztm', 'zth', 'convc', 'cacc'], ['cacc'])
                        stt('dve', c3, zt[:, :, 2:130], convc[:, m * 3 + 2:m * 3 + 3], c3, ALU.mult, ALU.add, ['ztm', 'zth', 'convc', 'cacc'], ['cacc'])
                        tt('dve', catT[:, 4 + m, :], cacc[:], pbg[:], ALU.mult, ['cacc', kbg], ['catTc'])
                    if dbg and s == 0:
                        dma('sp', dbg_cat, catT[:].rearrange("p c t -> p (c t)"), r=['catTa', 'catTc'], w=['dbg_cat'])
                    for j in range(4):
                        for half in range(2):
                            pw, kpw = nextp()
                            for c in range(8):
                                mm(pw[:], catT[:, c, j * 128:(j + 1) * 128], wo[:, c, half * 512:(half + 1) * 512], c == 0, c == 7, ['catTa', 'catTc', 'wo'], [kpw])
                            tt('dve', mixt[:], pw[:], g1bc[:, half * 512:(half + 1) * 512], ALU.mult, [kpw, f"gbc2{half}"], ['mixt'])
                            tt('pool', xk[:, j, half * 512:(half + 1) * 512], xk[:, j, half * 512:(half + 1) * 512], mixt[:], ALU.add, [f"xk{j}", 'mixt'], [f"xk{j}"])
                        norm_to_hT(xk[:, j, :], 128, f"xk{j}", xh3[:], 'xh3', PTR, 'PTR', tmp3[:], 'tmp3', h2T[:, :, j * 128:(j + 1) * 128], 'h2T', A2, B2, 'A2', 'B2', 128)
                    if dbg and s == 0:
                        dma('sp', dbg_x1, xk[:].rearrange("p j f -> p (j f)"), r=[f"xk{j}" for j in range(4)], w=['dbg_x1'])
                        dma('sp', dbg_h2, h2T[:].rearrange("p c t -> p (c t)"), r=['h2T'], w=['dbg_h2'])
                    for p in range(8):
                        wb = w1n[0] % 3
                        w1n[0] += 1
                        dma('sp', wr1[:, wb], wff1_s[:, :, p * 512:(p + 1) * 512], r=['wff1_s'], w=[f"wr1{wb}"])
                        for hc in range(4):
                            pf, kpf = nextp()
                            for c in range(8):
                                mm(pf[:], wr1[:, wb, c, hc * 128:(hc + 1) * 128], h2T[:, c, :], c == 0, c == 7, [f"wr1{wb}", 'h2T'], [kpf])
                            rb = (4 * p + hc) % 2
                            act(relu[:, rb, :], pf[:], AF.Relu, [kpf], [f"relu{rb}"])
                            tt('dve', hidT[:, 4 * p + hc, :], relu[:, rb, :], relu[:, rb, :], ALU.mult, [f"relu{rb}"], ['hidT'])
                    if dbg and s == 0:
                        dma('sp', dbg_hid, hidT[:, 0:4, :].rearrange("p c t -> p (c t)"), r=['hidT'], w=['dbg_hid'])
                    for half in range(2):
                        for p in range(8):
                            wb = w2n[0] % 3
                            w2n[0] += 1
                            dma('sp', wr2[:, wb], wff2_s[:, 4 * p:4 * p + 4, half * 512:(half + 1) * 512], r=['wff2_s'], w=[f"wr2{wb}"])
                            for j in range(4):
                                for hc in range(4):
                                    mm(ACC[j][:], hidT[:, 4 * p + hc, j * 128:(j + 1) * 128], wr2[:, wb, hc, :], p == 0 and hc == 0, p == 7 and hc == 3,
                                       ['hidT', f"wr2{wb}"], [f"ACC{j}"])
                        for j in range(4):
                            tt('dve', mixt[:], ACC[j][:], g2bc[:, half * 512:(half + 1) * 512], ALU.mult, [f"ACC{j}", f"gbc5{half}"], ['mixt'])
                            tt('pool', xk[:, j, half * 512:(half + 1) * 512], xk[:, j, half * 512:(half + 1) * 512], mixt[:], ALU.add, [f"xk{j}", 'mixt'], [f"xk{j}"])
                    for j in range(4):
                        dma('sp', out_own[(4 * s + j) * 128:(4 * s + j + 1) * 128, :], xk[:, j, :], r=[f"xk{j}"], w=[f"out{j}"])

    except _Stop:
        pass
    final()
    return nc


def _bucket(dist):
    n = np.maximum(dist, 0)
    nf = np.maximum(n, 16).astype(np.float32)
    large = 16 + (np.log(nf / np.float32(16)) / np.float32(np.log(64.0)) * np.float32(16)).astype(np.int32)
    large = np.minimum(large, 31)
    return np.where(n < 16, n, large).astype(np.int64)


def _tables(rel_bias, par):
    k = np.arange(128)[:, None]
    q = np.arange(128)[None, :]
    bias9 = np.empty((9, 128, 8, 128), np.float32)
    ms = np.empty((9, 128, 128), np.float32)
    mw = np.empty((6, 128, 128), np.float32)
    for dp in range(9):
        dist = 128 * (dp - 1 + par) + q - k
        bias9[dp] = rel_bias[_bucket(dist)].transpose(0, 2, 1)
        ms[dp] = np.where(dist >= 0, 0.0, NEGM)
        if dp < 6:
            mw[dp] = np.where((dist >= 0) & (dist < 512), 0.0, NEGM)
    cb = np.empty((12, 128, 8, 128), np.float32)
    cm = np.empty((2, 12, 128, 128), np.float32)
    for idx in range(12):
        dpp = 2 * idx + 1
        dist = 128 * (dpp - 1 + par) + q - 16 * k - 31
        cb[idx] = rel_bias[_bucket(dist)].transpose(0, 2, 1)
        cm[0, idx] = np.where(dist >= 0, 0.0, NEGM)
        cm[1, idx] = cm[0, idx]
        cm[1, idx, 127, :] = NEGM
    qq = np.arange(128)[:, None]
    u = np.arange(254)[None, :] - 126 - 2 * par
    cq = (qq >= 64).astype(np.int64)
    keep = (u < cq - 1).astype(np.float32)
    addc = np.where((u >= cq - 1) & (u <= cq), 1.0e6, np.where(u > cq, -1.0, 0.0)).astype(np.float32)
    return (bias9.reshape(9, 128, 1024), ms, mw, cb.reshape(12, 128, 1024), cm, keep, addc)


_PROG = {}


def _prep(x, c, w_in, q_norm, k_norm, cmp_pe_k, cmp_w1_k, cmp_w2_k, cmp_pe_v, cmp_w1_v, cmp_w2_v,
          rel_bias, conv_w, w_out, norm1, norm2, w_ada, b_ada, w_ff1, w_ff2):
    f = lambda a: np.ascontiguousarray(np.asarray(a, dtype=np.float32))
    x = f(x)
    c = f(c)
    rel_bias = f(rel_bias)
    kg = np.arange(8192)
    erows = f(((kg[None, :] // 64) % 64 == np.arange(64)[:, None]))
    n_ = np.arange(512)[:, None] * 16
    j_ = np.arange(128)[None, :] * 64
    ov = np.clip(np.minimum(n_ + 32, j_ + 64) - np.maximum(n_, j_), 0, None) / 32.0
    ov[511] = 0.0
    ov = f(ov.reshape(4, 128, 128).transpose(1, 0, 2).reshape(128, 512))
    ident = f(np.eye(128))
    bd = f(np.kron(np.eye(2), np.ones((64, 64))))
    col8 = lambda v: f(np.asarray(v).reshape(8, 128).T)
    shared = {
        "n1c": col8(norm1[0]), "n2c": col8(norm2[0]),
        "badaC": f(np.asarray(b_ada[0]).reshape(48, 128).T), "badaR": f(np.asarray(b_ada[0]).reshape(1, 6144)),
        "w_ada": f(w_ada[0]), "w_in": f(w_in[0]),
        "qg": f(np.broadcast_to(np.asarray(q_norm[0])[None, :], (128, 64))),
        "knc": f(np.tile(np.asarray(k_norm[0]), 2).reshape(128, 1)),
        "peTk": f(np.tile(np.asarray(cmp_pe_k[0]).T, (2, 1))), "peTv": f(np.tile(np.asarray(cmp_pe_v[0]).T, (2, 1))),
        "w1k": f(cmp_w1_k[0]), "w1v": f(cmp_w1_v[0]), "w2k": f(cmp_w2_k[0]), "w2v": f(cmp_w2_v[0]),
        "convc": f(np.asarray(conv_w[0]).reshape(3, 4, 128).transpose(2, 1, 0).reshape(128, 12)),
        "w_out": f(w_out[0]), "w_ff1": f(w_ff1[0]), "w_ff2": f(w_ff2[0]),
        "b31": f(np.broadcast_to(rel_bias[31][None, :], (128, 8))),
        "erows": erows, "ov": ov, "ident": ident, "bd": bd,
    }
    tabs = [_tables(rel_bias, par) for par in range(2)]
    in_maps = []
    own_idx = []
    for core in range(8):
        b, par = core // 2, core % 2
        tiles = 2 * np.arange(NT) + par
        rows = (tiles[:, None] * 128 + np.arange(128)[None, :]).reshape(-1)
        own_idx.append((b, rows))
        xb = x[b]
        halo = np.zeros((NT, 2, 1024), np.float32)
        hmk = np.ones((NT, 2), np.float32)
        for i, T in enumerate(tiles):
            if T == 0:
                hmk[i] = 0.0
            else:
                halo[i] = xb[T * 128 - 2:T * 128]
        bias9, ms, mw, cb, cm, keep, addc = tabs[par]
        m = dict(shared)
        m.update({
            "xfull": xb, "xown": f(xb[rows]), "xhalo": f(halo.reshape(8, 8, 1024)),
            "hmask": f(np.broadcast_to(hmk.reshape(1, 64), (128, 64))),
            "cT": col8(c[b]),
            "bias9": bias9, "ms": ms, "mw": mw, "cb": cb, "cm": cm, "keep": keep, "addc": addc,
        })
        in_maps.append(m)
    return in_maps, own_idx


def kernel(**inputs):
    in_maps, own_idx = _prep(**inputs)
    if 'nc' not in _PROG:
        _PROG['nc'] = build_program()
    nc = _PROG['nc']
    res = run_bass_kernel_spmd(nc, in_maps, core_ids=list(range(8)))
    out = np.empty((4, 8192, 1024), np.float32)
    for core in range(8):
        b, rows = own_idx[core]
        out[b, rows] = res.results[core]["out_own"]
    return out
```

```python
import numpy as np
from contextlib import ExitStack
import concourse.bass as bass
import concourse.mybir as mybir
from concourse.bass_utils import run_bass_kernel_spmd

F32, BF16 = mybir.dt.float32, mybir.dt.bfloat16
AF = mybir.ActivationFunctionType
ALU = mybir.AluOpType
AX = mybir.AxisListType
NEGM = -30000.0
EPS = 1e-6
NT = 32
NSTEP1 = 16


class Sched:
    EPOCH = 1000
    KD = 12

    def __init__(self, nc):
        self.nc = nc
        self.eng = {'pe': nc.tensor, 'act': nc.scalar, 'dve': nc.vector, 'pool': nc.gpsimd, 'sp': nc.sync}
        self.cnt = {e: 0 for e in self.eng}
        self.sems = {e: [] for e in self.eng}
        self.seen = {e: {} for e in self.eng}
        self.dn = {e: 0 for e in self.eng}
        self.dsems = {e: [nc.alloc_semaphore(f"d_{e}_{i}") for i in range(self.KD)] for e in ('sp', 'pool')}
        self.lastw = {}
        self.readers = {}

    def _sem(self, e, idx):
        ep = idx // self.EPOCH
        while len(self.sems[e]) <= ep:
            self.sems[e].append(self.nc.alloc_semaphore(f"c_{e}_{len(self.sems[e])}"))
        return self.sems[e][ep], idx % self.EPOCH + 1

    def _wait(self, e, tok):
        kind, f, idx = tok
        if kind == 'c':
            if f == e and (e == 'pe' or self.cnt[e] - idx > 3):
                return
            key = ('c', f, idx // self.EPOCH)
            sem, val = self._sem(f, idx)
        else:
            key = ('d', f, idx % self.KD)
            sem, val = self.dsems[f][idx % self.KD], 16 * (idx // self.KD + 1)
        if self.seen[e].get(key, 0) >= val:
            return
        self.seen[e][key] = val
        self.eng[e].wait_ge(sem, val)

    def _deps(self, e, reads, writes):
        toks = []
        for b in reads:
            t = self.lastw.get(b)
            if t is not None:
                toks.append(t)
        for b in writes:
            t = self.lastw.get(b)
            if t is not None:
                toks.append(t)
            toks.extend(self.readers.get(b, ()))
        for t in dict.fromkeys(toks):
            self._wait(e, t)

    def _commit(self, tok, reads, writes):
        for b in reads:
            lst = self.readers.setdefault(b, [])
            lst.append(tok)
            if len(lst) > 24:
                del lst[0:len(lst) - 24]
        for b in writes:
            self.lastw[b] = tok
            self.readers[b] = []

    def op(self, e, fn, r=(), w=()):
        self._deps(e, r, w)
        ins = fn(self.eng[e])
        idx = self.cnt[e]
        sem, _ = self._sem(e, idx)
        ins.then_inc(sem, 1)
        self.cnt[e] = idx + 1
        self._commit(('c', e, idx), r, w)

    def dma(self, q, out, in_, r=(), w=()):
        n = self.dn[q]
        if n >= self.KD:
            self._wait(q, ('d', q, n - self.KD))
        self._deps(q, r, w)
        ins = self.eng[q].dma_start(out=out, in_=in_)
        ins.then_inc(self.dsems[q][n % self.KD], 16)
        self.dn[q] = n + 1
        self._commit(('d', q, n), r, w)

    def barrier(self):
        for e in self.eng:
            for f in self.eng:
                if f != e and self.cnt[f] > 0:
                    self._wait(e, ('c', f, self.cnt[f] - 1))
            for q in self.dsems:
                n = self.dn[q]
                for k in range(max(0, n - self.KD), n):
                    self._wait(e, ('d', q, k))

    def finish(self, e, bufs):
        for b in bufs:
            t = self.lastw.get(b)
            if t is not None:
                self._wait(e, t)


def bc_last(ap, shape):
    return ap.unsqueeze(len(shape) - 1).to_broadcast(list(shape))


def bc_mid(ap, shape):
    return ap.unsqueeze(1).to_broadcast(list(shape))


class _Stop(Exception):
    pass


def build_program(stop=None, dbg=False):
    nc = bass.Bass("TRN2", target_bir_lowering=False)

    def din(name, shape):
        return nc.dram_tensor(name, list(shape), F32, kind="ExternalInput").ap()

    xfull = din("xfull", [8192, 1024])
    xown = din("xown", [4096, 1024])
    xhalo = din("xhalo", [8, 8, 1024])
    hmask = din("hmask", [128, 64])
    cT_d = din("cT", [128, 8])
    n1c_d = din("n1c", [128, 8])
    n2c_d = din("n2c", [128, 8])
    badaC_d = din("badaC", [128, 48])
    badaR_d = din("badaR", [1, 6144])
    w_ada = din("w_ada", [1024, 6144])
    w_in = din("w_in", [1024, 2840])
    qg_d = din("qg", [128, 64])
    knc_d = din("knc", [128, 1])
    peTk_d = din("peTk", [128, 32])
    peTv_d = din("peTv", [128, 32])
    w1k_d = din("w1k", [2048, 256])
    w1v_d = din("w1v", [2048, 256])
    w2k_d = din("w2k", [256, 64])
    w2v_d = din("w2v", [256, 64])
    convc_d = din("convc", [128, 12])
    w_out = din("w_out", [1024, 1024])
    w_ff1 = din("w_ff1", [1024, 4096])
    w_ff2 = din("w_ff2", [4096, 1024])
    bias9_d = din("bias9", [9, 128, 1024])
    ms_d = din("ms", [9, 128, 128])
    mw_d = din("mw", [6, 128, 128])
    cb_d = din("cb", [12, 128, 1024])
    cm_d = din("cm", [2, 12, 128, 128])
    b31_d = din("b31", [128, 8])
    keep_d = din("keep", [128, 254])
    addc_d = din("addc", [128, 254])
    erows_d = din("erows", [64, 8192])
    ov_d = din("ov", [128, 512])
    ident_d = din("ident", [128, 128])
    bd_d = din("bd", [128, 128])
    out_own = nc.dram_tensor("out_own", [4096, 1024], F32, kind="ExternalOutput").ap()
    kw_s = nc.dram_tensor("kw_s", [128, 8192], BF16).ap()
    vw_s = nc.dram_tensor("vw_s", [64, 128, 130], BF16).ap()
    attn_s = nc.dram_tensor("attn_s", [4096, 512], BF16, **({'kind': 'ExternalOutput'} if dbg else {})).ap()
    if dbg:
        dbg_k0 = nc.dram_tensor("dbg_k0", [128, 1024], BF16, kind="ExternalOutput").ap()
        dbg_k1 = nc.dram_tensor("dbg_k1", [128, 1024], BF16, kind="ExternalOutput").ap()
        dbg_vs = nc.dram_tensor("dbg_vs", [128, 1040], BF16, kind="ExternalOutput").ap()
        dbg_kct = nc.dram_tensor("dbg_kct", [128, 512], BF16, kind="ExternalOutput").ap()
        dbg_vc = nc.dram_tensor("dbg_vc", [128, 520], BF16, kind="ExternalOutput").ap()
        dbg_g1 = nc.dram_tensor("dbg_g1", [128, 1024], F32, kind="ExternalOutput").ap()
        dbg_ab = nc.dram_tensor("dbg_ab", [128, 32], F32, kind="ExternalOutput").ap()
        dbg_cat = nc.dram_tensor("dbg_cat", [128, 4096], BF16, kind="ExternalOutput").ap()
        dbg_x1 = nc.dram_tensor("dbg_x1", [128, 4096], F32, kind="ExternalOutput").ap()
        dbg_h2 = nc.dram_tensor("dbg_h2", [128, 4096], BF16, kind="ExternalOutput").ap()
        dbg_hid = nc.dram_tensor("dbg_hid", [128, 2048], BF16, kind="ExternalOutput").ap()
    wff1_s = nc.dram_tensor("wff1_s", [128, 8, 4096], BF16).ap()
    wff2_s = nc.dram_tensor("wff2_s", [128, 32, 1024], BF16).ap()

    S = Sched(nc)
    op, dma = S.op, S.dma

    def chk(label):
        if stop == label:
            raise _Stop()

    def final():
        for q in ('sp', 'pool'):
            n = S.dn[q]
            for k in range(max(0, n - S.KD), n):
                S._wait('sp', ('d', q, k))

    def bk(*aps):
        ks = []
        for a in aps:
            t = getattr(a, 'tensor', a)
            nm = getattr(t, 'name', None)
            if isinstance(nm, str) and nm.startswith('p_'):
                ks.append('BANK_' + nm)
        return ks

    def mm(out, lhsT, rhs, start, stop, r, w):
        op('pe', lambda e: e.matmul(out, lhsT, rhs, start=start, stop=stop), r, list(w) + bk(out))

    def tp(out, in_, ident, r, w):
        op('pe', lambda e: e.transpose(out, in_, ident), r, list(w) + bk(out))

    def act(out, in_, func, r, w, scale=1.0, bias=None, accum=None):
        kw = {}
        if bias is not None:
            kw['bias'] = bias
        if accum is not None:
            kw['accum_out'] = accum
        op('act', lambda e: e.activation(out=out, in_=in_, func=func, scale=scale, **kw), r, list(w) + bk(out, in_))

    def tt(eng, out, in0, in1, o, r, w):
        op(eng, lambda e: e.tensor_tensor(out=out, in0=in0, in1=in1, op=o), r, list(w) + bk(out, in0, in1))

    def ts(eng, out, in0, s1, s2, o0, o1, r, w):
        if o1 is None:
            op(eng, lambda e: e.tensor_scalar(out, in0, s1, None, op0=o0), r, list(w) + bk(out, in0))
        else:
            op(eng, lambda e: e.tensor_scalar(out, in0, s1, s2, op0=o0, op1=o1), r, list(w) + bk(out, in0))

    def stt(eng, out, in0, sc, in1, o0, o1, r, w):
        op(eng, lambda e: e.scalar_tensor_tensor(out=out, in0=in0, scalar=sc, in1=in1, op0=o0, op1=o1), r, list(w) + bk(out, in0, in1))

    def cp(eng, out, in_, r, w):
        if eng == 'act':
            op('act', lambda e: e.copy(out, in_), r, list(w) + bk(out, in_))
        else:
            op(eng, lambda e: e.tensor_copy(out, in_), r, list(w) + bk(out, in_))

    def ms_(eng, ap, val, w):
        op(eng, lambda e: e.memset(ap, val), (), w)

    try:
        w_in_r = w_in.rearrange("(c p) n -> p c n", p=128)

        with ExitStack() as es1:
            ident_b = es1.enter_context(nc.sbuf_tensor("s_ident_b", [128, 128], BF16))
            bd_b = es1.enter_context(nc.sbuf_tensor("s_bd_b", [128, 128], BF16))
            A1 = es1.enter_context(nc.sbuf_tensor("s_A1", [128, 8], F32))
            B1 = es1.enter_context(nc.sbuf_tensor("s_B1", [128, 8], F32))
            A2 = es1.enter_context(nc.sbuf_tensor("s_A2", [128, 8], F32))
            B2 = es1.enter_context(nc.sbuf_tensor("s_B2", [128, 8], F32))
            g1bc = es1.enter_context(nc.sbuf_tensor("s_g1bc", [128, 1024], F32))
            g2bc = es1.enter_context(nc.sbuf_tensor("s_g2bc", [128, 1024], F32))
            knc = es1.enter_context(nc.sbuf_tensor("s_knc", [128, 1], F32))
            b31t = es1.enter_context(nc.sbuf_tensor("s_b31t", [128, 8], F32))
            ss = es1.enter_context(nc.sbuf_tensor("s_ss", [128, 8], F32))
            junk = es1.enter_context(nc.sbuf_tensor("s_junk", [128, 1024], BF16))
            ss_i = [0]

            def rms_rstd(xt, P, kx):
                k = ss_i[0] % 4
                ss_i[0] += 1
                a, b_ = ss[0:P, 2 * k:2 * k + 1], ss[0:P, 2 * k + 1:2 * k + 2]
                ka, kb = f"ss{2 * k}", f"ss{2 * k + 1}"
                act(junk[0:P, :], xt, AF.Square, [kx], ['junk', ka], accum=a)
                act(b_, a, AF.Ln, [ka], [kb], scale=1.0 / 1024, bias=EPS)
                act(a, b_, AF.Exp, [kb], [ka], scale=-0.5)
                return a, ka

            dma('pool', ident_b[:], ident_d, w=['ident'])
            dma('pool', bd_b[:], bd_d, w=['bd'])
            dma('sp', knc[:], knc_d, w=['knc'])
            dma('sp', b31t[:], b31_d, w=['b31t'])
            with ExitStack() as es2:
                cT = es2.enter_context(nc.sbuf_tensor("s_cT", [128, 8], F32))
                n1c = es2.enter_context(nc.sbuf_tensor("s_n1c", [128, 8], F32))
                n2c = es2.enter_context(nc.sbuf_tensor("s_n2c", [128, 8], F32))
                badaC = es2.enter_context(nc.sbuf_tensor("s_badaC", [128, 48], F32))
                badaR = es2.enter_context(nc.sbuf_tensor("s_badaR", [1, 6144], BF16))
                e1 = es2.enter_context(nc.sbuf_tensor("s_e1", [128, 8], F32))
                scT = es2.enter_context(nc.sbuf_tensor("s_scT", [128, 8], BF16))
                screp = es2.enter_context(nc.sbuf_tensor("s_screp", [128, 8, 128], BF16))
                ones1 = es2.enter_context(nc.sbuf_tensor("s_ones1", [1, 128], BF16))
                modc = es2.enter_context(nc.sbuf_tensor("s_modc", [128, 6, 8], F32))
                wst0 = es2.enter_context(nc.sbuf_tensor("s_wst0", [128, 8, 512], BF16))
                wst1 = es2.enter_context(nc.sbuf_tensor("s_wst1", [128, 8, 512], BF16))
                pm0 = es2.enter_context(nc.psum_tensor("p_pm0", [128, 512], F32))
                pm1 = es2.enter_context(nc.psum_tensor("p_pm1", [128, 512], F32))
                pcol = es2.enter_context(nc.psum_tensor("p_pcol", [128, 8], F32))
                wst = [wst0, wst1]
                pm = [pm0, pm1]
                dma('sp', cT[:], cT_d, w=['cT'])
                dma('sp', n1c[:], n1c_d, w=['n1c'])
                dma('sp', n2c[:], n2c_d, w=['n2c'])
                dma('sp', badaC[:], badaC_d, w=['badaC'])
                dma('pool', badaR[:], badaR_d, w=['badaR'])
                act(e1[:], cT[:], AF.Exp, ['cT'], ['e1'], scale=-1.0)
                ts('dve', e1[:], e1[:], 1.0, None, ALU.add, None, ['e1'], ['e1'])
                op('dve', lambda e: e.reciprocal(e1[:], e1[:]), ['e1'], ['e1'])
                tt('dve', scT[:], cT[:], e1[:], ALU.mult, ['cT', 'e1'], ['scT'])
                cp('dve', screp[:], bc_last(scT[:], [128, 8, 128]), ['scT'], ['screp'])
                ms_('dve', ones1[:], 1.0, ['ones1'])
                w_ada_r = w_ada.rearrange("(c p) n -> p c n", p=128)
                gbc = {2: g1bc, 5: g2bc}
                for n in range(12):
                    vec, half = n // 2, n % 2
                    wt = wst[n % 2]
                    kwt = f"wst{n % 2}"
                    dma('pool', wt[:], w_ada_r[:, :, n * 512:(n + 1) * 512], w=[kwt])
                    if vec in gbc:
                        p = pm[half]
                        kp = f"pm{half}"
                        for c in range(8):
                            mm(p[:], screp[:, c, :], wt[:, c, :], c == 0, False, [kwt, 'screp'], [kp])
                        mm(p[:], ones1[0:1, :], badaR[0:1, n * 512:(n + 1) * 512], False, True, ['ones1', 'badaR'], [kp])
                        cp('act', gbc[vec][:, half * 512:(half + 1) * 512], p[:], [kp], [f"gbc{vec}{half}"])
                    else:
                        for cc in range(4):
                            for c in range(8):
                                mm(pcol[:, cc:cc + 1], wt[:, c, cc * 128:(cc + 1) * 128], scT[:, c:c + 1], c == 0, c == 7, [kwt, 'scT'], ['pcol'])
                        tt('dve', modc[:, vec, half * 4:(half + 1) * 4], pcol[:, 0:4], badaC[:, n * 4:(n + 1) * 4], ALU.add, ['pcol', 'badaC'], ['modc'])
                ts('dve', e1[:], modc[:, 1, :], 1.0, None, ALU.add, None, ['modc', 'e1'], ['e1'])
                tt('dve', A1[:], e1[:], n1c[:], ALU.mult, ['e1', 'n1c'], ['A1'])
                cp('dve', B1[:], modc[:, 0, :], ['modc'], ['B1'])
                ts('dve', e1[:], modc[:, 4, :], 1.0, None, ALU.add, None, ['modc', 'e1', 'A1'], ['e1'])
                tt('dve', A2[:], e1[:], n2c[:], ALU.mult, ['e1', 'n2c'], ['A2'])
                cp('dve', B2[:], modc[:, 3, :], ['modc'], ['B2'])
                S.barrier()

            if dbg:
                dma('sp', dbg_g1, g1bc[:], r=['gbc20', 'gbc21'], w=['dbg_g1'])
                dma('sp', dbg_ab[:, 0:8], A1[:], r=['A1'], w=['dbg_ab0'])
                dma('sp', dbg_ab[:, 8:16], B1[:], r=['B1'], w=['dbg_ab1'])
                dma('sp', dbg_ab[:, 16:24], A2[:], r=['A2'], w=['dbg_ab2'])
                dma('sp', dbg_ab[:, 24:32], B2[:], r=['B2'], w=['dbg_ab3'])
            chk('p0')

            def norm_to_hT(xt, P, kx, xh, kxh, ptr, kptr, tmpf, ktmp, hT_dst, khT, A, B, kA, kB, ncols):
                rstd, kr = rms_rstd(xt, P, kx)
                ts('dve', xh[0:P, :], xt, rstd, None, ALU.mult, None, [kx, kr], [kxh])
                for c in range(8):
                    tp(ptr[:, c, 0:P], xh[0:P, c * 128:(c + 1) * 128], ident_b[0:P, 0:P], [kxh, 'ident'], [kptr])
                tt('dve', tmpf[:, :, 0:P], ptr[:, :, 0:P], bc_last(A[:], [128, 8, P]), ALU.mult, [kptr, kA], [ktmp])
                tt('pool', hT_dst, tmpf[:, :, 0:P], bc_last(B[:], [128, 8, P]), ALU.add, [ktmp, kB], [khT])

            with ExitStack() as es3:
                Kaug0 = es3.enter_context(nc.sbuf_tensor("s_Kaug0", [128, 8192], BF16))
                Kaug1 = es3.enter_context(nc.sbuf_tensor("s_Kaug1", [128, 8192], BF16))
                VS = es3.enter_context(nc.sbuf_tensor("s_VS", [128, 64, 2, 65], BF16))
                KCT = es3.enter_context(nc.sbuf_tensor("s_KCT", [128, 512], BF16))
                VC = es3.enter_context(nc.sbuf_tensor("s_VC", [128, 4, 2, 65], BF16))
                Kaug = [Kaug0, Kaug1]
                dma('pool', Kaug0[64:128, :], erows_d, w=['Kaug0e'])
                dma('pool', Kaug1[0:64, :], erows_d, w=['Kaug1e'])
                ms_('pool', VS[:, :, :, 64:65], 1.0, ['VSone'])
                ms_('pool', VC[:, :, :, 64:65], 1.0, ['VCone'])

                with ExitStack() as es4:
                    w1p = es4.enter_context(nc.sbuf_tensor("s_w1p", [128, 8, 768], BF16))
                    xr = es4.enter_context(nc.sbuf_tensor("s_xr", [128, 3, 1024], F32))
                    xh = es4.enter_context(nc.sbuf_tensor("s_xh", [128, 2, 1024], BF16))
                    tmpf = es4.enter_context(nc.sbuf_tensor("s_tmpf", [128, 2, 8, 128], F32))
                    hT = es4.enter_context(nc.sbuf_tensor("s_hT", [128, 2, 8, 512], BF16))
                    rawk = es4.enter_context(nc.sbuf_tensor("s_rawk", [128, 8192], BF16))
                    rawv = es4.enter_context(nc.sbuf_tensor("s_rawv", [128, 8192], BF16))
                    W1k = es4.enter_context(nc.sbuf_tensor("s_W1k", [128, 32, 256], BF16))
                    W1v = es4.enter_context(nc.sbuf_tensor("s_W1v", [128, 32, 256], BF16))
                    sq = es4.enter_context(nc.sbuf_tensor("s_sq", [128, 512], BF16))
                    rs = es4.enter_context(nc.sbuf_tensor("s_rs", [128, 512], F32))
                    kwst = es4.enter_context(nc.sbuf_tensor("s_kwst", [128, 2, 512], BF16))
                    vwst = es4.enter_context(nc.sbuf_tensor("s_vwst", [128, 2, 2, 65], BF16))
                    peTk = es4.enter_context(nc.sbuf_tensor("s_peTk", [128, 32], BF16))
                    peTv = es4.enter_context(nc.sbuf_tensor("s_peTv", [128, 32], BF16))
                    cbias = es4.enter_context(nc.sbuf_tensor("s_cbias", [128, 4], F32))
                    gx = es4.enter_context(nc.sbuf_tensor("s_gx", [128, 512], F32))
                    g2 = es4.enter_context(nc.sbuf_tensor("s_g2", [128, 512], F32))
                    gel = es4.enter_context(nc.sbuf_tensor("s_gel", [128, 4, 2, 512], BF16))
                    w2kp = es4.enter_context(nc.sbuf_tensor("s_w2kp", [128, 2, 2, 128], BF16))
                    w2vb = es4.enter_context(nc.sbuf_tensor("s_w2vb", [128, 2, 64], BF16))
                    ptr0 = es4.enter_context(nc.psum_tensor("p_ptr0", [128, 8, 128], BF16))
                    ptr1 = es4.enter_context(nc.psum_tensor("p_ptr1", [128, 8, 128], BF16))
                    pf0 = es4.enter_context(nc.psum_tensor("p_pf0", [128, 512], F32))
                    pf1 = es4.enter_context(nc.psum_tensor("p_pf1", [128, 512], F32))
                    pbd = es4.enter_context(nc.psum_tensor("p_pbd", [128, 512], F32))
                    ptok = es4.enter_context(nc.psum_tensor("p_ptok", [128, 256], F32))
                    pcb = es4.enter_context(nc.psum_tensor("p_pcb", [128, 8], F32))
                    ptr = [ptr0, ptr1]
                    pfb = [pf0, pf1]
                    pfi = [0]
                    dma('pool', w1p[:], w_in_r[:, :, 512:1280], w=['w1p'])
                    for kv, (W1, w1d, peT, ped) in enumerate(((W1k, w1k_d, peTk, peTk_d), (W1v, w1v_d, peTv, peTv_d))):
                        w1r = w1d.rearrange("(l d) c -> d l c", d=64)
                        dma('pool', W1[0:64, :, :], w1r, w=[f"W1{kv}a"])
                        dma('pool', W1[64:128, :, :], w1r, w=[f"W1{kv}b"])
                        dma('pool', peT[:], ped, w=[f"peT{kv}"])
                    ms_('dve', w2kp[:], 0.0, ['w2kp'])
                    for cc in range(2):
                        for g in range(2):
                            dma('pool', w2kp[:, cc, g, 64 * g:64 * g + 64], w2k_d[cc * 128:(cc + 1) * 128, :], r=['w2kp'], w=[f"w2kp{cc}{g}"])
                        dma('pool', w2vb[:, cc, :], w2v_d[cc * 128:(cc + 1) * 128, :], w=[f"w2vb{cc}"])
                    ms_('pool', vwst[:, :, :, 64:65], 1.0, ['vwst1'])
                    ms_('dve', gel[:, :, :, 511:512], 0.0, ['gelz'])

                    def stacked_norm(pf, kpf, N, outs):
                        act(sq[:, 0:N], pf[:, 0:N], AF.Square, [kpf], ['sq'])
                        mm(pbd[:, 0:N], bd_b[:], sq[:, 0:N], True, True, ['bd', 'sq'], ['pbd'])
                        act(rs[:, 0:N], pbd[:, 0:N], AF.Ln, ['pbd'], ['rs'], scale=1.0 / 64, bias=EPS)
                        act(rs[:, 0:N], rs[:, 0:N], AF.Exp, ['rs'], ['rs'], scale=-0.5)
                        for dst, p0, p1, kd in outs:
                            stt('dve', dst, pf[p0:p1, 0:N], knc[p0:p1, 0:1], rs[p0:p1, 0:N], ALU.mult, ALU.mult, [kpf, 'rs', 'knc'], [kd])

                    chk('p1a')
                    wff_jobs = [('f1', c) for c in range(8)] + [('f2', c) for c in range(32)]
                    for s in range(NSTEP1):
                        hTs = hT[:, s % 2]
                        khT = f"hT{s % 2}"
                        for j in range(4):
                            T = 4 * s + j
                            xt = xr[:, T % 3, :]
                            kx = f"xr{T % 3}"
                            dma('sp', xt, xfull[T * 128:(T + 1) * 128, :], w=[kx])
                            norm_to_hT(xt, 128, kx, xh[:, T % 2, :], f"xh{T % 2}", ptr[T % 2], f"ptr{T % 2}",
                                       tmpf[:, T % 2], f"tmpf{T % 2}", hTs[:, :, j * 128:(j + 1) * 128], khT, A1, B1, 'A1', 'B1', 128)
                        tok = slice(s * 512, (s + 1) * 512)
                        for name, col in (('ks', 256), ('kw', 512), ('kc', 0), ('vc', 128)):
                            pf = pfb[pfi[0] % 2]
                            kpf = f"pf{pfi[0] % 2}"
                            pfi[0] += 1
                            for c in range(8):
                                mm(pf[:], w1p[:, c, col:col + 128], hTs[:, c, :], c == 0, c == 7, ['w1p', khT], [kpf])
                            if name == 'kc':
                                cp('act', rawk[:, tok], pf[:], [kpf], ['rawk'])
                            elif name == 'vc':
                                cp('act', rawv[:, tok], pf[:], [kpf], ['rawv'])
                            elif name == 'ks':
                                stacked_norm(pf, kpf, 512, [(Kaug0[0:64, tok], 0, 64, 'Kaug0d'), (Kaug1[64:128, tok], 64, 128, 'Kaug1d')])
                            else:
                                stacked_norm(pf, kpf, 512, [(kwst[:, s % 2, :], 0, 128, f"kwst{s % 2}")])
                                dma('sp', kw_s[:, tok], kwst[:, s % 2, :], r=[f"kwst{s % 2}"], w=['kw_s'])
                        for j in range(4):
                            T = 4 * s + j
                            for c in range(8):
                                mm(ptok[:, 0:128], hTs[:, c, j * 128:(j + 1) * 128], w1p[:, c, 384:512], c == 0, c == 7, ['w1p', khT], ['ptok'])
                            for c in range(8):
                                mm(ptok[:, 128:256], hTs[:, c, j * 128:(j + 1) * 128], w1p[:, c, 640:768], c == 0, c == 7, ['w1p', khT], ['ptok'])
                            cp('act', VS[:, T, :, 0:64], ptok[:, 0:128].rearrange("p (g d) -> p g d", g=2), ['ptok'], ['VSd'])
                            cp('dve', vwst[:, T % 2, :, 0:64], ptok[:, 128:256].rearrange("p (g d) -> p g d", g=2), ['ptok'], [f"vwst{T % 2}"])
                            dma('sp', vw_s[T], vwst[:, T % 2].rearrange("p g d -> p (g d)"), r=[f"vwst{T % 2}", 'vwst1'], w=['vw_s'])
                        if s == 0:
                            chk('p1b')
                        for _ in range(3):
                            if wff_jobs:
                                kind, c = wff_jobs.pop(0)
                                if kind == 'f1':
                                    dma('pool', wff1_s[:, c, :], w_ff1[c * 128:(c + 1) * 128, :], w=['wff1_s'])
                                else:
                                    dma('pool', wff2_s[:, c, :], w_ff2[c * 128:(c + 1) * 128, :], w=['wff2_s'])
                    while wff_jobs:
                        kind, c = wff_jobs.pop(0)
                        if kind == 'f1':
                            dma('pool', wff1_s[:, c, :], w_ff1[c * 128:(c + 1) * 128, :], w=['wff1_s'])
                        else:
                            dma('pool', wff2_s[:, c, :], w_ff2[c * 128:(c + 1) * 128, :], w=['wff2_s'])

                    chk('p1c')
                    for kv, (W1, peT) in enumerate(((W1k, peTk), (W1v, peTv))):
                        for cc in range(2):
                            for l in range(32):
                                mm(pcb[:, kv * 2 + cc:kv * 2 + cc + 1], W1[0:64, l, cc * 128:(cc + 1) * 128], peT[0:64, l:l + 1],
                                   l == 0, l == 31, [f"W1{kv}a", f"peT{kv}"], ['pcb'])
                    cp('dve', cbias[:], pcb[:, 0:4], ['pcb'], ['cbias'])
                    for kv, (W1, raw, kraw) in enumerate(((W1k, rawk, 'rawk'), (W1v, rawv, 'rawv'))):
                        for g in range(2):
                            rows = slice(64 * g, 64 * g + 64)
                            for cc in range(2):
                                pf = pfb[pfi[0] % 2]
                                kpf = f"pf{pfi[0] % 2}"
                                pfi[0] += 1
                                for l in range(32):
                                    mm(pf[:, 0:511], W1[rows, l, cc * 128:(cc + 1) * 128], raw[rows, l:l + 16 * 510 + 1:16], l == 0, l == 31,
                                       [f"W1{kv}a", f"W1{kv}b", kraw], [kpf])
                                kg = f"gel{kv}{g}"
                                act(gx[:, 0:511], pf[:, 0:511], AF.Identity, [kpf, 'cbias'], ['gx'], bias=cbias[:, kv * 2 + cc:kv * 2 + cc + 1])
                                tt('dve', g2[:, 0:511], gx[:, 0:511], gx[:, 0:511], ALU.mult, ['gx'], ['g2'])
                                ts('dve', g2[:, 0:511], g2[:, 0:511], 0.044715, 1.0, ALU.mult, ALU.add, ['g2'], ['g2'])
                                tt('dve', g2[:, 0:511], g2[:, 0:511], gx[:, 0:511], ALU.mult, ['g2', 'gx'], ['g2'])
                                act(g2[:, 0:511], g2[:, 0:511], AF.Tanh, ['g2'], ['g2'], scale=0.7978845608028654)
                                ts('dve', g2[:, 0:511], g2[:, 0:511], 0.5, 0.5, ALU.mult, ALU.add, ['g2'], ['g2'])
                                tt('dve', gel[:, kv * 2 + g, cc, 0:511], g2[:, 0:511], gx[:, 0:511], ALU.mult, ['g2', 'gx', 'gelz'], [kg])
                    pf = pfb[pfi[0] % 2]
                    kpf = f"pf{pfi[0] % 2}"
                    pfi[0] += 1
                    n_ = 0
                    for g in range(2):
                        for cc in range(2):
                            mm(pf[:], w2kp[:, cc, g, :], gel[:, g, cc, :], n_ == 0, n_ == 3, [f"w2kp{cc}{g}", f"gel0{g}", 'gelz'], [kpf])
                            n_ += 1
                    stacked_norm(pf, kpf, 512, [(KCT[:], 0, 128, 'KCT')])
                    for ct in range(4):
                        for g in range(2):
                            for cc in range(2):
                                mm(ptok[:, 0:64], gel[:, 2 + g, cc, ct * 128:(ct + 1) * 128], w2vb[:, cc, :], cc == 0, cc == 1,
                                   [f"gel1{g}", 'gelz', f"w2vb{cc}"], ['ptok'])
                            cp('act', VC[:, ct, g, 0:64], ptok[:, 0:64], ['ptok'], ['VCd'])
                    S.barrier()

                if dbg:
                    dma('sp', dbg_k0, Kaug0[:, 0:1024], r=['Kaug0d', 'Kaug0e'], w=['dbg_k0'])
                    dma('sp', dbg_k1, Kaug1[:, 0:1024], r=['Kaug1d', 'Kaug1e'], w=['dbg_k1'])
                    dma('sp', dbg_vs, VS[:, 0:8].rearrange("p t g d -> p (t g d)"), r=['VSd', 'VSone'], w=['dbg_vs'])
                    dma('sp', dbg_kct, KCT[:], r=['KCT'], w=['dbg_kct'])
                    dma('sp', dbg_vc, VC[:].rearrange("p t g d -> p (t g d)"), r=['VCd', 'VCone'], w=['dbg_vc'])
                chk('p1')
                with ExitStack() as es5:
                    w2p = es5.enter_context(nc.sbuf_tensor("s_w2p", [128, 8, 536], BF16))
                    SELT = es5.enter_context(nc.sbuf_tensor("s_SELT", [128, 9, 1024], BF16))
                    WINT = es5.enter_context(nc.sbuf_tensor("s_WINT", [128, 6, 1024], BF16))
                    CFAR = es5.enter_context(nc.sbuf_tensor("s_CFAR", [128, 1024], BF16))
                    ctab = es5.enter_context(nc.sbuf_tensor("s_ctab", [128, 4, 1024], BF16))
                    stg = es5.enter_context(nc.sbuf_tensor("s_stg", [128, 2, 1024], F32))
                    stg2 = es5.enter_context(nc.sbuf_tensor("s_stg2", [128, 1024], F32))
                    mst = es5.enter_context(nc.sbuf_tensor("s_mst", [128, 4, 128], F32))
                    keep = es5.enter_context(nc.sbuf_tensor("s_keep", [128, 254], F32))
                    addc = es5.enter_context(nc.sbuf_tensor("s_addc", [128, 254], F32))
                    OVb = es5.enter_context(nc.sbuf_tensor("s_OVb", [128, 512], BF16))
                    qg = es5.enter_context(nc.sbuf_tensor("s_qg", [128, 64], F32))
                    KWr = es5.enter_context(nc.sbuf_tensor("s_KWr", [128, 8, 128], BF16))
                    VWr = es5.enter_context(nc.sbuf_tensor("s_VWr", [128, 8, 130], BF16))
                    xo = es5.enter_context(nc.sbuf_tensor("s_xo", [128, 2, 1024], F32))
                    xho = es5.enter_context(nc.sbuf_tensor("s_xho", [128, 1024], BF16))
                    tmpo = es5.enter_context(nc.sbuf_tensor("s_tmpo", [128, 8, 128], F32))
                    hTo = es5.enter_context(nc.sbuf_tensor("s_hTo", [128, 8, 128], BF16))
                    qsq = es5.enter_context(nc.sbuf_tensor("s_qsq", [128, 512], F32))
                    qss = es5.enter_context(nc.sbuf_tensor("s_qss", [128, 16], F32))
                    qtmp = es5.enter_context(nc.sbuf_tensor("s_qtmp", [128, 512], F32))
                    qnp = es5.enter_context(nc.sbuf_tensor("s_qnp", [128, 512], BF16))
                    QQ = es5.enter_context(nc.sbuf_tensor("s_QQ", [128, 2, 4, 512], BF16))
                    sig = es5.enter_context(nc.sbuf_tensor("s_sig", [128, 24], F32))
                    PT = es5.enter_context(nc.sbuf_tensor("s_PT", [128, 4, 512], BF16))
                    OCs = es5.enter_context(nc.sbuf_tensor("s_OCs", [128, 2, 260], F32))
                    dsm = es5.enter_context(nc.sbuf_tensor("s_dsm", [128, 32], F32))
                    imp = es5.enter_context(nc.sbuf_tensor("s_imp", [128, 128], F32))
                    scr = es5.enter_context(nc.sbuf_tensor("s_scr", [128, 128], F32))
                    wk = es5.enter_context(nc.sbuf_tensor("s_wk", [128, 128], F32))
                    m8 = es5.enter_context(nc.sbuf_tensor("s_m8", [128, 16], F32))
                    NS = es5.enter_context(nc.sbuf_tensor("s_NS", [128, 2, 128], BF16))
                    acc = es5.enter_context(nc.sbuf_tensor("s_acc", [128, 256], F32))
                    acc2 = es5.enter_context(nc.sbuf_tensor("s_acc2", [128, 256], F32))
                    attn_bf = es5.enter_context(nc.sbuf_tensor("s_attn_bf", [128, 2, 512], BF16))
                    ST0 = es5.enter_context(nc.psum_tensor("p_ST0", [128, 512], F32))
                    ST1 = es5.enter_context(nc.psum_tensor("p_ST1", [128, 512], F32))
                    OC = es5.enter_context(nc.psum_tensor("p_OC", [128, 260], F32))
                    IMP = es5.enter_context(nc.psum_tensor("p_IMP", [128, 512], F32))
                    OS = es5.enter_context(nc.psum_tensor("p_OS", [128, 260], F32))
                    OW = es5.enter_context(nc.psum_tensor("p_OW", [128, 260], F32))
                    M0 = es5.enter_context(nc.psum_tensor("p_M0", [128, 512], F32))
                    M1 = es5.enter_context(nc.psum_tensor("p_M1", [128, 1024], BF16))
                    ST = [ST0, ST1]
                    dma('pool', w2p[:, :, 0:512], w_in_r[:, :, 0:512], w=['w2pq'])
                    dma('pool', w2p[:, :, 512:536], w_in_r[:, :, 1280:1304], w=['w2pg'])
                    dma('sp', keep[:], keep_d, w=['keep'])
                    dma('sp', addc[:], addc_d, w=['addc'])
                    dma('pool', OVb[:], ov_d, w=['OVb'])
                    dma('sp', qg[:], qg_d, w=['qg'])
                    ts('dve', qg[:], qg[:], 0.125, None, ALU.mult, None, ['qg'], ['qg'])
                    cp('dve', CFAR[:].rearrange("p (h q) -> p h q", h=8), bc_last(b31t[:], [128, 8, 128]), ['b31t'], ['CFAR'])
                    b31b = bc_last(b31t[:], [128, 8, 128])
                    for d in range(9):
                        sg = stg[:, d % 2, :]
                        ksg = f"stg{d % 2}"
                        dma('sp', sg, bias9_d[d], w=[ksg])
                        dma('sp', mst[:, d % 2, :], ms_d[d], w=[f"mst{d % 2}"])
                        if d < 6:
                            dma('sp', mst[:, 2 + d % 2, :], mw_d[d], w=[f"mst{2 + d % 2}"])
                            tt('pool', WINT[:, d, :].rearrange("p (h q) -> p h q", h=8), sg.rearrange("p (h q) -> p h q", h=8),
                               bc_mid(mst[:, 2 + d % 2, :], [128, 8, 128]), ALU.add, [ksg, f"mst{2 + d % 2}"], ['WINT'])
                        tt('dve', stg2[:].rearrange("p (h q) -> p h q", h=8), sg.rearrange("p (h q) -> p h q", h=8), b31b, ALU.subtract, [ksg, 'b31t'], ['stg2'])
                        tt('dve', SELT[:, d, :].rearrange("p (h q) -> p h q", h=8), stg2[:].rearrange("p (h q) -> p h q", h=8),
                           bc_mid(mst[:, d % 2, :], [128, 8, 128]), ALU.add, ['stg2', f"mst{d % 2}"], ['SELT'])

                    cnt = [0]
                    tabn = [0]

                    def branch(O, kO, kts, kfn, vfn, qfn, tabfn):
                        nk = len(kts)
                        state = []

                        def emit_qk(n):
                            kt = kts[n]
                            st = ST[cnt[0] % 2]
                            kst = f"ST{cnt[0] % 2}"
                            pt = PT[:, cnt[0] % 4, :]
                            kpt = f"PT{cnt[0] % 4}"
                            cnt[0] += 1
                            lk, rk = kfn(kt)
                            qa, rq = qfn(kt)
                            tab = tabfn(kt)
                            mm(st[:], lk, qa, True, tab is None, rk + rq, [kst])
                            if tab is not None:
                                mm(st[:], ident_b[:], tab[0], False, True, ['ident'] + tab[1], [kst])
                            act(pt, st[:], AF.Exp, [kst], [kpt])
                            state.append((pt, kpt))

                        emit_qk(0)
                        for n in range(nk):
                            if n + 1 < nk:
                                emit_qk(n + 1)
                            pt, kpt = state[n]
                            vv, rv = vfn(kts[n])
                            for r in range(4):
                                mm(O[:, r * 65:(r + 1) * 65], pt[:, r * 128:(r + 1) * 128], vv, n == 0 and r == 0, n == nk - 1, [kpt] + rv, [kO])
                            yield pt, kpt, n

                    for i in range(NT):
                        xt = xo[:, i % 2, :]
                        kx = f"xo{i % 2}"
                        dma('sp', xt, xown[i * 128:(i + 1) * 128, :], w=[kx])
                        for T in (2 * i, 2 * i + 1):
                            dma('sp', KWr[:, T % 8, :], kw_s[:, T * 128:(T + 1) * 128], r=['kw_s'], w=[f"KWr{T % 8}"])
                            dma('sp', VWr[:, T % 8, :], vw_s[T], r=['vw_s'], w=[f"VWr{T % 8}"])
                        nct = (2 * i + 1) // 16 + 1
                        ctabs = {}
                        for ct in range(nct):
                            dpp = 2 * i + 1 - 16 * ct
                            if dpp <= 23:
                                idx = (dpp - 1) // 2
                                sl = tabn[0] % 4
                                tabn[0] += 1
                                sg = stg[:, sl % 2, :]
                                ksg = f"stg{sl % 2}"
                                dma('sp', sg, cb_d[idx], w=[ksg])
                                dma('sp', mst[:, sl % 2, :], cm_d[1 if ct == 3 else 0, idx], w=[f"mst{sl % 2}"])
                                tt('pool', ctab[:, sl, :].rearrange("p (h q) -> p h q", h=8), sg.rearrange("p (h q) -> p h q", h=8),
                                   bc_mid(mst[:, sl % 2, :], [128, 8, 128]), ALU.add, [ksg, f"mst{sl % 2}"], [f"ctab{sl}"])
                                ctabs[ct] = (ctab[:, sl, :], [f"ctab{sl}"])
                            else:
                                ctabs[ct] = (CFAR[:], ['CFAR'])
                        norm_to_hT(xt, 128, kx, xho[:], 'xho', M1[:].rearrange("p (c t) -> p c t", c=8), 'M1', tmpo[:], 'tmpo', hTo[:], 'hTo', A1, B1, 'A1', 'B1', 128)
                        for c in range(8):
                            mm(M0[:], hTo[:, c, :], w2p[:, c, 0:512], c == 0, c == 7, ['hTo', 'w2pq'], ['M0'])
                        act(qsq[:], M0[:], AF.Square, ['M0'], ['qsq'])
                        op('dve', lambda e: e.tensor_reduce(out=qss[:, 0:8], in_=qsq[:].rearrange("p (h d) -> p h d", h=8), axis=AX.X, op=ALU.add), ['qsq'], ['qss0'])
                        act(qss[:, 8:16], qss[:, 0:8], AF.Ln, ['qss0'], ['qss1'], scale=1.0 / 64, bias=EPS)
                        act(qss[:, 0:8], qss[:, 8:16], AF.Exp, ['qss1'], ['qss0'], scale=-0.5)
                        tt('dve', qtmp[:].rearrange("p (h d) -> p h d", h=8), M0[:].rearrange("p (h d) -> p h d", h=8),
                           bc_last(qss[:, 0:8], [128, 8, 64]), ALU.mult, ['M0', 'qss0'], ['qtmp'])
                        tt('dve', qnp[:].rearrange("p (r g d) -> p g r d", r=4, g=2), qtmp[:].rearrange("p (g r d) -> p g r d", g=2, r=4),
                           qg[:].unsqueeze(1).unsqueeze(1).to_broadcast([128, 2, 4, 64]), ALU.mult, ['qtmp', 'qg'], ['qnp'])
                        for c in range(8):
                            mm(M0[:, 0:24], hTo[:, c, :], w2p[:, c, 512:536], c == 0, c == 7, ['hTo', 'w2pg', 'qtmp'], ['M0'])
                        act(sig[:], M0[:, 0:24], AF.Exp, ['M0'], ['sig'], scale=-1.0)
                        ts('dve', sig[:], sig[:], 1.0, None, ALU.add, None, ['sig'], ['sig'])
                        op('dve', lambda e: e.reciprocal(sig[:], sig[:]), ['sig'], ['sig'])
                        b = i % 2
                        useB = i >= 16
                        for r in range(4):
                            tp(M1[:, r * 128:(r + 1) * 128], qnp[:, r * 128:(r + 1) * 128], ident_b[:], ['qnp', 'ident', 'tmpo'], ['M1'])
                        QA = [QQ[:, b, 0, :], QQ[:, b, 1, :]]
                        QB = [QQ[:, b, 2, :], QQ[:, b, 3, :]]
                        kQ = [f"QQ{b}{n}" for n in range(4)]
                        cp('act', QA[0][0:64, :], M1[0:64, 0:512], ['M1'], [kQ[0] + 'd'])
                        cp('act', QA[1][64:128, :], M1[64:128, 0:512], ['M1'], [kQ[1] + 'd'])
                        if useB:
                            cp('act', QB[0][0:64, :], M1[0:64, 0:512], ['M1'], [kQ[2] + 'd'])
                            cp('act', QB[1][64:128, :], M1[64:128, 0:512], ['M1'], [kQ[3] + 'd'])
                        o_ = 126 - 4 * i
                        for g in range(2):
                            rows = slice(64 * g, 64 * g + 64)
                            gen = branch(OC, 'OC', list(range(nct)),
                                         lambda ct: (KCT[rows, ct * 128:(ct + 1) * 128], ['KCT']),
                                         lambda ct: (VC[:, ct, g, :], ['VCd', 'VCone']),
                                         lambda ct: (QA[g][rows, :], [kQ[g] + 'd']),
                                         lambda ct: (ctabs[ct][0][:, g * 512:(g + 1) * 512], ctabs[ct][1]))
                            for pt, kpt, n in gen:
                                for r in range(4):
                                    mm(IMP[:, r * 128:(r + 1) * 128], pt[:, r * 128:(r + 1) * 128], OVb[:, n * 128:(n + 1) * 128], n == 0 and r == 0, n == nct - 1, [kpt, 'OVb'], ['IMP'])
                            cp('dve', OCs[:, g, :], OC[:], ['OC'], [f"OCs{g}"])
                            dc = dsm[:, 0:4]
                            cp('dve', dc, OCs[:, g, 64:260:65], [f"OCs{g}"], ['dc'])
                            ts('dve', dc, dc, 1e-30, None, ALU.max, None, ['dc'], ['dc'])
                            op('dve', lambda e: e.reciprocal(dc, dc), ['dc'], ['dc'])
                            ts('dve', imp[:], IMP[:, 0:128], dsm[:, 0:1], None, ALU.mult, None, ['IMP', 'dc'], ['imp'])
                            for r in range(1, 4):
                                stt('dve', imp[:], IMP[:, r * 128:(r + 1) * 128], dsm[:, r:r + 1], imp[:], ALU.mult, ALU.add, ['IMP', 'dc', 'imp'], ['imp'])
                            tt('dve', scr[:], imp[:], keep[:, o_:o_ + 128], ALU.mult, ['imp', 'keep'], ['scr'])
                            tt('dve', scr[:], scr[:], addc[:, o_:o_ + 128], ALU.add, ['scr', 'addc'], ['scr'])
                            ms_('dve', scr[:, 0:1], 1.0e6, ['scr'])
                            op('dve', lambda e: e.max(out=m8[:, 0:8], in_=scr[:]), ['scr'], ['m8a'])
                            op('dve', lambda e: e.match_replace(out=wk[:], in_to_replace=m8[:, 0:8], in_values=scr[:], imm_value=-1.0e30), ['scr', 'm8a'], ['wk'])
                            op('dve', lambda e: e.max(out=m8[:, 8:16], in_=wk[:]), ['wk'], ['m8b'])
                            cA = slice(64, 128) if g == 0 else slice(0, 64)
                            ts('dve', NS[:, 0, cA], scr[:, 0:64], m8[:, 15:16], NEGM, ALU.is_lt, ALU.mult, ['scr', 'm8b'], [f"NS0{g}"])
                            ts('dve', NS[:, 1, cA], scr[:, 64:128], m8[:, 15:16], NEGM, ALU.is_lt, ALU.mult, ['scr', 'm8b'], [f"NS1{g}"])
                        for hb in range(2 if useB else 1):
                            tp(M1[:, 512 + hb * 128:512 + (hb + 1) * 128], NS[:, hb, :], ident_b[:], [f"NS{hb}0", f"NS{hb}1", 'ident'], ['M1'])
                            src = M1[:, 512 + hb * 128:512 + (hb + 1) * 128]
                            Qh = QA if hb == 0 else QB
                            tt('dve', Qh[0][64:128, :].rearrange("p (r q) -> p r q", r=4), bc_mid(src[64:128, :], [64, 4, 128]),
                               bc_last(b31t[64:128, 0:4], [64, 4, 128]), ALU.add, ['M1', 'b31t'], [kQ[2 * hb] + 'n'])
                            tt('dve', Qh[1][0:64, :].rearrange("p (r q) -> p r q", r=4), bc_mid(src[0:64, :], [64, 4, 128]),
                               bc_last(b31t[0:64, 4:8], [64, 4, 128]), ALU.add, ['M1', 'b31t'], [kQ[2 * hb + 1] + 'n'])
                        nk = 2 * i + 2
                        for g in range(2):
                            rows = slice(64 * g, 64 * g + 64)
                            for _ in branch(OS, 'OS', list(range(nk)),
                                            lambda kt: (Kaug[g][:, kt * 128:(kt + 1) * 128], [f"Kaug{g}d", f"Kaug{g}e"]),
                                            lambda kt: (VS[:, kt, g, :], ['VSd', 'VSone']),
                                            lambda kt: ((QA if kt < 32 else QB)[g][:, :], [kQ[(0 if kt < 32 else 2) + g] + 'd', kQ[(0 if kt < 32 else 2) + g] + 'n']),
                                            lambda kt: ((SELT[:, nk - 1 - kt, g * 512:(g + 1) * 512], ['SELT']) if nk - 1 - kt <= 8 else None)):
                                pass
                            wk_ts = list(range(max(0, 2 * i - 4), nk))
                            for _ in branch(OW, 'OW', wk_ts,
                                            lambda kt: (KWr[rows, kt % 8, :], [f"KWr{kt % 8}"]),
                                            lambda kt: (VWr[:, kt % 8, g * 65:(g + 1) * 65], [f"VWr{kt % 8}"]),
                                            lambda kt: (QA[g][rows, :], [kQ[g] + 'd']),
                                            lambda kt: (WINT[:, nk - 1 - kt, g * 512:(g + 1) * 512], ['WINT'])):
                                pass
                            den = dsm[:, 8:20].rearrange("p (b r) -> p b r", b=3)
                            cp('dve', den[:, 0, :], OCs[:, g, 64:260:65], [f"OCs{g}"], ['den'])
                            cp('dve', den[:, 1, :], OS[:, 64:260:65], ['OS'], ['den'])
                            cp('dve', den[:, 2, :], OW[:, 64:260:65], ['OW'], ['den'])
                            ts('dve', dsm[:, 8:20], dsm[:, 8:20], 1e-30, None, ALU.max, None, ['den'], ['den'])
                            op('dve', lambda e: e.reciprocal(dsm[:, 8:20], dsm[:, 8:20]), ['den'], ['den'])
                            tt('dve', den, den, sig[:, 12 * g:12 * g + 12].rearrange("p (r b) -> p b r", b=3), ALU.mult, ['den', 'sig'], ['den'])
                            a3 = acc[:].rearrange("p (r d) -> p r d", r=4)
                            a23 = acc2[:].rearrange("p (r d) -> p r d", r=4)
                            srcs = [(OCs[:, g, :], f"OCs{g}"), (OS[:], 'OS'), (OW[:], 'OW')]
                            for br, (Oap, kOb) in enumerate(srcs):
                                s3 = Oap.rearrange("p (r e) -> p r e", e=65)[:, :, 0:64]
                                cbr = bc_last(den[:, br, :], [128, 4, 64])
                                if br == 0:
                                    tt('dve', a3, s3, cbr, ALU.mult, [kOb, 'den'], ['acc'])
                                else:
                                    tt('dve', a23, s3, cbr, ALU.mult, [kOb, 'den'], ['acc2'])
                                    dst = acc[:] if br == 1 else attn_bf[:, b, g * 256:(g + 1) * 256]
                                    tt('pool', dst, acc[:], acc2[:], ALU.add, ['acc', 'acc2'], ['acc'] if br == 1 else [f"attn{b}"])
                        dma('sp', attn_s[i * 128:(i + 1) * 128, :], attn_bf[:, b, :], r=[f"attn{b}"], w=['attn_s'])
                    S.barrier()

            chk('p2')
            with ExitStack() as es6:
                w3c = es6.enter_context(nc.sbuf_tensor("s_w3c", [128, 8, 1536], BF16))
                wo = es6.enter_context(nc.sbuf_tensor("s_wo", [128, 8, 1024], BF16))
                convc = es6.enter_context(nc.sbuf_tensor("s_convc", [128, 12], F32))
                hm = es6.enter_context(nc.sbuf_tensor("s_hm", [128, 64], F32))
                xk = es6.enter_context(nc.sbuf_tensor("s_xk", [128, 4, 1024], F32))
                xhl = es6.enter_context(nc.sbuf_tensor("s_xhl", [8, 1024], F32))
                xh3 = es6.enter_context(nc.sbuf_tensor("s_xh3", [128, 1024], BF16))
                tmp3 = es6.enter_context(nc.sbuf_tensor("s_tmp3", [128, 8, 128], F32))
                hT3 = es6.enter_context(nc.sbuf_tensor("s_hT3", [128, 8, 512], BF16))
                hTh = es6.enter_context(nc.sbuf_tensor("s_hTh", [128, 8, 8], BF16))
                zt = es6.enter_context(nc.sbuf_tensor("s_zt", [128, 4, 130], F32))
                cgs = es6.enter_context(nc.sbuf_tensor("s_cgs", [128, 512], F32))
                cacc = es6.enter_context(nc.sbuf_tensor("s_cacc", [128, 512], F32))
                zh8 = es6.enter_context(nc.sbuf_tensor("s_zh8", [128, 16], F32))
                at4 = es6.enter_context(nc.sbuf_tensor("s_at4", [128, 4, 512], BF16))
                catT = es6.enter_context(nc.sbuf_tensor("s_catT", [128, 8, 512], BF16))
                mixt = es6.enter_context(nc.sbuf_tensor("s_mixt", [128, 512], F32))
                h2T = es6.enter_context(nc.sbuf_tensor("s_h2T", [128, 8, 512], BF16))
                hidT = es6.enter_context(nc.sbuf_tensor("s_hidT", [128, 32, 512], BF16))
                relu = es6.enter_context(nc.sbuf_tensor("s_relu", [128, 2, 512], F32))
                wr1 = es6.enter_context(nc.sbuf_tensor("s_wr1", [128, 3, 8, 512], BF16))
                wr2 = es6.enter_context(nc.sbuf_tensor("s_wr2", [128, 3, 4, 512], BF16))
                ACC0 = es6.enter_context(nc.psum_tensor("p_ACC0", [128, 512], F32))
                ACC1 = es6.enter_context(nc.psum_tensor("p_ACC1", [128, 512], F32))
                ACC2 = es6.enter_context(nc.psum_tensor("p_ACC2", [128, 512], F32))
                ACC3 = es6.enter_context(nc.psum_tensor("p_ACC3", [128, 512], F32))
                PA = es6.enter_context(nc.psum_tensor("p_PA", [128, 512], F32))
                PB = es6.enter_context(nc.psum_tensor("p_PB", [128, 512], F32))
                PC = es6.enter_context(nc.psum_tensor("p_PC", [128, 512], F32))
                PTR = es6.enter_context(nc.psum_tensor("p_PTR", [128, 8, 128], BF16))
                ACC = [ACC0, ACC1, ACC2, ACC3]
                P3 = [PA, PB, PC]
                p3i = [0]

                def nextp():
                    k = p3i[0] % 3
                    p3i[0] += 1
                    return P3[k], f"P3{k}"

                dma('pool', w3c[:], w_in_r[:, :, 1304:2840], w=['w3c'])
                dma('pool', wo[:], w_out.rearrange("(c p) n -> p c n", p=128), w=['wo'])
                dma('sp', convc[:], convc_d, w=['convc'])
                dma('sp', hm[:], hmask, w=['hm'])
                w1n = [0]
                w2n = [0]
                for s in range(8):
                    for j in range(4):
                        kx = f"xk{j}"
                        dma('sp', xk[:, j, :], xown[(4 * s + j) * 128:(4 * s + j + 1) * 128, :], w=[kx])
                    dma('sp', xhl[:], xhalo[s], w=['xhl'])
                    dma('sp', at4[:], attn_s[s * 512:(s + 1) * 512, :].rearrange("(j t) f -> t j f", t=128), r=['attn_s'], w=['at4'])
                    for j in range(4):
                        norm_to_hT(xk[:, j, :], 128, f"xk{j}", xh3[:], 'xh3', PTR, 'PTR', tmp3[:], 'tmp3', hT3[:, :, j * 128:(j + 1) * 128], 'hT3', A1, B1, 'A1', 'B1', 128)
                    norm_to_hT(xhl[:], 8, 'xhl', xh3[0:8, :], 'xh3', PTR, 'PTR', tmp3, 'tmp3', hTh[:], 'hTh', A1, B1, 'A1', 'B1', 8)
                    for j in range(4):
                        for fc in range(4):
                            tp(PTR[:, fc, :], at4[:, j, fc * 128:(fc + 1) * 128], ident_b[:], ['at4', 'ident', 'tmp3'], ['PTR'])
                        cp('act', catT[:, 0:4, j * 128:(j + 1) * 128], PTR[:, 0:4, :], ['PTR'], ['catTa'])
                    for m in range(4):
                        pcg, kcg = nextp()
                        pu, ku = nextp()
                        pbg, kbg = nextp()
                        for c in range(8):
                            mm(pcg[:], w3c[:, c, m * 128:(m + 1) * 128], hT3[:, c, :], c == 0, c == 7, ['w3c', 'hT3'], [kcg])
                        for c in range(8):
                            mm(pu[:], w3c[:, c, 1024 + m * 128:1024 + (m + 1) * 128], hT3[:, c, :], c == 0, c == 7, ['w3c', 'hT3'], [ku])
                        for c in range(8):
                            mm(pbg[:], w3c[:, c, 512 + m * 128:512 + (m + 1) * 128], hT3[:, c, :], c == 0, c == 7, ['w3c', 'hT3'], [kbg])
                        cp('act', cgs[:], pcg[:], [kcg], ['cgs'])
                        tt('dve', zt[:, :, 2:130], cgs[:].rearrange("p (j t) -> p j t", j=4), pu[:].rearrange("p (j t) -> p j t", j=4), ALU.mult, ['cgs', ku], ['ztm'])
                        ph, kph = nextp()
                        for c in range(8):
                            mm(ph[:, 0:8], w3c[:, c, m * 128:(m + 1) * 128], hTh[:, c, :], c == 0, c == 7, ['w3c', 'hTh'], [kph])
                        for c in range(8):
                            mm(ph[:, 8:16], w3c[:, c, 1024 + m * 128:1024 + (m + 1) * 128], hTh[:, c, :], c == 0, c == 7, ['w3c', 'hTh'], [kph])
                        cp('act', zh8[:], ph[:, 0:16], [kph], ['zh8'])
                        tt('dve', zh8[:, 0:8], zh8[:, 0:8], zh8[:, 8:16], ALU.mult, ['zh8'], ['zh8'])
                        tt('dve', zt[:, :, 0:2], zh8[:, 0:8].rearrange("p (j e) -> p j e", j=4), hm[:, s * 8:(s + 1) * 8].rearrange("p (j e) -> p j e", j=4),
                           ALU.mult, ['zh8', 'hm'], ['zth'])
                        c3 = cacc[:].rearrange("p (j t) -> p j t", j=4)
                        ts('dve', c3, zt[:, :, 0:128], convc[:, m * 3:m * 3 + 1], None, ALU.mult, None, ['ztm', 'zth', 'convc'], ['cacc'])
                        stt('dve', c3, zt[:, :, 1:129], convc[:, m * 3 + 1:m * 3 + 2], c3, ALU.mult, ALU.add, ['ztm', 'zth', 'convc', 'cacc'], ['cacc'])
                        stt('dve', c3, zt[:, :, 2:130], convc[:, m * 3 + 2:m * 3 + 3], c3, ALU.mult, ALU.add, ['ztm', 'zth', 'convc', 'cacc'], ['cacc'])
                        tt('dve', catT[:, 4 + m, :], cacc[:], pbg[:], ALU.mult, ['cacc', kbg], ['catTc'])
                    if dbg and s == 0:
                        dma('sp', dbg_cat, catT[:].rearrange("p c t -> p (c t)"), r=['catTa', 'catTc'], w=['dbg_cat'])
                    for j in range(4):
                        for half in range(2):
                            pw, kpw = nextp()
                            for c in range(8):
                                mm(pw[:], catT[:, c, j * 128:(j + 1) * 128], wo[:, c, half * 512:(half + 1) * 512], c == 0, c == 7, ['catTa', 'catTc', 'wo'], [kpw])
                            tt('dve', mixt[:], pw[:], g1bc[:, half * 512:(half + 1) * 512], ALU.mult, [kpw, f"gbc2{half}"], ['mixt'])
                            tt('pool', xk[:, j, half * 512:(half + 1) * 512], xk[:, j, half * 512:(half + 1) * 512], mixt[:], ALU.add, [f"xk{j}", 'mixt'], [f"xk{j}"])
                        norm_to_hT(xk[:, j, :], 128, f"xk{j}", xh3[:], 'xh3', PTR, 'PTR', tmp3[:], 'tmp3', h2T[:, :, j * 128:(j + 1) * 128], 'h2T', A2, B2, 'A2', 'B2', 128)
                    if dbg and s == 0:
                        dma('sp', dbg_x1, xk[:].rearrange("p j f -> p (j f)"), r=[f"xk{j}" for j in range(4)], w=['dbg_x1'])
                        dma('sp', dbg_h2, h2T[:].rearrange("p c t -> p (c t)"), r=['h2T'], w=['dbg_h2'])
                    for p in range(8):
                        wb = w1n[0] % 3
                        w1n[0] += 1
                        dma('sp', wr1[:, wb], wff1_s[:, :, p * 512:(p + 1) * 512], r=['wff1_s'], w=[f"wr1{wb}"])
                        for hc in range(4):
                            pf, kpf = nextp()
                            for c in range(8):
                                mm(pf[:], wr1[:, wb, c, hc * 128:(hc + 1) * 128], h2T[:, c, :], c == 0, c == 7, [f"wr1{wb}", 'h2T'], [kpf])
                            rb = (4 * p + hc) % 2
                            act(relu[:, rb, :], pf[:], AF.Relu, [kpf], [f"relu{rb}"])
                            tt('dve', hidT[:, 4 * p + hc, :], relu[:, rb, :], relu[:, rb, :], ALU.mult, [f"relu{rb}"], ['hidT'])
                    if dbg and s == 0:
                        dma('sp', dbg_hid, hidT[:, 0:4, :].rearrange("p c t -> p (c t)"), r=['hidT'], w=['dbg_hid'])
                    for half in range(2):
                        for p in range(8):
                            wb = w2n[0] % 3
                            w2n[0] += 1
                            dma('sp', wr2[:, wb], wff2_s[:, 4 * p:4 * p + 4, half * 512:(half + 1) * 512], r=['wff2_s'], w=[f"wr2{wb}"])
                            for j in range(4):
                                for hc in range(4):
                                    mm(ACC[j][:], hidT[:, 4 * p + hc, j * 128:(j + 1) * 128], wr2[:, wb, hc, :], p == 0 and hc == 0, p == 7 and hc == 3,
                                       ['hidT', f"wr2{wb}"], [f"ACC{j}"])
                        for j in range(4):
                            tt('dve', mixt[:], ACC[j][:], g2bc[:, half * 512:(half + 1) * 512], ALU.mult, [f"ACC{j}", f"gbc5{half}"], ['mixt'])
                            tt('pool', xk[:, j, half * 512:(half + 1) * 512], xk[:, j, half * 512:(half + 1) * 512], mixt[:], ALU.add, [f"xk{j}", 'mixt'], [f"xk{j}"])
                    for j in range(4):
                        dma('sp', out_own[(4 * s + j) * 128:(4 * s + j + 1) * 128, :], xk[:, j, :], r=[f"xk{j}"], w=[f"out{j}"])

    except _Stop:
        pass
    final()
    return nc


def _bucket(dist):
    n = np.maximum(dist, 0)
    nf = np.maximum(n, 16).astype(np.float32)
    large = 16 + (np.log(nf / np.float32(16)) / np.float32(np.log(64.0)) * np.float32(16)).astype(np.int32)
    large = np.minimum(large, 31)
    return np.where(n < 16, n, large).astype(np.int64)


def _tables(rel_bias, par):
    k = np.arange(128)[:, None]
    q = np.arange(128)[None, :]
    bias9 = np.empty((9, 128, 8, 128), np.float32)
    ms = np.empty((9, 128, 128), np.float32)
    mw = np.empty((6, 128, 128), np.float32)
    for dp in range(9):
        dist = 128 * (dp - 1 + par) + q - k
        bias9[dp] = rel_bias[_bucket(dist)].transpose(0, 2, 1)
        ms[dp] = np.where(dist >= 0, 0.0, NEGM)
        if dp < 6:
            mw[dp] = np.where((dist >= 0) & (dist < 512), 0.0, NEGM)
    cb = np.empty((12, 128, 8, 128), np.float32)
    cm = np.empty((2, 12, 128, 128), np.float32)
    for idx in range(12):
        dpp = 2 * idx + 1
        dist = 128 * (dpp - 1 + par) + q - 16 * k - 31
        cb[idx] = rel_bias[_bucket(dist)].transpose(0, 2, 1)
        cm[0, idx] = np.where(dist >= 0, 0.0, NEGM)
        cm[1, idx] = cm[0, idx]
        cm[1, idx, 127, :] = NEGM
    qq = np.arange(128)[:, None]
    u = np.arange(254)[None, :] - 126 - 2 * par
    cq = (qq >= 64).astype(np.int64)
    keep = (u < cq - 1).astype(np.float32)
    addc = np.where((u >= cq - 1) & (u <= cq), 1.0e6, np.where(u > cq, -1.0, 0.0)).astype(np.float32)
    return (bias9.reshape(9, 128, 1024), ms, mw, cb.reshape(12, 128, 1024), cm, keep, addc)


_PROG = {}


def _prep(x, c, w_in, q_norm, k_norm, cmp_pe_k, cmp_w1_k, cmp_w2_k, cmp_pe_v, cmp_w1_v, cmp_w2_v,
          rel_bias, conv_w, w_out, norm1, norm2, w_ada, b_ada, w_ff1, w_ff2):
    f = lambda a: np.ascontiguousarray(np.asarray(a, dtype=np.float32))
    x = f(x)
    c = f(c)
    rel_bias = f(rel_bias)
    kg = np.arange(8192)
    erows = f(((kg[None, :] // 64) % 64 == np.arange(64)[:, None]))
    n_ = np.arange(512)[:, None] * 16
    j_ = np.arange(128)[None, :] * 64
    ov = np.clip(np.minimum(n_ + 32, j_ + 64) - np.maximum(n_, j_), 0, None) / 32.0
    ov[511] = 0.0
    ov = f(ov.reshape(4, 128, 128).transpose(1, 0, 2).reshape(128, 512))
    ident = f(np.eye(128))
    bd = f(np.kron(np.eye(2), np.ones((64, 64))))
    col8 = lambda v: f(np.asarray(v).reshape(8, 128).T)
    shared = {
        "n1c": col8(norm1[0]), "n2c": col8(norm2[0]),
        "badaC": f(np.asarray(b_ada[0]).reshape(48, 128).T), "badaR": f(np.asarray(b_ada[0]).reshape(1, 6144)),
        "w_ada": f(w_ada[0]), "w_in": f(w_in[0]),
        "qg": f(np.broadcast_to(np.asarray(q_norm[0])[None, :], (128, 64))),
        "knc": f(np.tile(np.asarray(k_norm[0]), 2).reshape(128, 1)),
        "peTk": f(np.tile(np.asarray(cmp_pe_k[0]).T, (2, 1))), "peTv": f(np.tile(np.asarray(cmp_pe_v[0]).T, (2, 1))),
        "w1k": f(cmp_w1_k[0]), "w1v": f(cmp_w1_v[0]), "w2k": f(cmp_w2_k[0]), "w2v": f(cmp_w2_v[0]),
        "convc": f(np.asarray(conv_w[0]).reshape(3, 4, 128).transpose(2, 1, 0).reshape(128, 12)),
        "w_out": f(w_out[0]), "w_ff1": f(w_ff1[0]), "w_ff2": f(w_ff2[0]),
        "b31": f(np.broadcast_to(rel_bias[31][None, :], (128, 8))),
        "erows": erows, "ov": ov, "ident": ident, "bd": bd,
    }
    tabs = [_tables(rel_bias, par) for par in range(2)]
    in_maps = []
    own_idx = []
    for core in range(8):
        b, par = core // 2, core % 2
        tiles = 2 * np.arange(NT) + par
        rows = (tiles[:, None] * 128 + np.arange(128)[None, :]).reshape(-1)
        own_idx.append((b, rows))
        xb = x[b]
        halo = np.zeros((NT, 2, 1024), np.float32)
        hmk = np.ones((NT, 2), np.float32)
        for i, T in enumerate(tiles):
            if T == 0:
                hmk[i] = 0.0
            else:
                halo[i] = xb[T * 128 - 2:T * 128]
        bias9, ms, mw, cb, cm, keep, addc = tabs[par]
        m = dict(shared)
        m.update({
            "xfull": xb, "xown": f(xb[rows]), "xhalo": f(halo.reshape(8, 8, 1024)),
            "hmask": f(np.broadcast_to(hmk.reshape(1, 64), (128, 64))),
            "cT": col8(c[b]),
            "bias9": bias9, "ms": ms, "mw": mw, "cb": cb, "cm": cm, "keep": keep, "addc": addc,
        })
        in_maps.append(m)
    return in_maps, own_idx


def kernel(**inputs):
    in_maps, own_idx = _prep(**inputs)
    if 'nc' not in _PROG:
        _PROG['nc'] = build_program()
    nc = _PROG['nc']
    res = run_bass_kernel_spmd(nc, in_maps, core_ids=list(range(8)))
    out = np.empty((4, 8192, 1024), np.float32)
    for core in range(8):
        b, rows = own_idx[core]
        out[b, rows] = res.results[core]["out_own"]
    return out
```

```python
import numpy as np
from contextlib import ExitStack
import concourse.bass as bass
import concourse.mybir as mybir
from concourse.bass_utils import run_bass_kernel_spmd

F32, BF16 = mybir.dt.float32, mybir.dt.bfloat16
AF = mybir.ActivationFunctionType
ALU = mybir.AluOpType
AX = mybir.AxisListType
NEGM = -30000.0
EPS = 1e-6
NT = 32
NSTEP1 = 16


class Sched:
    EPOCH = 1000
    KD = 12

    def __init__(self, nc):
        self.nc = nc
        self.eng = {'pe': nc.tensor, 'act': nc.scalar, 'dve': nc.vector, 'pool': nc.gpsimd, 'sp': nc.sync}
        self.cnt = {e: 0 for e in self.eng}
        self.sems = {e: [] for e in self.eng}
        self.seen = {e: {} for e in self.eng}
        self.dn = {e: 0 for e in self.eng}
        self.dsems = {e: [nc.alloc_semaphore(f"d_{e}_{i}") for i in range(self.KD)] for e in ('sp', 'pool')}
        self.lastw = {}
        self.readers = {}

    def _sem(self, e, idx):
        ep = idx // self.EPOCH
        while len(self.sems[e]) <= ep:
            self.sems[e].append(self.nc.alloc_semaphore(f"c_{e}_{len(self.sems[e])}"))
        return self.sems[e][ep], idx % self.EPOCH + 1

    def _wait(self, e, tok):
        kind, f, idx = tok
        if kind == 'c':
            if f == e and (e == 'pe' or self.cnt[e] - idx > 3):
                return
            key = ('c', f, idx // self.EPOCH)
            sem, val = self._sem(f, idx)
        else:
            key = ('d', f, idx % self.KD)
            sem, val = self.dsems[f][idx % self.KD], 16 * (idx // self.KD + 1)
        if self.seen[e].get(key, 0) >= val:
            return
        self.seen[e][key] = val
        self.eng[e].wait_ge(sem, val)

    def _deps(self, e, reads, writes):
        toks = []
        for b in reads:
            t = self.lastw.get(b)
            if t is not None:
                toks.append(t)
        for b in writes:
            t = self.lastw.get(b)
            if t is not None:
                toks.append(t)
            toks.extend(self.readers.get(b, ()))
        for t in dict.fromkeys(toks):
            self._wait(e, t)

    def _commit(self, tok, reads, writes):
        for b in reads:
            lst = self.readers.setdefault(b, [])
            lst.append(tok)
            if len(lst) > 24:
                del lst[0:len(lst) - 24]
        for b in writes:
            self.lastw[b] = tok
            self.readers[b] = []

    def op(self, e, fn, r=(), w=()):
        self._deps(e, r, w)
        ins = fn(self.eng[e])
        idx = self.cnt[e]
        sem, _ = self._sem(e, idx)
        ins.then_inc(sem, 1)
        self.cnt[e] = idx + 1
        self._commit(('c', e, idx), r, w)

    def dma(self, q, out, in_, r=(), w=()):
        n = self.dn[q]
        if n >= self.KD:
            self._wait(q, ('d', q, n - self.KD))
        self._deps(q, r, w)
        ins = self.eng[q].dma_start(out=out, in_=in_)
        ins.then_inc(self.dsems[q][n % self.KD], 16)
        self.dn[q] = n + 1
        self._commit(('d', q, n), r, w)

    def barrier(self):
        for e in self.eng:
            for f in self.eng:
                if f != e and self.cnt[f] > 0:
                    self._wait(e, ('c', f, self.cnt[f] - 1))
            for q in self.dsems:
                n = self.dn[q]
                for k in range(max(0, n - self.KD), n):
                    self._wait(e, ('d', q, k))

    def finish(self, e, bufs):
        for b in bufs:
            t = self.lastw.get(b)
            if t is not None:
                self._wait(e, t)


def bc_last(ap, shape):
    return ap.unsqueeze(len(shape) - 1).to_broadcast(list(shape))


def bc_mid(ap, shape):
    return ap.unsqueeze(1).to_broadcast(list(shape))


class _Stop(Exception):
    pass


def build_program(stop=None, dbg=False):
    nc = bass.Bass("TRN2", target_bir_lowering=False)

    def din(name, shape):
        return nc.dram_tensor(name, list(shape), F32, kind="ExternalInput").ap()

    xfull = din("xfull", [8192, 1024])
    xown = din("xown", [4096, 1024])
    xhalo = din("xhalo", [8, 8, 1024])
    hmask = din("hmask", [128, 64])
    cT_d = din("cT", [128, 8])
    n1c_d = din("n1c", [128, 8])
    n2c_d = din("n2c", [128, 8])
    badaC_d = din("badaC", [128, 48])
    badaR_d = din("badaR", [1, 6144])
    w_ada = din("w_ada", [1024, 6144])
    w_in = din("w_in", [1024, 2840])
    qg_d = din("qg", [128, 64])
    knc_d = din("knc", [128, 1])
    peTk_d = din("peTk", [128, 32])
    peTv_d = din("peTv", [128, 32])
    w1k_d = din("w1k", [2048, 256])
    w1v_d = din("w1v", [2048, 256])
    w2k_d = din("w2k", [256, 64])
    w2v_d = din("w2v", [256, 64])
    convc_d = din("convc", [128, 12])
    w_out = din("w_out", [1024, 1024])
    w_ff1 = din("w_ff1", [1024, 4096])
    w_ff2 = din("w_ff2", [4096, 1024])
    bias9_d = din("bias9", [9, 128, 1024])
    ms_d = din("ms", [9, 128, 128])
    mw_d = din("mw", [6, 128, 128])
    cb_d = din("cb", [12, 128, 1024])
    cm_d = din("cm", [2, 12, 128, 128])
    b31_d = din("b31", [128, 8])
    keep_d = din("keep", [128, 254])
    addc_d = din("addc", [128, 254])
    erows_d = din("erows", [64, 8192])
    ov_d = din("ov", [128, 512])
    ident_d = din("ident", [128, 128])
    bd_d = din("bd", [128, 128])
    out_own = nc.dram_tensor("out_own", [4096, 1024], F32, kind="ExternalOutput").ap()
    kw_s = nc.dram_tensor("kw_s", [128, 8192], BF16).ap()
    vw_s = nc.dram_tensor("vw_s", [64, 128, 130], BF16).ap()
    attn_s = nc.dram_tensor("attn_s", [4096, 512], BF16, **({'kind': 'ExternalOutput'} if dbg else {})).ap()
    if dbg:
        dbg_k0 = nc.dram_tensor("dbg_k0", [128, 1024], BF16, kind="ExternalOutput").ap()
        dbg_k1 = nc.dram_tensor("dbg_k1", [128, 1024], BF16, kind="ExternalOutput").ap()
        dbg_vs = nc.dram_tensor("dbg_vs", [128, 1040], BF16, kind="ExternalOutput").ap()
        dbg_kct = nc.dram_tensor("dbg_kct", [128, 512], BF16, kind="ExternalOutput").ap()
        dbg_vc = nc.dram_tensor("dbg_vc", [128, 520], BF16, kind="ExternalOutput").ap()
        dbg_g1 = nc.dram_tensor("dbg_g1", [128, 1024], F32, kind="ExternalOutput").ap()
        dbg_ab = nc.dram_tensor("dbg_ab", [128, 32], F32, kind="ExternalOutput").ap()
        dbg_cat = nc.dram_tensor("dbg_cat", [128, 4096], BF16, kind="ExternalOutput").ap()
        dbg_x1 = nc.dram_tensor("dbg_x1", [128, 4096], F32, kind="ExternalOutput").ap()
        dbg_h2 = nc.dram_tensor("dbg_h2", [128, 4096], BF16, kind="ExternalOutput").ap()
        dbg_hid = nc.dram_tensor("dbg_hid", [128, 2048], BF16, kind="ExternalOutput").ap()
    wff1_s = nc.dram_tensor("wff1_s", [128, 8, 4096], BF16).ap()
    wff2_s = nc.dram_tensor("wff2_s", [128, 32, 1024], BF16).ap()

    S = Sched(nc)
    op, dma = S.op, S.dma

    def chk(label):
        if stop == label:
            raise _Stop()

    def final():
        for q in ('sp', 'pool'):
            n = S.dn[q]
            for k in range(max(0, n - S.KD), n):
                S._wait('sp', ('d', q, k))

    def bk(*aps):
        ks = []
        for a in aps:
            t = getattr(a, 'tensor', a)
            nm = getattr(t, 'name', None)
            if isinstance(nm, str) and nm.startswith('p_'):
                ks.append('BANK_' + nm)
        return ks

    def mm(out, lhsT, rhs, start, stop, r, w):
        op('pe', lambda e: e.matmul(out, lhsT, rhs, start=start, stop=stop), r, list(w) + bk(out))

    def tp(out, in_, ident, r, w):
        op('pe', lambda e: e.transpose(out, in_, ident), r, list(w) + bk(out))

    def act(out, in_, func, r, w, scale=1.0, bias=None, accum=None):
        kw = {}
        if bias is not None:
            kw['bias'] = bias
        if accum is not None:
            kw['accum_out'] = accum
        op('act', lambda e: e.activation(out=out, in_=in_, func=func, scale=scale, **kw), r, list(w) + bk(out, in_))

    def tt(eng, out, in0, in1, o, r, w):
        op(eng, lambda e: e.tensor_tensor(out=out, in0=in0, in1=in1, op=o), r, list(w) + bk(out, in0, in1))

    def ts(eng, out, in0, s1, s2, o0, o1, r, w):
        if o1 is None:
            op(eng, lambda e: e.tensor_scalar(out, in0, s1, None, op0=o0), r, list(w) + bk(out, in0))
        else:
            op(eng, lambda e: e.tensor_scalar(out, in0, s1, s2, op0=o0, op1=o1), r, list(w) + bk(out, in0))

    def stt(eng, out, in0, sc, in1, o0, o1, r, w):
        op(eng, lambda e: e.scalar_tensor_tensor(out=out, in0=in0, scalar=sc, in1=in1, op0=o0, op1=o1), r, list(w) + bk(out, in0, in1))

    def cp(eng, out, in_, r, w):
        if eng == 'act':
            op('act', lambda e: e.copy(out, in_), r, list(w) + bk(out, in_))
        else:
            op(eng, lambda e: e.tensor_copy(out, in_), r, list(w) + bk(out, in_))

    def ms_(eng, ap, val, w):
        op(eng, lambda e: e.memset(ap, val), (), w)

    try:
        w_in_r = w_in.rearrange("(c p) n -> p c n", p=128)

        with ExitStack() as es1:
            ident_b = es1.enter_context(nc.sbuf_tensor("s_ident_b", [128, 128], BF16))
            bd_b = es1.enter_context(nc.sbuf_tensor("s_bd_b", [128, 128], BF16))
            A1 = es1.enter_context(nc.sbuf_tensor("s_A1", [128, 8], F32))
            B1 = es1.enter_context(nc.sbuf_tensor("s_B1", [128, 8], F32))
            A2 = es1.enter_context(nc.sbuf_tensor("s_A2", [128, 8], F32))
            B2 = es1.enter_context(nc.sbuf_tensor("s_B2", [128, 8], F32))
            g1bc = es1.enter_context(nc.sbuf_tensor("s_g1bc", [128, 1024], F32))
            g2bc = es1.enter_context(nc.sbuf_tensor("s_g2bc", [128, 1024], F32))
            knc = es1.enter_context(nc.sbuf_tensor("s_knc", [128, 1], F32))
            b31t = es1.enter_context(nc.sbuf_tensor("s_b31t", [128, 8], F32))
            ss = es1.enter_context(nc.sbuf_tensor("s_ss", [128, 8], F32))
            junk = es1.enter_context(nc.sbuf_tensor("s_junk", [128, 1024], BF16))
            ss_i = [0]

            def rms_rstd(xt, P, kx):
                k = ss_i[0] % 4
                ss_i[0] += 1
                a, b_ = ss[0:P, 2 * k:2 * k + 1], ss[0:P, 2 * k + 1:2 * k + 2]
                ka, kb = f"ss{2 * k}", f"ss{2 * k + 1}"
                act(junk[0:P, :], xt, AF.Square, [kx], ['junk', ka], accum=a)
                act(b_, a, AF.Ln, [ka], [kb], scale=1.0 / 1024, bias=EPS)
                act(a, b_, AF.Exp, [kb], [ka], scale=-0.5)
                return a, ka

            dma('pool', ident_b[:], ident_d, w=['ident'])
            dma('pool', bd_b[:], bd_d, w=['bd'])
            dma('sp', knc[:], knc_d, w=['knc'])
            dma('sp', b31t[:], b31_d, w=['b31t'])
            with ExitStack() as es2:
                cT = es2.enter_context(nc.sbuf_tensor("s_cT", [128, 8], F32))
                n1c = es2.enter_context(nc.sbuf_tensor("s_n1c", [128, 8], F32))
                n2c = es2.enter_context(nc.sbuf_tensor("s_n2c", [128, 8], F32))
                badaC = es2.enter_context(nc.sbuf_tensor("s_badaC", [128, 48], F32))
                badaR = es2.enter_context(nc.sbuf_tensor("s_badaR", [1, 6144], BF16))
                e1 = es2.enter_context(nc.sbuf_tensor("s_e1", [128, 8], F32))
                scT = es2.enter_context(nc.sbuf_tensor("s_scT", [128, 8], BF16))
                screp = es2.enter_context(nc.sbuf_tensor("s_screp", [128, 8, 128], BF16))
                ones1 = es2.enter_context(nc.sbuf_tensor("s_ones1", [1, 128], BF16))
                modc = es2.enter_context(nc.sbuf_tensor("s_modc", [128, 6, 8], F32))
                wst0 = es2.enter_context(nc.sbuf_tensor("s_wst0", [128, 8, 512], BF16))
                wst1 = es2.enter_context(nc.sbuf_tensor("s_wst1", [128, 8, 512], BF16))
                pm0 = es2.enter_context(nc.psum_tensor("p_pm0", [128, 512], F32))
                pm1 = es2.enter_context(nc.psum_tensor("p_pm1", [128, 512], F32))
                pcol = es2.enter_context(nc.psum_tensor("p_pcol", [128, 8], F32))
                wst = [wst0, wst1]
                pm = [pm0, pm1]
                dma('sp', cT[:], cT_d, w=['cT'])
                dma('sp', n1c[:], n1c_d, w=['n1c'])
                dma('sp', n2c[:], n2c_d, w=['n2c'])
                dma('sp', badaC[:], badaC_d, w=['badaC'])
                dma('pool', badaR[:], badaR_d, w=['badaR'])
                act(e1[:], cT[:], AF.Exp, ['cT'], ['e1'], scale=-1.0)
                ts('dve', e1[:], e1[:], 1.0, None, ALU.add, None, ['e1'], ['e1'])
                op('dve', lambda e: e.reciprocal(e1[:], e1[:]), ['e1'], ['e1'])
                tt('dve', scT[:], cT[:], e1[:], ALU.mult, ['cT', 'e1'], ['scT'])
                cp('dve', screp[:], bc_last(scT[:], [128, 8, 128]), ['scT'], ['screp'])
                ms_('dve', ones1[:], 1.0, ['ones1'])
                w_ada_r = w_ada.rearrange("(c p) n -> p c n", p=128)
                gbc = {2: g1bc, 5: g2bc}
                for n in range(12):
                    vec, half = n // 2, n % 2
                    wt = wst[n % 2]
                    kwt = f"wst{n % 2}"
                    dma('pool', wt[:], w_ada_r[:, :, n * 512:(n + 1) * 512], w=[kwt])
                    if vec in gbc:
                        p = pm[half]
                        kp = f"pm{half}"
                        for c in range(8):
                            mm(p[:], screp[:, c, :], wt[:, c, :], c == 0, False, [kwt, 'screp'], [kp])
                        mm(p[:], ones1[0:1, :], badaR[0:1, n * 512:(n + 1) * 512], False, True, ['ones1', 'badaR'], [kp])
                        cp('act', gbc[vec][:, half * 512:(half + 1) * 512], p[:], [kp], [f"gbc{vec}{half}"])
                    else:
                        for cc in range(4):
                            for c in range(8):
                                mm(pcol[:, cc:cc + 1], wt[:, c, cc * 128:(cc + 1) * 128], scT[:, c:c + 1], c == 0, c == 7, [kwt, 'scT'], ['pcol'])
                        tt('dve', modc[:, vec, half * 4:(half + 1) * 4], pcol[:, 0:4], badaC[:, n * 4:(n + 1) * 4], ALU.add, ['pcol', 'badaC'], ['modc'])
                ts('dve', e1[:], modc[:, 1, :], 1.0, None, ALU.add, None, ['modc', 'e1'], ['e1'])
                tt('dve', A1[:], e1[:], n1c[:], ALU.mult, ['e1', 'n1c'], ['A1'])
                cp('dve', B1[:], modc[:, 0, :], ['modc'], ['B1'])
                ts('dve', e1[:], modc[:, 4, :], 1.0, None, ALU.add, None, ['modc', 'e1', 'A1'], ['e1'])
                tt('dve', A2[:], e1[:], n2c[:], ALU.mult, ['e1', 'n2c'], ['A2'])
                cp('dve', B2[:], modc[:, 3, :], ['modc'], ['B2'])
                S.barrier()

            if dbg:
                dma('sp', dbg_g1, g1bc[:], r=['gbc20', 'gbc21'], w=['dbg_g1'])
                dma('sp', dbg_ab[:, 0:8], A1[:], r=['A1'], w=['dbg_ab0'])
                dma('sp', dbg_ab[:, 8:16], B1[:], r=['B1'], w=['dbg_ab1'])
                dma('sp', dbg_ab[:, 16:24], A2[:], r=['A2'], w=['dbg_ab2'])
                dma('sp', dbg_ab[:, 24:32], B2[:], r=['B2'], w=['dbg_ab3'])
            chk('p0')

            def norm_to_hT(xt, P, kx, xh, kxh, ptr, kptr, tmpf, ktmp, hT_dst, khT, A, B, kA, kB, ncols):
                rstd, kr = rms_rstd(xt, P, kx)
                ts('dve', xh[0:P, :], xt, rstd, None, ALU.mult, None, [kx, kr], [kxh])
                for c in range(8):
                    tp(ptr[:, c, 0:P], xh[0:P, c * 128:(c + 1) * 128], ident_b[0:P, 0:P], [kxh, 'ident'], [kptr])
                tt('dve', tmpf[:, :, 0:P], ptr[:, :, 0:P], bc_last(A[:], [128, 8, P]), ALU.mult, [kptr, kA], [ktmp])
                tt('pool', hT_dst, tmpf[:, :, 0:P], bc_last(B[:], [128, 8, P]), ALU.add, [ktmp, kB], [khT])

            with ExitStack() as es3:
                Kaug0 = es3.enter_context(nc.sbuf_tensor("s_Kaug0", [128, 8192], BF16))
                Kaug1 = es3.enter_context(nc.sbuf_tensor("s_Kaug1", [128, 8192], BF16))
                VS = es3.enter_context(nc.sbuf_tensor("s_VS", [128, 64, 2, 65], BF16))
                KCT = es3.enter_context(nc.sbuf_tensor("s_KCT", [128, 512], BF16))
                VC = es3.enter_context(nc.sbuf_tensor("s_VC", [128, 4, 2, 65], BF16))
                Kaug = [Kaug0, Kaug1]
                dma('pool', Kaug0[64:128, :], erows_d, w=['Kaug0e'])
                dma('pool', Kaug1[0:64, :], erows_d, w=['Kaug1e'])
                ms_('pool', VS[:, :, :, 64:65], 1.0, ['VSone'])
                ms_('pool', VC[:, :, :, 64:65], 1.0, ['VCone'])

                with ExitStack() as es4:
                    w1p = es4.enter_context(nc.sbuf_tensor("s_w1p", [128, 8, 768], BF16))
                    xr = es4.enter_context(nc.sbuf_tensor("s_xr", [128, 3, 1024], F32))
                    xh = es4.enter_context(nc.sbuf_tensor("s_xh", [128, 2, 1024], BF16))
                    tmpf = es4.enter_context(nc.sbuf_tensor("s_tmpf", [128, 2, 8, 128], F32))
                    hT = es4.enter_context(nc.sbuf_tensor("s_hT", [128, 2, 8, 512], BF16))
                    rawk = es4.enter_context(nc.sbuf_tensor("s_rawk", [128, 8192], BF16))
                    rawv = es4.enter_context(nc.sbuf_tensor("s_rawv", [128, 8192], BF16))
                    W1k = es4.enter_context(nc.sbuf_tensor("s_W1k", [128, 32, 256], BF16))
                    W1v = es4.enter_context(nc.sbuf_tensor("s_W1v", [128, 32, 256], BF16))
                    sq = es4.enter_context(nc.sbuf_tensor("s_sq", [128, 512], BF16))
                    rs = es4.enter_context(nc.sbuf_tensor("s_rs", [128, 512], F32))
                    kwst = es4.enter_context(nc.sbuf_tensor("s_kwst", [128, 2, 512], BF16))
                    vwst = es4.enter_context(nc.sbuf_tensor("s_vwst", [128, 2, 2, 65], BF16))
                    peTk = es4.enter_context(nc.sbuf_tensor("s_peTk", [128, 32], BF16))
                    peTv = es4.enter_context(nc.sbuf_tensor("s_peTv", [128, 32], BF16))
                    cbias = es4.enter_context(nc.sbuf_tensor("s_cbias", [128, 4], F32))
                    gx = es4.enter_context(nc.sbuf_tensor("s_gx", [128, 512], F32))
                    g2 = es4.enter_context(nc.sbuf_tensor("s_g2", [128, 512], F32))
                    gel = es4.enter_context(nc.sbuf_tensor("s_gel", [128, 4, 2, 512], BF16))
                    w2kp = es4.enter_context(nc.sbuf_tensor("s_w2kp", [128, 2, 2, 128], BF16))
                    w2vb = es4.enter_context(nc.sbuf_tensor("s_w2vb", [128, 2, 64], BF16))
                    ptr0 = es4.enter_context(nc.psum_tensor("p_ptr0", [128, 8, 128], BF16))
                    ptr1 = es4.enter_context(nc.psum_tensor("p_ptr1", [128, 8, 128], BF16))
                    pf0 = es4.enter_context(nc.psum_tensor("p_pf0", [128, 512], F32))
                    pf1 = es4.enter_context(nc.psum_tensor("p_pf1", [128, 512], F32))
                    pbd = es4.enter_context(nc.psum_tensor("p_pbd", [128, 512], F32))
                    ptok = es4.enter_context(nc.psum_tensor("p_ptok", [128, 256], F32))
                    pcb = es4.enter_context(nc.psum_tensor("p_pcb", [128, 8], F32))
                    ptr = [ptr0, ptr1]
                    pfb = [pf0, pf1]
                    pfi = [0]
                    dma('pool', w1p[:], w_in_r[:, :, 512:1280], w=['w1p'])
                    for kv, (W1, w1d, peT, ped) in enumerate(((W1k, w1k_d, peTk, peTk_d), (W1v, w1v_d, peTv, peTv_d))):
                        w1r = w1d.rearrange("(l d) c -> d l c", d=64)
                        dma('pool', W1[0:64, :, :], w1r, w=[f"W1{kv}a"])
                        dma('pool', W1[64:128, :, :], w1r, w=[f"W1{kv}b"])
                        dma('pool', peT[:], ped, w=[f"peT{kv}"])
                    ms_('dve', w2kp[:], 0.0, ['w2kp'])
                    for cc in range(2):
                        for g in range(2):
                            dma('pool', w2kp[:, cc, g, 64 * g:64 * g + 64], w2k_d[cc * 128:(cc + 1) * 128, :], r=['w2kp'], w=[f"w2kp{cc}{g}"])
                        dma('pool', w2vb[:, cc, :], w2v_d[cc * 128:(cc + 1) * 128, :], w=[f"w2vb{cc}"])
                    ms_('pool', vwst[:, :, :, 64:65], 1.0, ['vwst1'])
                    ms_('dve', gel[:, :, :, 511:512], 0.0, ['gelz'])

                    def stacked_norm(pf, kpf, N, outs):
                        act(sq[:, 0:N], pf[:, 0:N], AF.Square, [kpf], ['sq'])
                        mm(pbd[:, 0:N], bd_b[:], sq[:, 0:N], True, True, ['bd', 'sq'], ['pbd'])
                        act(rs[:, 0:N], pbd[:, 0:N], AF.Ln, ['pbd'], ['rs'], scale=1.0 / 64, bias=EPS)
                        act(rs[:, 0:N], rs[:, 0:N], AF.Exp, ['rs'], ['rs'], scale=-0.5)
                        for dst, p0, p1, kd in outs:
                            stt('dve', dst, pf[p0:p1, 0:N], knc[p0:p1, 0:1], rs[p0:p1, 0:N], ALU.mult, ALU.mult, [kpf, 'rs', 'knc'], [kd])

                    chk('p1a')
                    wff_jobs = [('f1', c) for c in range(8)] + [('f2', c) for c in range(32)]
                    for s in range(NSTEP1):
                        hTs = hT[:, s % 2]
                        khT = f"hT{s % 2}"
                        for j in range(4):
                            T = 4 * s + j
                            xt = xr[:, T % 3, :]
                            kx = f"xr{T % 3}"
                            dma('sp', xt, xfull[T * 128:(T + 1) * 128, :], w=[kx])
                            norm_to_hT(xt, 128, kx, xh[:, T % 2, :], f"xh{T % 2}", ptr[T % 2], f"ptr{T % 2}",
                                       tmpf[:, T % 2], f"tmpf{T % 2}", hTs[:, :, j * 128:(j + 1) * 128], khT, A1, B1, 'A1', 'B1', 128)
                        tok = slice(s * 512, (s + 1) * 512)
                        for name, col in (('ks', 256), ('kw', 512), ('kc', 0), ('vc', 128)):
                            pf = pfb[pfi[0] % 2]
                            kpf = f"pf{pfi[0] % 2}"
                            pfi[0] += 1
                            for c in range(8):
                                mm(pf[:], w1p[:, c, col:col + 128], hTs[:, c, :], c == 0, c == 7, ['w1p', khT], [kpf])
                            if name == 'kc':
                                cp('act', rawk[:, tok], pf[:], [kpf], ['rawk'])
                            elif name == 'vc':
                                cp('act', rawv[:, tok], pf[:], [kpf], ['rawv'])
                            elif name == 'ks':
                                stacked_norm(pf, kpf, 512, [(Kaug0[0:64, tok], 0, 64, 'Kaug0d'), (Kaug1[64:128, tok], 64, 128, 'Kaug1d')])
                            else:
                                stacked_norm(pf, kpf, 512, [(kwst[:, s % 2, :], 0, 128, f"kwst{s % 2}")])
                                dma('sp', kw_s[:, tok], kwst[:, s % 2, :], r=[f"kwst{s % 2}"], w=['kw_s'])
                        for j in range(4):
                            T = 4 * s + j
                            for c in range(8):
                                mm(ptok[:, 0:128], hTs[:, c, j * 128:(j + 1) * 128], w1p[:, c, 384:512], c == 0, c == 7, ['w1p', khT], ['ptok'])
                            for c in range(8):
                                mm(ptok[:, 128:256], hTs[:, c, j * 128:(j + 1) * 128], w1p[:, c, 640:768], c == 0, c == 7, ['w1p', khT], ['ptok'])
                            cp('act', VS[:, T, :, 0:64], ptok[:, 0:128].rearrange("p (g d) -> p g d", g=2), ['ptok'], ['VSd'])
                            cp('dve', vwst[:, T % 2, :, 0:64], ptok[:, 128:256].rearrange("p (g d) -> p g d", g=2), ['ptok'], [f"vwst{T % 2}"])
                            dma('sp', vw_s[T], vwst[:, T % 2].rearrange("p g d -> p (g d)"), r=[f"vwst{T % 2}", 'vwst1'], w=['vw_s'])
                        if s == 0:
                            chk('p1b')
                        for _ in range(3):
                            if wff_jobs:
                                kind, c = wff_jobs.pop(0)
                                if kind == 'f1':
                                    dma('pool', wff1_s[:, c, :], w_ff1[c * 128:(c + 1) * 128, :], w=['wff1_s'])
                                else:
                                    dma('pool', wff2_s[:, c, :], w_ff2[c * 128:(c + 1) * 128, :], w=['wff2_s'])
                    while wff_jobs:
                        kind, c = wff_jobs.pop(0)
                        if kind == 'f1':
                            dma('pool', wff1_s[:, c, :], w_ff1[c * 128:(c + 1) * 128, :], w=['wff1_s'])
                        else:
                            dma('pool', wff2_s[:, c, :], w_ff2[c * 128:(c + 1) * 128, :], w=['wff2_s'])

                    chk('p1c')
                    for kv, (W1, peT) in enumerate(((W1k, peTk), (W1v, peTv))):
                        for cc in range(2):
                            for l in range(32):
                                mm(pcb[:, kv * 2 + cc:kv * 2 + cc + 1], W1[0:64, l, cc * 128:(cc + 1) * 128], peT[0:64, l:l + 1],
                                   l == 0, l == 31, [f"W1{kv}a", f"peT{kv}"], ['pcb'])
                    cp('dve', cbias[:], pcb[:, 0:4], ['pcb'], ['cbias'])
                    for kv, (W1, raw, kraw) in enumerate(((W1k, rawk, 'rawk'), (W1v, rawv, 'rawv'))):
                        for g in range(2):
                            rows = slice(64 * g, 64 * g + 64)
                            for cc in range(2):
                                pf = pfb[pfi[0] % 2]
                                kpf = f"pf{pfi[0] % 2}"
                                pfi[0] += 1
                                for l in range(32):
                                    mm(pf[:, 0:511], W1[rows, l, cc * 128:(cc + 1) * 128], raw[rows, l:l + 16 * 510 + 1:16], l == 0, l == 31,
                                       [f"W1{kv}a", f"W1{kv}b", kraw], [kpf])
                                kg = f"gel{kv}{g}"
                                act(gx[:, 0:511], pf[:, 0:511], AF.Identity, [kpf, 'cbias'], ['gx'], bias=cbias[:, kv * 2 + cc:kv * 2 + cc + 1])
                                tt('dve', g2[:, 0:511], gx[:, 0:511], gx[:, 0:511], ALU.mult, ['gx'], ['g2'])
                                ts('dve', g2[:, 0:511], g2[:, 0:511], 0.044715, 1.0, ALU.mult, ALU.add, ['g2'], ['g2'])
                                tt('dve', g2[:, 0:511], g2[:, 0:511], gx[:, 0:511], ALU.mult, ['g2', 'gx'], ['g2'])
                                act(g2[:, 0:511], g2[:, 0:511], AF.Tanh, ['g2'], ['g2'], scale=0.7978845608028654)
                                ts('dve', g2[:, 0:511], g2[:, 0:511], 0.5, 0.5, ALU.mult, ALU.add, ['g2'], ['g2'])
                                tt('dve', gel[:, kv * 2 + g, cc, 0:511], g2[:, 0:511], gx[:, 0:511], ALU.mult, ['g2', 'gx', 'gelz'], [kg])
                    pf = pfb[pfi[0] % 2]
                    kpf = f"pf{pfi[0] % 2}"
                    pfi[0] += 1
                    n_ = 0
                    for g in range(2):
                        for cc in range(2):
                            mm(pf[:], w2kp[:, cc, g, :], gel[:, g, cc, :], n_ == 0, n_ == 3, [f"w2kp{cc}{g}", f"gel0{g}", 'gelz'], [kpf])
                            n_ += 1
                    stacked_norm(pf, kpf, 512, [(KCT[:], 0, 128, 'KCT')])
                    for ct in range(4):
                        for g in range(2):
                            for cc in range(2):
                                mm(ptok[:, 0:64], gel[:, 2 + g, cc, ct * 128:(ct + 1) * 128], w2vb[:, cc, :], cc == 0, cc == 1,
                                   [f"gel1{g}", 'gelz', f"w2vb{cc}"], ['ptok'])
                            cp('act', VC[:, ct, g, 0:64], ptok[:, 0:64], ['ptok'], ['VCd'])
                    S.barrier()

                if dbg:
                    dma('sp', dbg_k0, Kaug0[:, 0:1024], r=['Kaug0d', 'Kaug0e'], w=['dbg_k0'])
                    dma('sp', dbg_k1, Kaug1[:, 0:1024], r=['Kaug1d', 'Kaug1e'], w=['dbg_k1'])
                    dma('sp', dbg_vs, VS[:, 0:8].rearrange("p t g d -> p (t g d)"), r=['VSd', 'VSone'], w=['dbg_vs'])
                    dma('sp', dbg_kct, KCT[:], r=['KCT'], w=['dbg_kct'])
                    dma('sp', dbg_vc, VC[:].rearrange("p t g d -> p (t g d)"), r=['VCd', 'VCone'], w=['dbg_vc'])
                chk('p1')
                with ExitStack() as es5:
                    w2p = es5.enter_context(nc.sbuf_tensor("s_w2p", [128, 8, 536], BF16))
                    SELT = es5.enter_context(nc.sbuf_tensor("s_SELT", [128, 9, 1024], BF16))
                    WINT = es5.enter_context(nc.sbuf_tensor("s_WINT", [128, 6, 1024], BF16))
                    CFAR = es5.enter_context(nc.sbuf_tensor("s_CFAR", [128, 1024], BF16))
                    ctab = es5.enter_context(nc.sbuf_tensor("s_ctab", [128, 4, 1024], BF16))
                    stg = es5.enter_context(nc.sbuf_tensor("s_stg", [128, 2, 1024], F32))
                    stg2 = es5.enter_context(nc.sbuf_tensor("s_stg2", [128, 1024], F32))
                    mst = es5.enter_context(nc.sbuf_tensor("s_mst", [128, 4, 128], F32))
                    keep = es5.enter_context(nc.sbuf_tensor("s_keep", [128, 254], F32))
                    addc = es5.enter_context(nc.sbuf_tensor("s_addc", [128, 254], F32))
                    OVb = es5.enter_context(nc.sbuf_tensor("s_OVb", [128, 512], BF16))
                    qg = es5.enter_context(nc.sbuf_tensor("s_qg", [128, 64], F32))
                    KWr = es5.enter_context(nc.sbuf_tensor("s_KWr", [128, 8, 128], BF16))
                    VWr = es5.enter_context(nc.sbuf_tensor("s_VWr", [128, 8, 130], BF16))
                    xo = es5.enter_context(nc.sbuf_tensor("s_xo", [128, 2, 1024], F32))
                    xho = es5.enter_context(nc.sbuf_tensor("s_xho", [128, 1024], BF16))
                    tmpo = es5.enter_context(nc.sbuf_tensor("s_tmpo", [128, 8, 128], F32))
                    hTo = es5.enter_context(nc.sbuf_tensor("s_hTo", [128, 8, 128], BF16))
                    qsq = es5.enter_context(nc.sbuf_tensor("s_qsq", [128, 512], F32))
                    qss = es5.enter_context(nc.sbuf_tensor("s_qss", [128, 16], F32))
                    qtmp = es5.enter_context(nc.sbuf_tensor("s_qtmp", [128, 512], F32))
                    qnp = es5.enter_context(nc.sbuf_tensor("s_qnp", [128, 512], BF16))
                    QQ = es5.enter_context(nc.sbuf_tensor("s_QQ", [128, 2, 4, 512], BF16))
                    sig = es5.enter_context(nc.sbuf_tensor("s_sig", [128, 24], F32))
                    PT = es5.enter_context(nc.sbuf_tensor("s_PT", [128, 6, 512], BF16))
                    OCs = es5.enter_context(nc.sbuf_tensor("s_OCs", [128, 2, 260], F32))
                    dsm = es5.enter_context(nc.sbuf_tensor("s_dsm", [128, 32], F32))
                    imp = es5.enter_context(nc.sbuf_tensor("s_imp", [128, 128], F32))
                    scr = es5.enter_context(nc.sbuf_tensor("s_scr", [128, 128], F32))
                    wk = es5.enter_context(nc.sbuf_tensor("s_wk", [128, 128], F32))
                    m8 = es5.enter_context(nc.sbuf_tensor("s_m8", [128, 16], F32))
                    NS = es5.enter_context(nc.sbuf_tensor("s_NS", [128, 2, 128], BF16))
                    acc = es5.enter_context(nc.sbuf_tensor("s_acc", [128, 256], F32))
                    acc2 = es5.enter_context(nc.sbuf_tensor("s_acc2", [128, 256], F32))
                    attn_bf = es5.enter_context(nc.sbuf_tensor("s_attn_bf", [128, 2, 512], BF16))
                    ST0 = es5.enter_context(nc.psum_tensor("p_ST0", [128, 512], F32))
                    ST1 = es5.enter_context(nc.psum_tensor("p_ST1", [128, 512], F32))
                    OC = es5.enter_context(nc.psum_tensor("p_OC", [128, 260], F32))
                    IMP = es5.enter_context(nc.psum_tensor("p_IMP", [128, 512], F32))
                    OS = es5.enter_context(nc.psum_tensor("p_OS", [128, 260], F32))
                    OW = es5.enter_context(nc.psum_tensor("p_OW", [128, 260], F32))
                    M0 = es5.enter_context(nc.psum_tensor("p_M0", [128, 512], F32))
                    M1 = es5.enter_context(nc.psum_tensor("p_M1", [128, 1024], BF16))
                    ST = [ST0, ST1]
                    ST3 = [ST0, ST1, IMP]
                    dma('pool', w2p[:, :, 0:512], w_in_r[:, :, 0:512], w=['w2pq'])
                    dma('pool', w2p[:, :, 512:536], w_in_r[:, :, 1280:1304], w=['w2pg'])
                    dma('sp', keep[:], keep_d, w=['keep'])
                    dma('sp', addc[:], addc_d, w=['addc'])
                    dma('pool', OVb[:], ov_d, w=['OVb'])
                    dma('sp', qg[:], qg_d, w=['qg'])
                    ts('dve', qg[:], qg[:], 0.125, None, ALU.mult, None, ['qg'], ['qg'])
                    cp('dve', CFAR[:].rearrange("p (h q) -> p h q", h=8), bc_last(b31t[:], [128, 8, 128]), ['b31t'], ['CFAR'])
                    b31b = bc_last(b31t[:], [128, 8, 128])
                    for d in range(9):
                        sg = stg[:, d % 2, :]
                        ksg = f"stg{d % 2}"
                        dma('sp', sg, bias9_d[d], w=[ksg])
                        dma('sp', mst[:, d % 2, :], ms_d[d], w=[f"mst{d % 2}"])
                        if d < 6:
                            dma('sp', mst[:, 2 + d % 2, :], mw_d[d], w=[f"mst{2 + d % 2}"])
                            tt('pool', WINT[:, d, :].rearrange("p (h q) -> p h q", h=8), sg.rearrange("p (h q) -> p h q", h=8),
                               bc_mid(mst[:, 2 + d % 2, :], [128, 8, 128]), ALU.add, [ksg, f"mst{2 + d % 2}"], ['WINT'])
                        tt('dve', stg2[:].rearrange("p (h q) -> p h q", h=8), sg.rearrange("p (h q) -> p h q", h=8), b31b, ALU.subtract, [ksg, 'b31t'], ['stg2'])
                        tt('dve', SELT[:, d, :].rearrange("p (h q) -> p h q", h=8), stg2[:].rearrange("p (h q) -> p h q", h=8),
                           bc_mid(mst[:, d % 2, :], [128, 8, 128]), ALU.add, ['stg2', f"mst{d % 2}"], ['SELT'])

                    cnt = [0]
                    tabn = [0]

                    def branch(O, kO, kts, kfn, vfn, qfn, tabfn, banks=None, D=1):
                        nk = len(kts)
                        state = []

                        def emit_qk(n):
                            kt = kts[n]
                            bl = banks if banks is not None else ST
                            st = bl[cnt[0] % len(bl)]
                            kst = f"STb{cnt[0] % len(bl)}"
                            pt = PT[:, cnt[0] % 6, :]
                            kpt = f"PT{cnt[0] % 6}"
                            cnt[0] += 1
                            lk, rk = kfn(kt)
                            qa, rq = qfn(kt)
                            tab = tabfn(kt)
                            mm(st[:], lk, qa, True, tab is None, rk + rq, [kst])
                            if tab is not None:
                                mm(st[:], ident_b[:], tab[0], False, True, ['ident'] + tab[1], [kst])
                            act(pt, st[:], AF.Exp, [kst], [kpt])
                            state.append((pt, kpt))

                        for n0 in range(min(D, nk)):
                            emit_qk(n0)
                        for n in range(nk):
                            if n + D < nk:
                                emit_qk(n + D)
                            pt, kpt = state[n]
                            vv, rv = vfn(kts[n])
                            for r in range(4):
                                mm(O[:, r * 65:(r + 1) * 65], pt[:, r * 128:(r + 1) * 128], vv, n == 0 and r == 0, n == nk - 1, [kpt] + rv, [kO])
                            yield pt, kpt, n

                    for i in range(NT):
                        xt = xo[:, i % 2, :]
                        kx = f"xo{i % 2}"
                        dma('sp', xt, xown[i * 128:(i + 1) * 128, :], w=[kx])
                        for T in (2 * i, 2 * i + 1):
                            dma('sp', KWr[:, T % 8, :], kw_s[:, T * 128:(T + 1) * 128], r=['kw_s'], w=[f"KWr{T % 8}"])
                            dma('sp', VWr[:, T % 8, :], vw_s[T], r=['vw_s'], w=[f"VWr{T % 8}"])
                        nct = (2 * i + 1) // 16 + 1
                        ctabs = {}
                        for ct in range(nct):
                            dpp = 2 * i + 1 - 16 * ct
                            if dpp <= 23:
                                idx = (dpp - 1) // 2
                                sl = tabn[0] % 4
                                tabn[0] += 1
                                sg = stg[:, sl % 2, :]
                                ksg = f"stg{sl % 2}"
                                dma('sp', sg, cb_d[idx], w=[ksg])
                                dma('sp', mst[:, sl % 2, :], cm_d[1 if ct == 3 else 0, idx], w=[f"mst{sl % 2}"])
                                tt('pool', ctab[:, sl, :].rearrange("p (h q) -> p h q", h=8), sg.rearrange("p (h q) -> p h q", h=8),
                                   bc_mid(mst[:, sl % 2, :], [128, 8, 128]), ALU.add, [ksg, f"mst{sl % 2}"], [f"ctab{sl}"])
                                ctabs[ct] = (ctab[:, sl, :], [f"ctab{sl}"])
                            else:
                                ctabs[ct] = (CFAR[:], ['CFAR'])
                        norm_to_hT(xt, 128, kx, xho[:], 'xho', M1[:].rearrange("p (c t) -> p c t", c=8), 'M1', tmpo[:], 'tmpo', hTo[:], 'hTo', A1, B1, 'A1', 'B1', 128)
                        for c in range(8):
                            mm(M0[:], hTo[:, c, :], w2p[:, c, 0:512], c == 0, c == 7, ['hTo', 'w2pq'], ['M0'])
                        act(qsq[:], M0[:], AF.Square, ['M0'], ['qsq'])
                        op('dve', lambda e: e.tensor_reduce(out=qss[:, 0:8], in_=qsq[:].rearrange("p (h d) -> p h d", h=8), axis=AX.X, op=ALU.add), ['qsq'], ['qss0'])
                        act(qss[:, 8:16], qss[:, 0:8], AF.Ln, ['qss0'], ['qss1'], scale=1.0 / 64, bias=EPS)
                        act(qss[:, 0:8], qss[:, 8:16], AF.Exp, ['qss1'], ['qss0'], scale=-0.5)
                        tt('dve', qtmp[:].rearrange("p (h d) -> p h d", h=8), M0[:].rearrange("p (h d) -> p h d", h=8),
                           bc_last(qss[:, 0:8], [128, 8, 64]), ALU.mult, ['M0', 'qss0'], ['qtmp'])
                        tt('dve', qnp[:].rearrange("p (r g d) -> p g r d", r=4, g=2), qtmp[:].rearrange("p (g r d) -> p g r d", g=2, r=4),
                           qg[:].unsqueeze(1).unsqueeze(1).to_broadcast([128, 2, 4, 64]), ALU.mult, ['qtmp', 'qg'], ['qnp'])
                        for c in range(8):
                            mm(M0[:, 0:24], hTo[:, c, :], w2p[:, c, 512:536], c == 0, c == 7, ['hTo', 'w2pg', 'qtmp'], ['M0'])
                        act(sig[:], M0[:, 0:24], AF.Exp, ['M0'], ['sig'], scale=-1.0)
                        ts('dve', sig[:], sig[:], 1.0, None, ALU.add, None, ['sig'], ['sig'])
                        op('dve', lambda e: e.reciprocal(sig[:], sig[:]), ['sig'], ['sig'])
                        b = i % 2
                        useB = i >= 16
                        for r in range(4):
                            tp(M1[:, r * 128:(r + 1) * 128], qnp[:, r * 128:(r + 1) * 128], ident_b[:], ['qnp', 'ident', 'tmpo'], ['M1'])
                        QA = [QQ[:, b, 0, :], QQ[:, b, 1, :]]
                        QB = [QQ[:, b, 2, :], QQ[:, b, 3, :]]
                        kQ = [f"QQ{b}{n}" for n in range(4)]
                        cp('act', QA[0][0:64, :], M1[0:64, 0:512], ['M1'], [kQ[0] + 'd'])
                        cp('act', QA[1][64:128, :], M1[64:128, 0:512], ['M1'], [kQ[1] + 'd'])
                        if useB:
                            cp('act', QB[0][0:64, :], M1[0:64, 0:512], ['M1'], [kQ[2] + 'd'])
                            cp('act', QB[1][64:128, :], M1[64:128, 0:512], ['M1'], [kQ[3] + 'd'])
                        o_ = 126 - 4 * i
                        for g in range(2):
                            rows = slice(64 * g, 64 * g + 64)
                            gen = branch(OC, 'OC', list(range(nct)),
                                         lambda ct: (KCT[rows, ct * 128:(ct + 1) * 128], ['KCT']),
                                         lambda ct: (VC[:, ct, g, :], ['VCd', 'VCone']),
                                         lambda ct: (QA[g][rows, :], [kQ[g] + 'd']),
                                         lambda ct: (ctabs[ct][0][:, g * 512:(g + 1) * 512], ctabs[ct][1]))
                            for pt, kpt, n in gen:
                                for r in range(4):
                                    mm(IMP[:, r * 128:(r + 1) * 128], pt[:, r * 128:(r + 1) * 128], OVb[:, n * 128:(n + 1) * 128], n == 0 and r == 0, n == nct - 1, [kpt, 'OVb'], ['IMP'])
                            cp('dve', OCs[:, g, :], OC[:], ['OC'], [f"OCs{g}"])
                            dc = dsm[:, 0:4]
                            cp('dve', dc, OCs[:, g, 64:260:65], [f"OCs{g}"], ['dc'])
                            ts('dve', dc, dc, 1e-30, None, ALU.max, None, ['dc'], ['dc'])
                            op('dve', lambda e: e.reciprocal(dc, dc), ['dc'], ['dc'])
                            ts('dve', imp[:], IMP[:, 0:128], dsm[:, 0:1], None, ALU.mult, None, ['IMP', 'dc'], ['imp'])
                            for r in range(1, 4):
                                stt('dve', imp[:], IMP[:, r * 128:(r + 1) * 128], dsm[:, r:r + 1], imp[:], ALU.mult, ALU.add, ['IMP', 'dc', 'imp'], ['imp'])
                            tt('dve', scr[:], imp[:], keep[:, o_:o_ + 128], ALU.mult, ['imp', 'keep'], ['scr'])
                            tt('dve', scr[:], scr[:], addc[:, o_:o_ + 128], ALU.add, ['scr', 'addc'], ['scr'])
                            ms_('dve', scr[:, 0:1], 1.0e6, ['scr'])
                            op('dve', lambda e: e.max(out=m8[:, 0:8], in_=scr[:]), ['scr'], ['m8a'])
                            op('dve', lambda e: e.match_replace(out=wk[:], in_to_replace=m8[:, 0:8], in_values=scr[:], imm_value=-1.0e30), ['scr', 'm8a'], ['wk'])
                            op('dve', lambda e: e.max(out=m8[:, 8:16], in_=wk[:]), ['wk'], ['m8b'])
                            cA = slice(64, 128) if g == 0 else slice(0, 64)
                            ts('dve', NS[:, 0, cA], scr[:, 0:64], m8[:, 15:16], NEGM, ALU.is_lt, ALU.mult, ['scr', 'm8b'], [f"NS0{g}"])
                            ts('dve', NS[:, 1, cA], scr[:, 64:128], m8[:, 15:16], NEGM, ALU.is_lt, ALU.mult, ['scr', 'm8b'], [f"NS1{g}"])
                        for hb in range(2 if useB else 1):
                            tp(M1[:, 512 + hb * 128:512 + (hb + 1) * 128], NS[:, hb, :], ident_b[:], [f"NS{hb}0", f"NS{hb}1", 'ident'], ['M1'])
                            src = M1[:, 512 + hb * 128:512 + (hb + 1) * 128]
                            Qh = QA if hb == 0 else QB
                            tt('dve', Qh[0][64:128, :].rearrange("p (r q) -> p r q", r=4), bc_mid(src[64:128, :], [64, 4, 128]),
                               bc_last(b31t[64:128, 0:4], [64, 4, 128]), ALU.add, ['M1', 'b31t'], [kQ[2 * hb] + 'n'])
                            tt('dve', Qh[1][0:64, :].rearrange("p (r q) -> p r q", r=4), bc_mid(src[0:64, :], [64, 4, 128]),
                               bc_last(b31t[0:64, 4:8], [64, 4, 128]), ALU.add, ['M1', 'b31t'], [kQ[2 * hb + 1] + 'n'])
                        nk = 2 * i + 2
                        for g in range(2):
                            rows = slice(64 * g, 64 * g + 64)
                            for _ in branch(OS, 'OS', list(range(nk)),
                                            lambda kt: (Kaug[g][:, kt * 128:(kt + 1) * 128], [f"Kaug{g}d", f"Kaug{g}e"]),
                                            lambda kt: (VS[:, kt, g, :], ['VSd', 'VSone']),
                                            lambda kt: ((QA if kt < 32 else QB)[g][:, :], [kQ[(0 if kt < 32 else 2) + g] + 'd', kQ[(0 if kt < 32 else 2) + g] + 'n']),
                                            lambda kt: ((SELT[:, nk - 1 - kt, g * 512:(g + 1) * 512], ['SELT']) if nk - 1 - kt <= 8 else None), banks=ST3, D=2):
                                pass
                            wk_ts = list(range(max(0, 2 * i - 4), nk))
                            for _ in branch(OW, 'OW', wk_ts,
                                            lambda kt: (KWr[rows, kt % 8, :], [f"KWr{kt % 8}"]),
                                            lambda kt: (VWr[:, kt % 8, g * 65:(g + 1) * 65], [f"VWr{kt % 8}"]),
                                            lambda kt: (QA[g][rows, :], [kQ[g] + 'd']),
                                            lambda kt: (WINT[:, nk - 1 - kt, g * 512:(g + 1) * 512], ['WINT']), banks=ST3, D=2):
                                pass
                            den = dsm[:, 8:20].rearrange("p (b r) -> p b r", b=3)
                            cp('dve', den[:, 0, :], OCs[:, g, 64:260:65], [f"OCs{g}"], ['den'])
                            cp('dve', den[:, 1, :], OS[:, 64:260:65], ['OS'], ['den'])
                            cp('dve', den[:, 2, :], OW[:, 64:260:65], ['OW'], ['den'])
                            ts('dve', dsm[:, 8:20], dsm[:, 8:20], 1e-30, None, ALU.max, None, ['den'], ['den'])
                            op('dve', lambda e: e.reciprocal(dsm[:, 8:20], dsm[:, 8:20]), ['den'], ['den'])
                            tt('dve', den, den, sig[:, 12 * g:12 * g + 12].rearrange("p (r b) -> p b r", b=3), ALU.mult, ['den', 'sig'], ['den'])
                            a3 = acc[:].rearrange("p (r d) -> p r d", r=4)
                            a23 = acc2[:].rearrange("p (r d) -> p r d", r=4)
                            srcs = [(OCs[:, g, :], f"OCs{g}"), (OS[:], 'OS'), (OW[:], 'OW')]
                            for br, (Oap, kOb) in enumerate(srcs):
                                s3 = Oap.rearrange("p (r e) -> p r e", e=65)[:, :, 0:64]
                                cbr = bc_last(den[:, br, :], [128, 4, 64])
                                if br == 0:
                                    tt('dve', a3, s3, cbr, ALU.mult, [kOb, 'den'], ['acc'])
                                else:
                                    tt('dve', a23, s3, cbr, ALU.mult, [kOb, 'den'], ['acc2'])
                                    dst = acc[:] if br == 1 else attn_bf[:, b, g * 256:(g + 1) * 256]
                                    tt('pool', dst, acc[:], acc2[:], ALU.add, ['acc', 'acc2'], ['acc'] if br == 1 else [f"attn{b}"])
                        dma('sp', attn_s[i * 128:(i + 1) * 128, :], attn_bf[:, b, :], r=[f"attn{b}"], w=['attn_s'])
                    S.barrier()

            chk('p2')
            with ExitStack() as es6:
                w3c = es6.enter_context(nc.sbuf_tensor("s_w3c", [128, 8, 1536], BF16))
                wo = es6.enter_context(nc.sbuf_tensor("s_wo", [128, 8, 1024], BF16))
                convc = es6.enter_context(nc.sbuf_tensor("s_convc", [128, 12], F32))
                hm = es6.enter_context(nc.sbuf_tensor("s_hm", [128, 64], F32))
                xk = es6.enter_context(nc.sbuf_tensor("s_xk", [128, 4, 1024], F32))
                xhl = es6.enter_context(nc.sbuf_tensor("s_xhl", [8, 1024], F32))
                xh3 = es6.enter_context(nc.sbuf_tensor("s_xh3", [128, 1024], BF16))
                tmp3 = es6.enter_context(nc.sbuf_tensor("s_tmp3", [128, 8, 128], F32))
                hT3 = es6.enter_context(nc.sbuf_tensor("s_hT3", [128, 8, 512], BF16))
                hTh = es6.enter_context(nc.sbuf_tensor("s_hTh", [128, 8, 8], BF16))
                zt = es6.enter_context(nc.sbuf_tensor("s_zt", [128, 4, 130], F32))
                cgs = es6.enter_context(nc.sbuf_tensor("s_cgs", [128, 512], F32))
                cacc = es6.enter_context(nc.sbuf_tensor("s_cacc", [128, 512], F32))
                zh8 = es6.enter_context(nc.sbuf_tensor("s_zh8", [128, 16], F32))
                at4 = es6.enter_context(nc.sbuf_tensor("s_at4", [128, 4, 512], BF16))
                catT = es6.enter_context(nc.sbuf_tensor("s_catT", [128, 8, 512], BF16))
                mixt = es6.enter_context(nc.sbuf_tensor("s_mixt", [128, 512], F32))
                h2T = es6.enter_context(nc.sbuf_tensor("s_h2T", [128, 8, 512], BF16))
                hidT = es6.enter_context(nc.sbuf_tensor("s_hidT", [128, 32, 512], BF16))
                relu = es6.enter_context(nc.sbuf_tensor("s_relu", [128, 2, 512], F32))
                wr1 = es6.enter_context(nc.sbuf_tensor("s_wr1", [128, 3, 8, 512], BF16))
                wr2 = es6.enter_context(nc.sbuf_tensor("s_wr2", [128, 3, 4, 512], BF16))
                ACC0 = es6.enter_context(nc.psum_tensor("p_ACC0", [128, 512], F32))
                ACC1 = es6.enter_context(nc.psum_tensor("p_ACC1", [128, 512], F32))
                ACC2 = es6.enter_context(nc.psum_tensor("p_ACC2", [128, 512], F32))
                ACC3 = es6.enter_context(nc.psum_tensor("p_ACC3", [128, 512], F32))
                PA = es6.enter_context(nc.psum_tensor("p_PA", [128, 512], F32))
                PB = es6.enter_context(nc.psum_tensor("p_PB", [128, 512], F32))
                PC = es6.enter_context(nc.psum_tensor("p_PC", [128, 512], F32))
                PTR = es6.enter_context(nc.psum_tensor("p_PTR", [128, 8, 128], BF16))
                ACC = [ACC0, ACC1, ACC2, ACC3]
                P3 = [PA, PB, PC]
                p3i = [0]

                def nextp():
                    k = p3i[0] % 3
                    p3i[0] += 1
                    return P3[k], f"P3{k}"

                dma('pool', w3c[:], w_in_r[:, :, 1304:2840], w=['w3c'])
                dma('pool', wo[:], w_out.rearrange("(c p) n -> p c n", p=128), w=['wo'])
                dma('sp', convc[:], convc_d, w=['convc'])
                dma('sp', hm[:], hmask, w=['hm'])
                w1n = [0]
                w2n = [0]
                for s in range(8):
                    for j in range(4):
                        kx = f"xk{j}"
                        dma('sp', xk[:, j, :], xown[(4 * s + j) * 128:(4 * s + j + 1) * 128, :], w=[kx])
                    dma('sp', xhl[:], xhalo[s], w=['xhl'])
                    dma('sp', at4[:], attn_s[s * 512:(s + 1) * 512, :].rearrange("(j t) f -> t j f", t=128), r=['attn_s'], w=['at4'])
                    for j in range(4):
                        norm_to_hT(xk[:, j, :], 128, f"xk{j}", xh3[:], 'xh3', PTR, 'PTR', tmp3[:], 'tmp3', hT3[:, :, j * 128:(j + 1) * 128], 'hT3', A1, B1, 'A1', 'B1', 128)
                    norm_to_hT(xhl[:], 8, 'xhl', xh3[0:8, :], 'xh3', PTR, 'PTR', tmp3, 'tmp3', hTh[:], 'hTh', A1, B1, 'A1', 'B1', 8)
                    for j in range(4):
                        for fc in range(4):
                            tp(PTR[:, fc, :], at4[:, j, fc * 128:(fc + 1) * 128], ident_b[:], ['at4', 'ident', 'tmp3'], ['PTR'])
                        cp('act', catT[:, 0:4, j * 128:(j + 1) * 128], PTR[:, 0:4, :], ['PTR'], ['catTa'])
                    for m in range(4):
                        pcg, kcg = nextp()
                        pu, ku = nextp()
                        pbg, kbg = nextp()
                        for c in range(8):
                            mm(pcg[:], w3c[:, c, m * 128:(m + 1) * 128], hT3[:, c, :], c == 0, c == 7, ['w3c', 'hT3'], [kcg])
                        for c in range(8):
                            mm(pu[:], w3c[:, c, 1024 + m * 128:1024 + (m + 1) * 128], hT3[:, c, :], c == 0, c == 7, ['w3c', 'hT3'], [ku])
                        for c in range(8):
                            mm(pbg[:], w3c[:, c, 512 + m * 128:512 + (m + 1) * 128], hT3[:, c, :], c == 0, c == 7, ['w3c', 'hT3'], [kbg])
                        cp('act', cgs[:], pcg[:], [kcg], ['cgs'])
                        tt('dve', zt[:, :, 2:130], cgs[:].rearrange("p (j t) -> p j t", j=4), pu[:].rearrange("p (j t) -> p j t", j=4), ALU.mult, ['cgs', ku], ['ztm'])
                        ph, kph = nextp()
                        for c in range(8):
                            mm(ph[:, 0:8], w3c[:, c, m * 128:(m + 1) * 128], hTh[:, c, :], c == 0, c == 7, ['w3c', 'hTh'], [kph])
                        for c in range(8):
                            mm(ph[:, 8:16], w3c[:, c, 1024 + m * 128:1024 + (m + 1) * 128], hTh[:, c, :], c == 0, c == 7, ['w3c', 'hTh'], [kph])
                        cp('act', zh8[:], ph[:, 0:16], [kph], ['zh8'])
                        tt('dve', zh8[:, 0:8], zh8[:, 0:8], zh8[:, 8:16], ALU.mult, ['zh8'], ['zh8'])
                        tt('dve', zt[:, :, 0:2], zh8[:, 0:8].rearrange("p (j e) -> p j e", j=4), hm[:, s * 8:(s + 1) * 8].rearrange("p (j e) -> p j e", j=4),
                           ALU.mult, ['zh8', 'hm'], ['zth'])
                        c3 = cacc[:].rearrange("p (j t) -> p j t", j=4)
                        ts('dve', c3, zt[:, :, 0:128], convc[:, m * 3:m * 3 + 1], None, ALU.mult, None, ['ztm', 'zth', 'convc'], ['cacc'])
                        stt('dve', c3, zt[:, :, 1:129], convc[:, m * 3 + 1:m * 3 + 2], c3, ALU.mult, ALU.add, ['ztm', 'zth', 'convc', 'cacc'], ['cacc'])
                        stt('dve', c3, zt[:, :, 2:130], convc[:, m * 3 + 2:m * 3 + 3], c3, ALU.mult, ALU.add, ['ztm', 'zth', 'convc', 'cacc'], ['cacc'])
                        tt('dve', catT[:, 4 + m, :], cacc[:], pbg[:], ALU.mult, ['cacc', kbg], ['catTc'])
                    if dbg and s == 0:
                        dma('sp', dbg_cat, catT[:].rearrange("p c t -> p (c t)"), r=['catTa', 'catTc'], w=['dbg_cat'])
                    for j in range(4):
                        for half in range(2):
                            pw, kpw = nextp()
                            for c in range(8):
                                mm(pw[:], catT[:, c, j * 128:(j + 1) * 128], wo[:, c, half * 512:(half + 1) * 512], c == 0, c == 7, ['catTa', 'catTc', 'wo'], [kpw])
                            tt('dve', mixt[:], pw[:], g1bc[:, half * 512:(half + 1) * 512], ALU.mult, [kpw, f"gbc2{half}"], ['mixt'])
                            tt('pool', xk[:, j, half * 512:(half + 1) * 512], xk[:, j, half * 512:(half + 1) * 512], mixt[:], ALU.add, [f"xk{j}", 'mixt'], [f"xk{j}"])
                        norm_to_hT(xk[:, j, :], 128, f"xk{j}", xh3[:], 'xh3', PTR, 'PTR', tmp3[:], 'tmp3', h2T[:, :, j * 128:(j + 1) * 128], 'h2T', A2, B2, 'A2', 'B2', 128)
                    if dbg and s == 0:
                        dma('sp', dbg_x1, xk[:].rearrange("p j f -> p (j f)"), r=[f"xk{j}" for j in range(4)], w=['dbg_x1'])
                        dma('sp', dbg_h2, h2T[:].rearrange("p c t -> p (c t)"), r=['h2T'], w=['dbg_h2'])
                    for p in range(8):
                        wb = w1n[0] % 3
                        w1n[0] += 1
                        dma('sp', wr1[:, wb], wff1_s[:, :, p * 512:(p + 1) * 512], r=['wff1_s'], w=[f"wr1{wb}"])
                        for hc in range(4):
                            pf, kpf = nextp()
                            for c in range(8):
                                mm(pf[:], wr1[:, wb, c, hc * 128:(hc + 1) * 128], h2T[:, c, :], c == 0, c == 7, [f"wr1{wb}", 'h2T'], [kpf])
                            rb = (4 * p + hc) % 2
                            act(relu[:, rb, :], pf[:], AF.Relu, [kpf], [f"relu{rb}"])
                            tt('dve', hidT[:, 4 * p + hc, :], relu[:, rb, :], relu[:, rb, :], ALU.mult, [f"relu{rb}"], ['hidT'])
                    if dbg and s == 0:
                        dma('sp', dbg_hid, hidT[:, 0:4, :].rearrange("p c t -> p (c t)"), r=['hidT'], w=['dbg_hid'])
                    for half in range(2):
                        for p in range(8):
                            wb = w2n[0] % 3
                            w2n[0] += 1
                            dma('sp', wr2[:, wb], wff2_s[:, 4 * p:4 * p + 4, half * 512:(half + 1) * 512], r=['wff2_s'], w=[f"wr2{wb}"])
                            for j in range(4):
                                for hc in range(4):
                                    mm(ACC[j][:], hidT[:, 4 * p + hc, j * 128:(j + 1) * 128], wr2[:, wb, hc, :], p == 0 and hc == 0, p == 7 and hc == 3,
                                       ['hidT', f"wr2{wb}"], [f"ACC{j}"])
                        for j in range(4):
                            tt('dve', mixt[:], ACC[j][:], g2bc[:, half * 512:(half + 1) * 512], ALU.mult, [f"ACC{j}", f"gbc5{half}"], ['mixt'])
                            tt('pool', xk[:, j, half * 512:(half + 1) * 512], xk[:, j, half * 512:(half + 1) * 512], mixt[:], ALU.add, [f"xk{j}", 'mixt'], [f"xk{j}"])
                    for j in range(4):
                        dma('sp', out_own[(4 * s + j) * 128:(4 * s + j + 1) * 128, :], xk[:, j, :], r=[f"xk{j}"], w=[f"out{j}"])

    except _Stop:
        pass
    final()
    return nc


def _bucket(dist):
    n = np.maximum(dist, 0)
    nf = np.maximum(n, 16).astype(np.float32)
    large = 16 + (np.log(nf / np.float32(16)) / np.float32(np.log(64.0)) * np.float32(16)).astype(np.int32)
    large = np.minimum(large, 31)
    return np.where(n < 16, n, large).astype(np.int64)


def _tables(rel_bias, par):
    k = np.arange(128)[:, None]
    q = np.arange(128)[None, :]
    bias9 = np.empty((9, 128, 8, 128), np.float32)
    ms = np.empty((9, 128, 128), np.float32)
    mw = np.empty((6, 128, 128), np.float32)
    for dp in range(9):
        dist = 128 * (dp - 1 + par) + q - k
        bias9[dp] = rel_bias[_bucket(dist)].transpose(0, 2, 1)
        ms[dp] = np.where(dist >= 0, 0.0, NEGM)
        if dp < 6:
            mw[dp] = np.where((dist >= 0) & (dist < 512), 0.0, NEGM)
    cb = np.empty((12, 128, 8, 128), np.float32)
    cm = np.empty((2, 12, 128, 128), np.float32)
    for idx in range(12):
        dpp = 2 * idx + 1
        dist = 128 * (dpp - 1 + par) + q - 16 * k - 31
        cb[idx] = rel_bias[_bucket(dist)].transpose(0, 2, 1)
        cm[0, idx] = np.where(dist >= 0, 0.0, NEGM)
        cm[1, idx] = cm[0, idx]
        cm[1, idx, 127, :] = NEGM
    qq = np.arange(128)[:, None]
    u = np.arange(254)[None, :] - 126 - 2 * par
    cq = (qq >= 64).astype(np.int64)
    keep = (u < cq - 1).astype(np.float32)
    addc = np.where((u >= cq - 1) & (u <= cq), 1.0e6, np.where(u > cq, -1.0, 0.0)).astype(np.float32)
    return (bias9.reshape(9, 128, 1024), ms, mw, cb.reshape(12, 128, 1024), cm, keep, addc)


_PROG = {}


def _prep(x, c, w_in, q_norm, k_norm, cmp_pe_k, cmp_w1_k, cmp_w2_k, cmp_pe_v, cmp_w1_v, cmp_w2_v,
          rel_bias, conv_w, w_out, norm1, norm2, w_ada, b_ada, w_ff1, w_ff2):
    f = lambda a: np.ascontiguousarray(np.asarray(a, dtype=np.float32))
    x = f(x)
    c = f(c)
    rel_bias = f(rel_bias)
    kg = np.arange(8192)
    erows = f(((kg[None, :] // 64) % 64 == np.arange(64)[:, None]))
    n_ = np.arange(512)[:, None] * 16
    j_ = np.arange(128)[None, :] * 64
    ov = np.clip(np.minimum(n_ + 32, j_ + 64) - np.maximum(n_, j_), 0, None) / 32.0
    ov[511] = 0.0
    ov = f(ov.reshape(4, 128, 128).transpose(1, 0, 2).reshape(128, 512))
    ident = f(np.eye(128))
    bd = f(np.kron(np.eye(2), np.ones((64, 64))))
    col8 = lambda v: f(np.asarray(v).reshape(8, 128).T)
    shared = {
        "n1c": col8(norm1[0]), "n2c": col8(norm2[0]),
        "badaC": f(np.asarray(b_ada[0]).reshape(48, 128).T), "badaR": f(np.asarray(b_ada[0]).reshape(1, 6144)),
        "w_ada": f(w_ada[0]), "w_in": f(w_in[0]),
        "qg": f(np.broadcast_to(np.asarray(q_norm[0])[None, :], (128, 64))),
        "knc": f(np.tile(np.asarray(k_norm[0]), 2).reshape(128, 1)),
        "peTk": f(np.tile(np.asarray(cmp_pe_k[0]).T, (2, 1))), "peTv": f(np.tile(np.asarray(cmp_pe_v[0]).T, (2, 1))),
        "w1k": f(cmp_w1_k[0]), "w1v": f(cmp_w1_v[0]), "w2k": f(cmp_w2_k[0]), "w2v": f(cmp_w2_v[0]),
        "convc": f(np.asarray(conv_w[0]).reshape(3, 4, 128).transpose(2, 1, 0).reshape(128, 12)),
        "w_out": f(w_out[0]), "w_ff1": f(w_ff1[0]), "w_ff2": f(w_ff2[0]),
        "b31": f(np.broadcast_to(rel_bias[31][None, :], (128, 8))),
        "erows": erows, "ov": ov, "ident": ident, "bd": bd,
    }
    tabs = [_tables(rel_bias, par) for par in range(2)]
    in_maps = []
    own_idx = []
    for core in range(8):
        b, par = core // 2, core % 2
        tiles = 2 * np.arange(NT) + par
        rows = (tiles[:, None] * 128 + np.arange(128)[None, :]).reshape(-1)
        own_idx.append((b, rows))
        xb = x[b]
        halo = np.zeros((NT, 2, 1024), np.float32)
        hmk = np.ones((NT, 2), np.float32)
        for i, T in enumerate(tiles):
            if T == 0:
                hmk[i] = 0.0
            else:
                halo[i] = xb[T * 128 - 2:T * 128]
        bias9, ms, mw, cb, cm, keep, addc = tabs[par]
        m = dict(shared)
        m.update({
            "xfull": xb, "xown": f(xb[rows]), "xhalo": f(halo.reshape(8, 8, 1024)),
            "hmask": f(np.broadcast_to(hmk.reshape(1, 64), (128, 64))),
            "cT": col8(c[b]),
            "bias9": bias9, "ms": ms, "mw": mw, "cb": cb, "cm": cm, "keep": keep, "addc": addc,
        })
        in_maps.append(m)
    return in_maps, own_idx


def kernel(**inputs):
    in_maps, own_idx = _prep(**inputs)
    if 'nc' not in _PROG:
        _PROG['nc'] = build_program()
    nc = _PROG['nc']
    res = run_bass_kernel_spmd(nc, in_maps, core_ids=list(range(8)))
    out = np.empty((4, 8192, 1024), np.float32)
    for core in range(8):
        b, rows = own_idx[core]
        out[b, rows] = res.results[core]["out_own"]
    return out
```
